# Optimizing a Trainium2 kernel written in Bass

```python
import math
import jax, jax.numpy as jnp
from jax import lax
import numpy as np

D_MODEL = 1024
BATCH = 16
SEQ = 4096
DEPTH = 4

GRID_W = 64
N_MIXERS = 2
N_ATTN_LAYERS = (DEPTH + 1) // 2
N_HYENA_LAYERS = DEPTH // 2
N_Q_HEADS = 16
N_KV_HEADS = 4
HEAD_DIM = D_MODEL // N_Q_HEADS
Q_PER_KV = N_Q_HEADS // N_KV_HEADS
D_ATTN = N_Q_HEADS * HEAD_DIM
D_KV = N_KV_HEADS * HEAD_DIM
D_QKV = D_ATTN + 2 * D_KV
ROPE_THETA = 10000.0
ROPE_AXIS_DIM = HEAD_DIM // 2
Q_BLOCK = 128
D_HYENA = D_MODEL
SHORT_CONV = 3
POS_EMB_DIM = 33
FILTER_HIDDEN = 64
DECAY_TARGET = 1e-2
FAST_DECAY_PCT = 0.3
SLOW_DECAY_PCT = 1.5
D_FF = 2816
FFN_RES_SCALE = 0.5
NORM_EPS = 1e-6

kernel_name = 'hybrid_gqa_hyena_macaron_encoder'


def rms_norm(x, g):
    x32 = x.astype(jnp.float32)
    y = x32 * lax.rsqrt(jnp.mean(x32 * x32, axis=-1, keepdims=True) + NORM_EPS)
    return (y * g.astype(jnp.float32)).astype(x.dtype)


def swiglu(h, w_in, w_out):
    gate, up = jnp.split(h @ w_in, 2, axis=-1)
    return (jax.nn.silu(gate) * up) @ w_out


def axial_rope_tables(L):
    rows = L // GRID_W
    row = jnp.repeat(jnp.arange(rows, dtype=jnp.float32), GRID_W)
    col = jnp.tile(jnp.arange(GRID_W, dtype=jnp.float32), rows)
    inv = ROPE_THETA ** (-jnp.arange(0, ROPE_AXIS_DIM, 2, dtype=jnp.float32) / ROPE_AXIS_DIM)
    ang = jnp.concatenate([row[:, None] * inv, col[:, None] * inv], axis=-1)
    return jnp.cos(ang), jnp.sin(ang)


def apply_rope(x, cos, sin):
    shape = (1, x.shape[1]) + (1,) * (x.ndim - 3) + (HEAD_DIM // 2,)
    c = cos.reshape(shape)
    s = sin.reshape(shape)
    xp = x.reshape(x.shape[:-1] + (HEAD_DIM // 2, 2))
    x1, x2 = xp[..., 0], xp[..., 1]
    return jnp.stack([x1 * c - x2 * s, x1 * s + x2 * c], axis=-1).reshape(x.shape)


def attention_mixer(h, w_in, q_gain, k_gain, w_out, cos, sin):
    B, L, _ = h.shape
    qkv = h @ w_in
    q = qkv[..., :D_ATTN].reshape(B, L, N_KV_HEADS, Q_PER_KV, HEAD_DIM)
    k = qkv[..., D_ATTN:D_ATTN + D_KV].reshape(B, L, N_KV_HEADS, HEAD_DIM)
    v = qkv[..., D_ATTN + D_KV:].reshape(B, L, N_KV_HEADS, HEAD_DIM)
    q = (apply_rope(rms_norm(q, q_gain).astype(jnp.float32), cos, sin) * (HEAD_DIM ** -0.5)).astype(h.dtype)
    k = apply_rope(rms_norm(k, k_gain).astype(jnp.float32), cos, sin).astype(h.dtype)
    nb = L // Q_BLOCK
    q_blocks = q.reshape(B, nb, Q_BLOCK, N_KV_HEADS, Q_PER_KV, HEAD_DIM).transpose(1, 0, 2, 3, 4, 5)

    def one_block(qb):
        s = jnp.einsum('bqkgd,bskd->bkgqs', qb, k, preferred_element_type=jnp.float32)
        p = jax.nn.softmax(s, axis=-1).astype(v.dtype)
        return jnp.einsum('bkgqs,bskd->bqkgd', p, v)

    o = lax.map(one_block, q_blocks)
    o = o.transpose(1, 0, 2, 3, 4, 5).reshape(B, L, D_ATTN)
    return o @ w_out


def hyena_positions(L):
    t = jnp.linspace(0.0, 1.0, L, dtype=jnp.float32)[:, None]
    bands = (POS_EMB_DIM - 1) // 2
    f = jnp.linspace(1e-4, bands - 1, bands, dtype=jnp.float32)
    w = 2.0 * math.pi * jnp.arange(L, dtype=jnp.float32)[:, None] / L
    z = jnp.concatenate([t, jnp.cos(f * w), -jnp.sin(f * w)], axis=-1)
    return z, t


def hyena_filter(z, t, w1, b1, w2, b2, w3, b3, freq, w_out, decay):
    L = z.shape[0]
    fr = freq.astype(jnp.float32)
    hid = jnp.sin(fr * (z @ w1.astype(jnp.float32) + b1.astype(jnp.float32)))
    hid = jnp.sin(fr * (hid @ w2.astype(jnp.float32) + b2.astype(jnp.float32)))
    hid = jnp.sin(fr * (hid @ w3.astype(jnp.float32) + b3.astype(jnp.float32)))
    filt = (hid @ w_out.astype(jnp.float32)).reshape(L, 2, D_HYENA)
    filt = filt * jnp.exp(-t[:, :, None] * jnp.abs(decay.astype(jnp.float32))[None])
    h_fwd, h_bwd = filt[:, 0], filt[:, 1]
    k2 = jnp.concatenate([h_fwd, jnp.zeros((1, D_HYENA), jnp.float32), h_bwd[:0:-1]], axis=0)
    return k2 / jnp.sum(jnp.abs(k2), axis=0, keepdims=True)


def hyena_mixer(h, w_in, b_in, conv_w, conv_b, k2, d_bias, w_out, b_out):
    B, L, _ = h.shape
    u = h @ w_in + b_in
    pad = SHORT_CONV // 2
    up = jnp.pad(u, ((0, 0), (pad, pad), (0, 0)))
    u = sum(conv_w[j] * up[:, j:j + L] for j in range(SHORT_CONV)) + conv_b
    x0, x1, v = jnp.split(u, 3, axis=-1)
    v = (v * x1).astype(jnp.float32)
    n = 2 * L
    y = jnp.fft.irfft(jnp.fft.rfft(v, n=n, axis=1) * jnp.fft.rfft(k2, n=n, axis=0)[None], n=n, axis=1)[:, :L]
    y = y + d_bias.astype(jnp.float32) * v
    return (y.astype(h.dtype) * x0) @ w_out + b_out


def setup_inputs(seed: int = 0) -> dict:
    key = jax.random.key(seed)
    ks = jax.random.split(key, 32)

    def nrm(k, shape, scale):
        return jax.random.normal(k, shape, jnp.float32) * scale

    min_decay = math.log(DECAY_TARGET) / SLOW_DECAY_PCT
    max_decay = math.log(DECAY_TARGET) / FAST_DECAY_PCT
    base_decay = jnp.linspace(min_decay, max_decay, D_HYENA, dtype=jnp.float32)
    nh = N_HYENA_LAYERS
    na = N_ATTN_LAYERS
    return {
        'x': nrm(ks[0], (BATCH, SEQ, D_MODEL), 1.0),
        'ffn_norm_g': 1.0 + nrm(ks[1], (DEPTH, 2, D_MODEL), 0.02),
        'mix_norm_g': 1.0 + nrm(ks[2], (DEPTH, D_MODEL), 0.02),
        'ffn_w_in': nrm(ks[3], (DEPTH, 2, D_MODEL, 2 * D_FF), D_MODEL ** -0.5),
        'ffn_w_out': nrm(ks[4], (DEPTH, 2, D_FF, D_MODEL), D_FF ** -0.5),
        'attn_w_in': nrm(ks[5], (na, D_MODEL, D_QKV), D_MODEL ** -0.5),
        'attn_q_gain': 1.0 + nrm(ks[6], (na, HEAD_DIM), 0.02),
        'attn_k_gain': 1.0 + nrm(ks[7], (na, HEAD_DIM), 0.02),
        'attn_w_out': nrm(ks[8], (na, D_ATTN, D_MODEL), D_ATTN ** -0.5),
        'hy_w_in': nrm(ks[9], (nh, D_MODEL, 3 * D_HYENA), D_MODEL ** -0.5),
        'hy_b_in': nrm(ks[10], (nh, 3 * D_HYENA), 0.02),
        'hy_conv_w': nrm(ks[11], (nh, SHORT_CONV, 3 * D_HYENA), SHORT_CONV ** -0.5),
        'hy_conv_b': nrm(ks[12], (nh, 3 * D_HYENA), 0.02),
        'hy_f_w1': nrm(ks[13], (nh, POS_EMB_DIM, FILTER_HIDDEN), POS_EMB_DIM ** -0.5),
        'hy_f_b1': nrm(ks[14], (nh, FILTER_HIDDEN), 0.1),
        'hy_f_w2': nrm(ks[15], (nh, FILTER_HIDDEN, FILTER_HIDDEN), FILTER_HIDDEN ** -0.5),
        'hy_f_b2': nrm(ks[16], (nh, FILTER_HIDDEN), 0.1),
        'hy_f_w3': nrm(ks[17], (nh, FILTER_HIDDEN, FILTER_HIDDEN), FILTER_HIDDEN ** -0.5),
        'hy_f_b3': nrm(ks[18], (nh, FILTER_HIDDEN), 0.1),
        'hy_f_freq': 1.0 + nrm(ks[19], (nh, FILTER_HIDDEN), 0.01),
        'hy_f_w_out': nrm(ks[20], (nh, FILTER_HIDDEN, 2 * D_HYENA), FILTER_HIDDEN ** -0.5),
        'hy_decay': base_decay + nrm(ks[21], (nh, 2, D_HYENA), 0.1),
        'hy_d_bias': nrm(ks[22], (nh, D_HYENA), 1.0),
        'hy_w_out': nrm(ks[23], (nh, D_HYENA, D_MODEL), D_HYENA ** -0.5),
        'hy_b_out': nrm(ks[24], (nh, D_MODEL), 0.02),
        'final_norm_g': 1.0 + nrm(ks[25], (D_MODEL,), 0.02),
    }


def reference(x, ffn_norm_g, mix_norm_g, ffn_w_in, ffn_w_out, attn_w_in, attn_q_gain, attn_k_gain,
              attn_w_out, hy_w_in, hy_b_in, hy_conv_w, hy_conv_b, hy_f_w1, hy_f_b1, hy_f_w2, hy_f_b2,
              hy_f_w3, hy_f_b3, hy_f_freq, hy_f_w_out, hy_decay, hy_d_bias, hy_w_out, hy_b_out,
              final_norm_g):
    L = x.shape[1]
    cos, sin = axial_rope_tables(L)
    z_pos, t_pos = hyena_positions(L)
    for i in range(DEPTH):
        x = x + FFN_RES_SCALE * swiglu(rms_norm(x, ffn_norm_g[i, 0]), ffn_w_in[i, 0], ffn_w_out[i, 0])
        h = rms_norm(x, mix_norm_g[i])
        j = i // N_MIXERS
        if i % N_MIXERS == 0:
            y = attention_mixer(h, attn_w_in[j], attn_q_gain[j], attn_k_gain[j], attn_w_out[j], cos, sin)
        else:
            k2 = hyena_filter(z_pos, t_pos, hy_f_w1[j], hy_f_b1[j], hy_f_w2[j], hy_f_b2[j], hy_f_w3[j],
                              hy_f_b3[j], hy_f_freq[j], hy_f_w_out[j], hy_decay[j])
            y = hyena_mixer(h, hy_w_in[j], hy_b_in[j], hy_conv_w[j], hy_conv_b[j], k2, hy_d_bias[j],
                            hy_w_out[j], hy_b_out[j])
        x = x + y
        x = x + FFN_RES_SCALE * swiglu(rms_norm(x, ffn_norm_g[i, 1]), ffn_w_in[i, 1], ffn_w_out[i, 1])
    return rms_norm(x, final_norm_g)
```

```python
import math
import numpy as np
import ml_dtypes
import concourse.bass as bass
import concourse.mybir as mybir
from concourse.bass_utils import run_bass_kernel_spmd

F32 = mybir.dt.float32
BF16 = mybir.dt.bfloat16
ALU = mybir.AluOpType
AF = mybir.ActivationFunctionType
AX = mybir.AxisListType

ENGS = ("tensor", "vector", "scalar", "gpsimd", "sync")

D = 1024
DFF = 2816
NJ = DFF // 128
L = 4096
NCORES = 8
EPS = 1e-6
TWO_PI = 2.0 * math.pi


class Buf:
    __slots__ = ("w", "r", "strict")

    def __init__(self):
        self.w = None
        self.r = {}
        self.strict = False


class Stage:
    def __init__(self, nc, name):
        self.nc = nc
        self.name = name
        self.q = {e: [] for e in ENGS}
        self.cnt = {e: 0 for e in ENGS}
        self.seen = {e: {} for e in ENGS}
        self.sems = {}
        self.cleanup = nc.cleanup_on_exit()
        self.cleanup.__enter__()
        for e in ENGS:
            self.sems[e] = nc.alloc_semaphore(name=f"{name}_s_{e}")
        self.dsems = []
        self.nps = 0

    def sb(self, name, shape, dtype):
        return self.nc.alloc_sbuf_tensor(f"{self.name}_{name}", list(shape), dtype)

    def ps(self, shape=(128, 512), dtype=F32):
        self.nps += 1
        return self.nc.alloc_psum_tensor(f"{self.name}_ps{self.nps}", list(shape), dtype)

    def dsem(self):
        s = self.nc.alloc_semaphore(name=f"{self.name}_d{len(self.dsems)}")
        d = [s, 0]
        self.dsems.append(d)
        return d

    def _waits(self, eng, reads, writes, force_waw=False):
        need = {}

        def add(kv, raw, waw=False):
            if kv is None:
                return
            k, v = kv
            if isinstance(k, str) and k == eng and not raw and (eng == "tensor" or not waw):
                return
            kk = k if isinstance(k, str) else id(k)
            if kk not in need or need[kk][1] < v:
                need[kk] = (k, v)

        for b in reads:
            add(b.w, True)
        for b in writes:
            add(b.w, False, force_waw or b.strict)
            for kv in b.r.values():
                add(kv, False)
        out = []
        for kk, (k, v) in need.items():
            if self.seen[eng].get(kk, 0) < v:
                self.seen[eng][kk] = v
                out.append((self.sems[k] if isinstance(k, str) else k[0], v))
        return out

    @staticmethod
    def _note_read(b, k, v):
        kk = k if isinstance(k, str) else id(k)
        if kk not in b.r or b.r[kk][1] < v:
            b.r[kk] = (k, v)

    def op(self, eng, fn, reads=(), writes=(), memset=False):
        wl = self._waits(eng, reads, writes, force_waw=memset)
        self.cnt[eng] += 1
        v = self.cnt[eng]
        sem = self.sems[eng]

        def emit(e):
            for s, val in wl:
                e.wait_ge(s, val)
            fn(e).then_inc(sem, 1)

        self.q[eng].append(emit)
        for b in reads:
            self._note_read(b, eng, v)
        for b in writes:
            b.w = (eng, v)
            b.r = {}
            b.strict = memset

    def dma(self, queue, out_ap, in_ap, ds, reads=(), writes=()):
        wl = self._waits(queue, reads, writes)
        ds[1] += 16
        v = ds[1]
        sem = ds[0]

        def emit(e):
            for s, val in wl:
                e.wait_ge(s, val)
            e.dma_start(out=out_ap, in_=in_ap).then_inc(sem, 16)

        self.q[queue].append(emit)
        for b in reads:
            self._note_read(b, ds, v)
        for b in writes:
            b.w = (ds, v)
            b.r = {}

    def finish(self):
        nc = self.nc
        q = self.q
        fin = [(d[0], d[1]) for d in self.dsems if d[1] > 0]

        def emit_fin(e):
            for s_, v_ in fin:
                e.wait_ge(s_, v_)

        q["sync"].append(emit_fin)
        with nc.Block() as block:
            @block.tensor
            def _(e):
                for f in q["tensor"]:
                    f(e)

            @block.vector
            def _(e):
                for f in q["vector"]:
                    f(e)

            @block.scalar
            def _(e):
                for f in q["scalar"]:
                    f(e)

            @block.gpsimd
            def _(e):
                for f in q["gpsimd"]:
                    f(e)

            @block.sync
            def _(e):
                for f in q["sync"]:
                    f(e)
        self.cleanup.__exit__(None, None, None)


class VecTable:
    def __init__(self):
        self.cols = []
        self.idx = {}

    def add(self, key, arr2d):
        a = np.zeros((128, arr2d.shape[1]), np.float32)
        a[:arr2d.shape[0]] = arr2d
        self.idx[key] = (sum(c.shape[1] for c in self.cols), a.shape[1])
        self.cols.append(a)

    def add_feat(self, key, vec):
        v = np.asarray(vec, np.float32)
        self.add(key, np.ascontiguousarray(v.reshape(-1, 128).T))

    def build(self):
        return np.ascontiguousarray(np.concatenate(self.cols, axis=1))


def stage_transpose_in(nc, x_d, xT_d, ident_d, ntok):
    st = Stage(nc, "tin")
    ident = st.sb("ident", [128, 128], F32)
    b_id = Buf()
    d_id = st.dsem()
    st.dma("sync", ident[:], ident_d, d_id, writes=[b_id])
    xin = [st.sb(f"xin{i}", [128, 4, D], F32) for i in range(2)]
    b_xin = [Buf(), Buf()]
    xo = [st.sb(f"xo{i}", [128, 8, 512], F32) for i in range(2)]
    b_xo = [Buf(), Buf()]
    d_in = [st.dsem(), st.dsem()]
    d_out = [st.dsem(), st.dsem()]
    pss = [st.ps() for _ in range(8)]
    b_ps = [Buf() for _ in range(8)]
    xv = x_d.rearrange("(n s p) f -> n p s f", p=128, s=4)
    xTv = xT_d.rearrange("(c p) t -> p c t", p=128)
    NT = ntok // 512
    for i in range(NT):
        sl = i % 2
        st.dma("sync", xin[sl][:], xv[i], d_in[sl], writes=[b_xin[sl]])
        for c in range(8):
            def mm(e, c=c, sl=sl):
                ins = None
                for s in range(4):
                    ins = e.transpose(pss[c][:, s * 128:(s + 1) * 128], xin[sl][:, s, c * 128:(c + 1) * 128], ident[:])
                return ins
            st.op("tensor", mm, reads=[b_xin[sl], b_id], writes=[b_ps[c]])
            if c % 2 == 0:
                st.op("vector", lambda e, c=c, sl=sl: e.tensor_copy(xo[sl][:, c, :], pss[c][:]),
                      reads=[b_ps[c]], writes=[b_xo[sl]])
            else:
                st.op("scalar", lambda e, c=c, sl=sl: e.activation(xo[sl][:, c, :], pss[c][:], AF.Copy),
                      reads=[b_ps[c]], writes=[b_xo[sl]])
        st.dma("sync", xTv[:, :, i * 512:(i + 1) * 512], xo[sl][:], d_out[sl], reads=[b_xo[sl]])
    st.finish()


def emit_norm_stats(st, xt, b_x, sq, b_sq, ones, b_ones, ss_ps, b_ss, rs, b_rs, width=512, lnexp=False):
    for h in range(4):
        s2 = h % 2
        st.op("gpsimd", lambda e, h=h, s2=s2: e.tensor_tensor(sq[s2][:], xt[:, 2 * h:2 * h + 2, :], xt[:, 2 * h:2 * h + 2, :], ALU.mult),
              reads=[b_x], writes=[b_sq[s2]])

        def mm(e, h=h, s2=s2):
            ins = None
            for k in range(2):
                ins = e.matmul(ss_ps[:, 0:width], ones[:], sq[s2][:, k, :], start=(h == 0 and k == 0), stop=(h == 3 and k == 1))
            return ins
        st.op("tensor", mm, reads=[b_sq[s2], b_ones], writes=[b_ss])
    if lnexp:
        st.op("scalar", lambda e: e.activation(rs[:, 0:width], ss_ps[:, 0:width], AF.Ln, bias=EPS, scale=1.0 / D), reads=[b_ss], writes=[b_rs])
        st.op("scalar", lambda e: e.activation(rs[:, 0:width], rs[:, 0:width], AF.Exp, scale=-0.5), reads=[b_rs], writes=[b_rs])
        return
    st.op("scalar", lambda e: e.activation(rs[:, 0:width], ss_ps[:, 0:width], AF.Sqrt, bias=EPS, scale=1.0 / D), reads=[b_ss], writes=[b_rs])
    st.op("vector", lambda e: e.reciprocal(rs[:, 0:width], rs[:, 0:width]), reads=[b_rs], writes=[b_rs])


def emit_weight_cast(st, jobs):
    for src, dst in jobs:
        sv = src.rearrange("(c p) n -> p c n", p=128)
        dv = dst.rearrange("(c p) n -> p c n", p=128)
        nchunk = sv.shape[1]
        step = 2 if nchunk <= 8 else 6
        for c0 in range(0, nchunk, step):
            c1 = min(nchunk, c0 + step)
            st.dma("gpsimd", dv[:, c0:c1, :], sv[:, c0:c1, :], st.dsem(), writes=[Buf()])


def stage_ffn(nc, name, xT_d, w_in_d, w_out_d, vecs_d, nv, gcol, ntok, precast=False):
    st = Stage(nc, name)
    NT = ntok // 512
    w_in = st.sb("w_in", [128, 8, 2 * DFF], BF16)
    w_out = st.sb("w_out", [128, NJ, D], BF16)
    JB = [0, 6, 12, 17, 22]
    b_win = [Buf() for _ in range(4)]
    d_wq = [st.dsem() for _ in range(4)]
    jq = [max(q for q in range(4) if JB[q] <= j) for j in range(NJ)]
    b_wout = [Buf() for _ in range(2)]
    d_w = st.dsem()
    d_w2 = st.dsem()
    vecs = st.sb("vecs", [128, nv], F32)
    b_vecs = Buf()
    d_v = st.dsem()
    st.dma("sync", vecs[:], vecs_d, d_v, writes=[b_vecs])
    ones = st.sb("ones", [128, 128], BF16)
    b_ones = Buf()
    st.op("vector", lambda e: e.memset(ones[:], 1.0), writes=[b_ones], memset=True)
    xt = [st.sb(f"xt{i}", [128, 8, 512], F32) for i in range(2)]
    b_x = [Buf(), Buf()]
    d_x = [st.dsem(), st.dsem()]
    d_o = [st.dsem(), st.dsem()]
    hT = st.sb("hT", [128, 8, 512], BF16)
    b_h = Buf()
    aT = st.sb("aT", [128, NJ, 512], BF16)
    b_a = [Buf() for _ in range(NJ)]
    sq = [st.sb(f"sq{i}", [128, 2, 512], BF16) for i in range(2)]
    b_sq = [Buf(), Buf()]
    rs = st.sb("rs", [128, 512], F32)
    b_rs = Buf()
    sl_t = [st.sb(f"sl{i}", [128, 512], F32) for i in range(2)]
    b_sl = [Buf(), Buf()]
    ss_ps = st.ps()
    b_ss = Buf()
    g_ps = [st.ps(), st.ps()]
    u_ps = [st.ps(), st.ps()]
    b_g = [Buf(), Buf()]
    b_u = [Buf(), Buf()]
    y_ps = [st.ps(), st.ps()]
    b_y = [Buf(), Buf()]
    xTv = xT_d.rearrange("(c p) t -> p c t", p=128)
    w_in_v = w_in_d.rearrange("(c p) n -> p c n", p=128)
    w_out_v = w_out_d.rearrange("(j p) n -> p j n", p=128)

    def load_x(i):
        s = i % 2
        st.dma("sync", xt[s][:], xTv[:, :, i * 512:(i + 1) * 512], d_x[s], writes=[b_x[s]])

    load_x(0)
    if NT > 1:
        load_x(1)
    def pro_a(i):
        s = i % 2
        emit_norm_stats(st, xt[s], b_x[s], sq, b_sq, ones, b_ones, ss_ps, b_ss, rs, b_rs)

    def pro_b(i):
        s = i % 2
        for c in range(8):
            st.op("vector", lambda e, c=c, s=s: e.scalar_tensor_tensor(
                hT[:, c, :], xt[s][:, c, :], vecs[:, gcol + c:gcol + c + 1], rs[:], ALU.mult, ALU.mult),
                reads=[b_x[s], b_rs, b_vecs], writes=[b_h])

    def up(i):
        for j in range(NJ):
            p = j % 2

            def mmg(e, j=j, p=p):
                ins = None
                for c in range(8):
                    ins = e.matmul(g_ps[p][:], w_in[:, c, j * 128:(j + 1) * 128], hT[:, c, :], start=(c == 0), stop=(c == 7))
                return ins

            def mmu(e, j=j, p=p):
                ins = None
                for c in range(8):
                    ins = e.matmul(u_ps[p][:], w_in[:, c, DFF + j * 128:DFF + (j + 1) * 128], hT[:, c, :], start=(c == 0), stop=(c == 7))
                return ins
            st.op("tensor", mmg, reads=[b_h, b_win[jq[j]]], writes=[b_g[p]])
            st.op("tensor", mmu, reads=[b_h, b_win[jq[j]]], writes=[b_u[p]])
            st.op("scalar", lambda e, p=p: e.activation(sl_t[p][:], g_ps[p][:], AF.Silu), reads=[b_g[p]], writes=[b_sl[p]])
            st.op("vector", lambda e, p=p, j=j: e.tensor_tensor(aT[:, j, :], u_ps[p][:], sl_t[p][:], ALU.mult),
                  reads=[b_u[p], b_sl[p]], writes=[b_a[j]])

    def down(i):
        s = i % 2
        for m in range(8):
            p = m % 2

            def mmy(e, m=m, p=p):
                ins = None
                for j in range(NJ):
                    ins = e.matmul(y_ps[p][:], w_out[:, j, m * 128:(m + 1) * 128], aT[:, j, :], start=(j == 0), stop=(j == NJ - 1))
                return ins
            st.op("tensor", mmy, reads=b_a + b_wout, writes=[b_y[p]])
            st.op("vector", lambda e, m=m, p=p, s=s: e.scalar_tensor_tensor(
                xt[s][:, m, :], y_ps[p][:], 0.5, xt[s][:, m, :], ALU.mult, ALU.add),
                reads=[b_y[p], b_x[s]], writes=[b_x[s]])
        st.dma("sync", xTv[:, :, i * 512:(i + 1) * 512], xt[s][:], d_o[s], reads=[b_x[s]])

    pro_a(0)
    for q in range(4):
        for off in (0, DFF):
            ca, cb_ = off + JB[q] * 128, off + JB[q + 1] * 128
            st.dma("scalar" if precast else "gpsimd", w_in[:, :, ca:cb_], w_in_v[:, :, ca:cb_], d_wq[q], writes=[b_win[q]])
    for hh in range(2):
        st.dma("scalar" if precast else "gpsimd", w_out[:, hh * 11:(hh + 1) * 11, :], w_out_v[:, hh * 11:(hh + 1) * 11, :], d_w2, writes=[b_wout[hh]])

    pro_b(0)
    for i in range(NT):
        up(i)
        if i + 1 < NT:
            pro_a(i + 1)
            pro_b(i + 1)
        down(i)
        if i + 2 < NT:
            load_x(i + 2)
    st.finish()


def stage_final(nc, xT_d, out_d, vecs_d, nv, gcol, ident_d, ntok):
    st = Stage(nc, "fin")
    NT = ntok // 512
    vecs = st.sb("vecs", [128, nv], F32)
    b_vecs = Buf()
    d_v = st.dsem()
    st.dma("sync", vecs[:], vecs_d, d_v, writes=[b_vecs])
    ident = st.sb("ident", [128, 128], F32)
    b_id = Buf()
    d_id = st.dsem()
    st.dma("sync", ident[:], ident_d, d_id, writes=[b_id])
    ones = st.sb("ones", [128, 128], BF16)
    b_ones = Buf()
    st.op("vector", lambda e: e.memset(ones[:], 1.0), writes=[b_ones], memset=True)
    xt = [st.sb(f"xt{i}", [128, 8, 512], F32) for i in range(2)]
    b_x = [Buf(), Buf()]
    d_x = [st.dsem(), st.dsem()]
    d_o = [st.dsem(), st.dsem()]
    sq = [st.sb(f"sq{i}", [128, 2, 512], BF16) for i in range(2)]
    b_sq = [Buf(), Buf()]
    rs = st.sb("rs", [128, 512], F32)
    b_rs = Buf()
    ot = [st.sb(f"ot{i}", [128, 4, D], F32) for i in range(2)]
    b_ot = [Buf(), Buf()]
    ss_ps = st.ps()
    b_ss = Buf()
    pss = [st.ps() for _ in range(6)]
    b_ps = [Buf() for _ in range(6)]
    xTv = xT_d.rearrange("(c p) t -> p c t", p=128)
    ov = out_d.rearrange("(n s p) f -> n p s f", p=128, s=4)
    kk = 0
    for i in range(NT):
        s = i % 2
        st.dma("sync", xt[s][:], xTv[:, :, i * 512:(i + 1) * 512], d_x[s], writes=[b_x[s]])
        emit_norm_stats(st, xt[s], b_x[s], sq, b_sq, ones, b_ones, ss_ps, b_ss, rs, b_rs)
        for c in range(8):
            st.op("vector", lambda e, c=c, s=s: e.scalar_tensor_tensor(
                xt[s][:, c, :], xt[s][:, c, :], vecs[:, gcol + c:gcol + c + 1], rs[:], ALU.mult, ALU.mult),
                reads=[b_x[s], b_rs, b_vecs], writes=[b_x[s]])
        for sb_ in range(4):
            for hf in range(2):
                p = kk % 6
                kk += 1

                def mm(e, sb_=sb_, hf=hf, p=p, s=s):
                    ins = None
                    for c4 in range(4):
                        c = hf * 4 + c4
                        ins = e.transpose(pss[p][:, c4 * 128:(c4 + 1) * 128], xt[s][:, c, sb_ * 128:(sb_ + 1) * 128], ident[:])
                    return ins
                st.op("tensor", mm, reads=[b_x[s], b_id], writes=[b_ps[p]])
                if kk % 2 == 0:
                    st.op("vector", lambda e, sb_=sb_, hf=hf, p=p, s=s: e.tensor_copy(ot[s][:, sb_, hf * 512:(hf + 1) * 512], pss[p][:]),
                          reads=[b_ps[p]], writes=[b_ot[s]])
                else:
                    st.op("scalar", lambda e, sb_=sb_, hf=hf, p=p, s=s: e.activation(ot[s][:, sb_, hf * 512:(hf + 1) * 512], pss[p][:], AF.Copy),
                          reads=[b_ps[p]], writes=[b_ot[s]])
        st.dma("sync", ov[i], ot[s][:], d_o[s], reads=[b_ot[s]])
    st.finish()


NQX = 2048
NKX = 512
NVX = 256
HEAD_A = [0, 1, 2, 3, 8, 9, 10, 11]
HEAD_B = [4, 5, 6, 7, 12, 13, 14, 15]


def emit_headnorm_rope(st, src_ps, b_src, bones, b_bones, ssq_ps, b_ssq, sqh, b_sqh, rsh, b_rsh, t1, b_t1, t2, b_t2,
                       vecs, b_vecs, gc, gsc, rc, rsn, b_rope, outs):
    st.op("scalar", lambda e: e.activation(sqh[:], src_ps[:, 0:512], AF.Square), reads=[b_src], writes=[b_sqh])
    st.op("tensor", lambda e: e.matmul(ssq_ps[:], bones[:], sqh[:], start=True, stop=True), reads=[b_sqh, b_bones], writes=[b_ssq])
    st.op("scalar", lambda e: e.activation(rsh[:], ssq_ps[:], AF.Ln, bias=EPS, scale=1.0 / 64), reads=[b_ssq], writes=[b_rsh])
    st.op("scalar", lambda e: e.activation(rsh[:], rsh[:], AF.Exp, scale=-0.5), reads=[b_rsh], writes=[b_rsh])
    st.op("vector", lambda e: e.scalar_tensor_tensor(t1[:], src_ps[:, 0:512], vecs[:, gc:gc + 1], rsh[:], ALU.mult, ALU.mult),
          reads=[b_src, b_rsh, b_vecs], writes=[b_t1])
    st.op("vector", lambda e: e.scalar_tensor_tensor(t2[:], src_ps[:, 512:1024], vecs[:, gsc:gsc + 1], rsh[:], ALU.mult, ALU.mult),
          reads=[b_src, b_rsh, b_vecs], writes=[b_t2])
    st.op("gpsimd", lambda e: e.tensor_tensor(t1[:], t1[:], rc, ALU.mult), reads=[b_t1, b_rope], writes=[b_t1])
    st.op("vector", lambda e: e.tensor_tensor(t2[:], t2[:], rsn, ALU.mult), reads=[b_t2, b_rope], writes=[b_t2])
    for out_ap, rows, b_out in outs:
        st.op("gpsimd", lambda e, out_ap=out_ap, rows=rows: e.tensor_tensor(out_ap, t1[rows, :], t2[rows, :], ALU.add),
              reads=[b_t1, b_t2], writes=[b_out])


def make_bones(st):
    bones = st.sb("bones", [128, 128], BF16)
    b = Buf()
    st.op("vector", lambda e: e.memset(bones[:], 0.0), writes=[b], memset=True)
    st.op("vector", lambda e: e.memset(bones[0:64, 0:64], 1.0), writes=[b], memset=True)
    st.op("vector", lambda e: e.memset(bones[64:128, 64:128], 1.0), writes=[b], memset=True)
    return bones, b


def stage_attn_kv(nc, name, xT_d, wk_d, wv_d, vecs_d, nv, gcol, kgc, kgsc, ropec_d, ropes_d, KT_d, VA_d, nseq):
    st = Stage(nc, name)
    vecs = st.sb("vecs", [128, nv], F32)
    b_vecs = Buf()
    d_v = st.dsem()
    st.dma("sync", vecs[:], vecs_d, d_v, writes=[b_vecs])
    wk = st.sb("wk", [128, 8, NKX], BF16)
    wv = st.sb("wv", [128, 8, NVX], BF16)
    b_wk, b_wv = Buf(), Buf()
    d_w = st.dsem()
    st.dma("gpsimd", wk[:], wk_d.rearrange("(c p) n -> p c n", p=128), d_w, writes=[b_wk])
    d_w2 = st.dsem()
    st.dma("gpsimd", wv[:], wv_d.rearrange("(c p) n -> p c n", p=128), d_w2, writes=[b_wv])
    ones = st.sb("ones", [128, 128], BF16)
    b_ones = Buf()
    st.op("vector", lambda e: e.memset(ones[:], 1.0), writes=[b_ones], memset=True)
    bones, b_bones = make_bones(st)
    xt = [st.sb(f"xt{i}", [128, 8, 512], F32) for i in range(2)]
    b_x = [Buf(), Buf()]
    d_x = [st.dsem(), st.dsem()]
    rope = [st.sb(f"rope{i}", [128, 2, 512], F32) for i in range(2)]
    b_rope = [Buf(), Buf()]
    d_r = [st.dsem(), st.dsem()]
    hTs = [st.sb(f"hT{i}", [128, 8, 512], BF16) for i in range(2)]
    b_hs = [Buf(), Buf()]
    sq = [st.sb(f"sq{i}", [128, 2, 512], BF16) for i in range(2)]
    b_sq = [Buf(), Buf()]
    rs = st.sb("rs", [128, 512], F32)
    b_rs = Buf()
    sqh = st.sb("sqh", [128, 512], BF16)
    b_sqh = Buf()
    rsh = st.sb("rsh", [128, 512], F32)
    b_rsh = Buf()
    t1 = st.sb("t1", [128, 512], F32)
    t2 = st.sb("t2", [128, 512], F32)
    b_t1, b_t2 = Buf(), Buf()
    kst = [st.sb(f"kst{i}", [128, 2, 512], BF16) for i in range(2)]
    b_kst = [Buf(), Buf()]
    d_k = [st.dsem(), st.dsem()]
    vst = [st.sb(f"vst{i}", [128, 4, 4, 128], BF16) for i in range(2)]
    b_vst = [Buf(), Buf()]
    d_vs = [st.dsem(), st.dsem()]
    for i in range(2):
        st.op("vector", lambda e, i=i: e.memset(vst[i][:], 1.0), writes=[b_vst[i]], memset=True)
    ss_ps = st.ps()
    b_ss = Buf()
    ssq_ps = st.ps()
    b_ssq = Buf()
    kps = [st.ps([128, 1024]) for _ in range(2)]
    b_kps = [Buf(), Buf()]
    v_ps = [st.ps(), st.ps()]
    b_vps = [Buf(), Buf()]
    xTv = xT_d.rearrange("(c p) t -> p c t", p=128)
    NT = nseq * 8
    KTv = KT_d.rearrange("s g p t -> s p g t")

    def load(i):
        s = i % 2
        st.dma("sync", xt[s][:], xTv[:, :, i * 512:(i + 1) * 512], d_x[s], writes=[b_x[s]])
        tl = (i % 8) * 512
        st.dma("sync", rope[s][:, 0, :], ropec_d[:, tl:tl + 512], d_r[s], writes=[b_rope[s]])
        st.dma("sync", rope[s][:, 1, :], ropes_d[:, tl:tl + 512], d_r[s], writes=[b_rope[s]])

    def pro(i):
        s = i % 2
        emit_norm_stats(st, xt[s], b_x[s], sq, b_sq, ones, b_ones, ss_ps, b_ss, rs, b_rs, lnexp=True)
        for c in range(8):
            st.op("vector", lambda e, c=c, s=s: e.scalar_tensor_tensor(
                hTs[s][:, c, :], xt[s][:, c, :], vecs[:, gcol + c:gcol + c + 1], rs[:], ALU.mult, ALU.mult),
                reads=[b_x[s], b_rs, b_vecs], writes=[b_hs[s]])

    def kv(i):
        s = i % 2
        hT, b_h = hTs[s], b_hs[s]
        for kc in range(2):
            p = kc % 2

            def mm(e, kc=kc, p=p):
                ins = None
                for sw in range(2):
                    col = sw * 256 + kc * 128
                    for c in range(8):
                        ins = e.matmul(kps[p][:, sw * 512:(sw + 1) * 512], wk[:, c, col:col + 128], hT[:, c, :], start=(c == 0), stop=(c == 7))
                return ins
            st.op("tensor", mm, reads=[b_h, b_wk], writes=[b_kps[p]])
            emit_headnorm_rope(st, kps[p], b_kps[p], bones, b_bones, ssq_ps, b_ssq, sqh, b_sqh, rsh, b_rsh, t1, b_t1, t2, b_t2,
                               vecs, b_vecs, kgc, kgsc, rope[s][:, 0, :], rope[s][:, 1, :], b_rope[s],
                               [(kst[s][:, kc, :], slice(0, 128), b_kst[s])])
        seq, tl = i // 8, (i % 8) * 512
        st.dma("sync", KTv[seq][:, :, tl:tl + 512], kst[s][:], d_k[s], reads=[b_kst[s]])
        for sb_ in range(4):
            p = sb_ % 2

            def mmv(e, sb_=sb_, p=p):
                ins = None
                for c in range(8):
                    ins = e.matmul(v_ps[p][:, 0:256], hT[:, c, sb_ * 128:(sb_ + 1) * 128], wv[:, c, :], start=(c == 0), stop=(c == 7))
                return ins
            st.op("tensor", mmv, reads=[b_h, b_wv], writes=[b_vps[p]])
            st.op("vector", lambda e, sb_=sb_, p=p, s=s: e.tensor_copy(
                vst[s][:, sb_, :, 0:64], v_ps[p][:, 0:256].rearrange("p (g d) -> p g d", d=64)),
                reads=[b_vps[p]], writes=[b_vst[s]])
        st.dma("sync", VA_d[seq][:, (i % 8) * 2048:(i % 8 + 1) * 2048], vst[s][:].rearrange("p a g d -> p (a g d)"), d_vs[s], reads=[b_vst[s]])

    load(0)
    if NT > 1:
        load(1)
    pro(0)
    for i in range(NT):
        if i + 1 < NT:
            pro(i + 1)
        kv(i)
        if i + 2 < NT:
            load(i + 2)
    st.finish()


def stage_attn_q(nc, name, xT_d, tok0, wq_d, wo_d, vecs_d, nv, gcol, qgc, qgsc, ropec_d, ropes_d, KT_d, VA_d, cast_jobs=None):
    st = Stage(nc, name)
    vecs = st.sb("vecs", [128, nv], F32)
    b_vecs = Buf()
    d_v = st.dsem()
    st.dma("sync", vecs[:], vecs_d, d_v, writes=[b_vecs])
    wq = st.sb("wq", [128, 8, NQX], BF16)
    wo = st.sb("wo", [128, 8, D], BF16)
    b_wq, b_wo = Buf(), Buf()
    d_w = st.dsem()
    st.dma("gpsimd", wq[:], wq_d.rearrange("(c p) n -> p c n", p=128), d_w, writes=[b_wq])
    d_w2 = st.dsem()
    st.dma("gpsimd", wo[:], wo_d.rearrange("(c p) n -> p c n", p=128), d_w2, writes=[b_wo])
    if cast_jobs:
        emit_weight_cast(st, cast_jobs)
    kT2 = st.sb("kT2", [128, 2, L], BF16)
    b_k = Buf()
    va = st.sb("va", [128, 32, 4, 128], BF16)
    b_va = Buf()
    d_kv = st.dsem()
    st.dma("sync", kT2[:], KT_d.rearrange("g p t -> p g t"), d_kv, writes=[b_k])
    d_kv2 = st.dsem()
    st.dma("sync", va[:].rearrange("p a g d -> p (a g d)"), VA_d, d_kv2, writes=[b_va])
    ones = st.sb("ones", [128, 128], BF16)
    b_ones = Buf()
    st.op("vector", lambda e: e.memset(ones[:], 1.0), writes=[b_ones], memset=True)
    bones, b_bones = make_bones(st)
    xts = [st.sb(f"xt{i}", [128, 8, 512], F32) for i in range(2)]
    b_xs = [Buf(), Buf()]
    d_xs = [st.dsem(), st.dsem()]
    d_os = [st.dsem(), st.dsem()]
    ropes = [st.sb(f"rope{i}", [128, 2, 512], F32) for i in range(2)]
    b_ropes = [Buf(), Buf()]
    d_rs = [st.dsem(), st.dsem()]
    hT = st.sb("hT", [128, 8, 512], BF16)
    b_h = Buf()
    qT = st.sb("qT", [128, 16, 512], BF16)
    b_q = [Buf() for _ in range(8)]
    st.op("gpsimd", lambda e: e.memset(qT[:], 0.0), writes=b_q, memset=True)
    oT = st.sb("oT", [128, 8, 512], BF16)
    b_o = [Buf() for _ in range(8)]
    sq = [st.sb(f"sq{i}", [128, 2, 512], BF16) for i in range(2)]
    b_sq = [Buf(), Buf()]
    rs = st.sb("rs", [128, 512], F32)
    b_rs = Buf()
    sqh = st.sb("sqh", [128, 512], BF16)
    b_sqh = Buf()
    rsh = st.sb("rsh", [128, 512], F32)
    b_rsh = Buf()
    t1 = st.sb("t1", [128, 512], F32)
    t2 = st.sb("t2", [128, 512], F32)
    b_t1, b_t2 = Buf(), Buf()
    NPT = 4
    pT = [st.sb(f"pT{i}", [128, 1024], BF16) for i in range(NPT)]
    b_p = [Buf() for _ in range(NPT)]
    rcp = [st.sb(f"rcp{i}", [128, 512], F32) for i in range(2)]
    b_rcp = [Buf(), Buf()]
    s_ps = [st.ps([128, 1024]) for _ in range(3)]
    b_s = [Buf(), Buf(), Buf()]
    o_ps = [st.ps(), st.ps()]
    b_ops = [Buf(), Buf()]
    m_ps = [s_ps[2][:, 0:512], s_ps[2][:, 512:1024]]
    b_m = [b_s[2], b_s[2]]
    xTv = xT_d.rearrange("(c p) t -> p c t", p=128)

    def load(i):
        sl_ = i % 2
        ta = tok0 + i * 512
        st.dma("sync", xts[sl_][:], xTv[:, :, ta:ta + 512], d_xs[sl_], writes=[b_xs[sl_]])
        st.dma("sync", ropes[sl_][:, 0, :], ropec_d[:, i * 512:(i + 1) * 512], d_rs[sl_], writes=[b_ropes[sl_]])
        st.dma("sync", ropes[sl_][:, 1, :], ropes_d[:, i * 512:(i + 1) * 512], d_rs[sl_], writes=[b_ropes[sl_]])

    def pro_stats(i):
        xt, b_x = xts[i % 2], b_xs[i % 2]
        emit_norm_stats(st, xt, b_x, sq, b_sq, ones, b_ones, m_ps[1], b_m[1], rs, b_rs, lnexp=True)
        for c in range(8):
            st.op("vector", lambda e, c=c, xt=xt: e.scalar_tensor_tensor(
                hT[:, c, :], xt[:, c, :], vecs[:, gcol + c:gcol + c + 1], rs[:], ALU.mult, ALU.mult),
                reads=[b_x, b_rs, b_vecs], writes=[b_h])

    def pro_q(i, c):
        rope, b_rope = ropes[i % 2], b_ropes[i % 2]
        p = c % 2

        def mm(e):
            ins = None
            for sw in range(2):
                col = sw * 1024 + c * 128
                for cc in range(8):
                    ins = e.matmul(s_ps[p][:, sw * 512:(sw + 1) * 512], wq[:, cc, col:col + 128], hT[:, cc, :], start=(cc == 0), stop=(cc == 7))
            return ins
        st.op("tensor", mm, reads=[b_h, b_wq], writes=[b_s[p]])
        emit_headnorm_rope(st, s_ps[p], b_s[p], bones, b_bones, m_ps[0], b_m[0], sqh, b_sqh, rsh, b_rsh, t1, b_t1, t2, b_t2,
                           vecs, b_vecs, qgc, qgsc, rope[:, 0, :], rope[:, 1, :], b_rope,
                           [(qT[0:64, 2 * c, :], slice(0, 64), b_q[c]), (qT[64:128, 2 * c + 1, :], slice(64, 128), b_q[c])])

    def outproj(i, m):
        xt, b_x = xts[i % 2], b_xs[i % 2]
        p = m % 2

        def mmy(e):
            ins = None
            for c in range(8):
                ins = e.matmul(o_ps[p][:], wo[:, c, m * 128:(m + 1) * 128], oT[:, c, :], start=(c == 0), stop=(c == 7))
            return ins
        st.op("tensor", mmy, reads=b_o + [b_wo], writes=[b_ops[p]])
        st.op("vector", lambda e: e.tensor_tensor(xt[:, m, :], o_ps[p][:], xt[:, m, :], ALU.add),
              reads=[b_ops[p], b_x], writes=[b_x])

    NP = 16
    seqn = [(h, kp) for h in range(16) for kp in range(NP)]
    NN = len(seqn)

    def S(n):
        hh, kp = seqn[n]
        c, half = hh // 2, hh % 2
        g = (HEAD_A[c] if half == 0 else HEAD_B[c]) // 4
        sl = n % 3

        def mm(e):
            ins = None
            for k2 in range(2):
                kc = 2 * kp + k2
                ins = e.matmul(s_ps[sl][:, k2 * 512:(k2 + 1) * 512], kT2[:, g // 2, kc * 128:(kc + 1) * 128], qT[:, hh, :], start=True, stop=True)
            return ins
        st.op("tensor", mm, reads=[b_k, b_q[c]], writes=[b_s[sl]])
        st.op("scalar", lambda e: e.activation(pT[n % NPT][:], s_ps[sl][:], AF.Exp, scale=0.125), reads=[b_s[sl]], writes=[b_p[n % NPT]])

    def PV(n):
        h, kp = seqn[n]
        g = (HEAD_A[h // 2] if h % 2 == 0 else HEAD_B[h // 2]) // 4
        os_ = h % 2

        def mm(e):
            ins = None
            for k2 in range(2):
                kc = 2 * kp + k2
                ins = e.matmul(o_ps[os_][:], va[:, kc, g, :], pT[n % NPT][:, k2 * 512:(k2 + 1) * 512],
                               start=(kp == 0 and k2 == 0), stop=(kp == NP - 1 and k2 == 1))
            return ins
        st.op("tensor", mm, reads=[b_va, b_p[n % NPT]], writes=[b_ops[os_]])
        if kp == NP - 1:
            c, half = h // 2, h % 2
            rows = slice(64 * half, 64 * half + 64)
            st.op("vector", lambda e: e.reciprocal(rcp[os_][64:128, :], o_ps[os_][64:128, :]), reads=[b_ops[os_]], writes=[b_rcp[os_]])
            st.op("vector", lambda e: e.tensor_tensor(oT[rows, c, :], o_ps[os_][0:64, :], rcp[os_][64:128, :], ALU.mult),
                  reads=[b_ops[os_], b_rcp[os_]], writes=[b_o[c]])

    load(0)
    pro_stats(0)
    for c in range(8):
        pro_q(0, c)
    for i in range(8):
        t0 = tok0 + i * 512
        if i + 1 < 8:
            load(i + 1)
        S(0)
        S(1)
        S(2)
        for n in range(NN):
            PV(n)
            if n + 3 < NN:
                S(n + 3)
            if n + 3 == NN - 1 and i + 1 < 8:
                pro_stats(i + 1)
        for k in range(8):
            if i + 1 < 8:
                pro_q(i + 1, k)
            outproj(i, k)
        st.dma("sync", xTv[:, :, t0:t0 + 512], xts[i % 2][:], d_os[i % 2], reads=[b_xs[i % 2]])
    st.finish()


def _pair_swap_perm(n):
    p = np.arange(n)
    return p ^ 1


def q_head_perm():
    cols = []
    for c in range(8):
        for h in (HEAD_A[c], HEAD_B[c]):
            cols.append(np.arange(h * 64, (h + 1) * 64))
    return np.concatenate(cols)


def prep_attn_weights(w_in):
    q = w_in[:, :1024][:, q_head_perm()]
    k = w_in[:, 1024:1280]
    v = w_in[:, 1280:1536]
    wq = np.concatenate([q, q[:, _pair_swap_perm(1024)]], axis=1)
    wk = np.concatenate([k, k[:, _pair_swap_perm(256)]], axis=1)
    return np.ascontiguousarray(wq), np.ascontiguousarray(wk), np.ascontiguousarray(v)


def rope_tables():
    t = np.arange(L)
    row = (t // 64).astype(np.float32)
    col = (t % 64).astype(np.float32)
    inv = (10000.0 ** (-np.arange(0, 32, 2, dtype=np.float32) / 32)).astype(np.float32)
    ang = np.concatenate([row[:, None] * inv, col[:, None] * inv], axis=-1).astype(np.float32)
    c, s = np.cos(ang), np.sin(ang)
    p = np.arange(128)
    pi = (p % 64) // 2
    sign = np.where(p % 2 == 0, -1.0, 1.0).astype(np.float32)
    rc = np.ascontiguousarray(c[:, pi].T.astype(np.float32))
    rsn = np.ascontiguousarray((s[:, pi] * sign[None, :]).T.astype(np.float32))
    return rc, rsn


NFC = 17
NFP = NFC * 128
NTJ = 32


def _chunk_t():
    j = np.arange(NTJ)
    par, jj = j // 16, j % 16
    p = np.arange(128)
    return (2 * (128 * jj[:, None] + p[None, :]) + par[:, None])


def dft_tables():
    N = 2 * L
    k = np.arange(N)
    ct = np.cos(2 * np.pi * k / N)
    sn = np.sin(2 * np.pi * k / N)
    f = np.arange(NFP)
    tt = _chunk_t().reshape(-1)
    ft = (f[:, None] * tt[None, :]) % N
    C = ct[ft].astype(np.float32)
    S = sn[ft].astype(np.float32)
    wf = np.full(NFP, 2.0, np.float32)
    wf[0] = 1.0
    wf[2049:] = 0.0
    bf = ml_dtypes.bfloat16
    FC = np.ascontiguousarray(C.reshape(NFC, 128, NTJ, 128).transpose(0, 3, 2, 1)).astype(bf)
    FS = np.ascontiguousarray(S.reshape(NFC, 128, NTJ, 128).transpose(0, 3, 2, 1)).astype(bf)
    Gc = (C * (wf / N)[:, None]).reshape(NFC, 128, NTJ, 128)
    Gs = (-S * (wf / N)[:, None]).reshape(NFC, 128, NTJ, 128)
    GC = np.ascontiguousarray(Gc.transpose(2, 1, 0, 3)).astype(bf)
    GS = np.ascontiguousarray(Gs.transpose(2, 1, 0, 3)).astype(bf)
    return FC, FS, GC, GS


def hyena_pos_tables():
    t = np.linspace(0.0, 1.0, L, dtype=np.float32)[:, None]
    bands = 16
    fr = np.linspace(1e-4, bands - 1, bands, dtype=np.float32)
    w = (2.0 * math.pi * np.arange(L, dtype=np.float32)[:, None] / L).astype(np.float32)
    z = np.concatenate([t, np.cos(fr * w), -np.sin(fr * w)], axis=-1).astype(np.float32)
    zT = np.ascontiguousarray(z.T)
    trow = np.ascontiguousarray(np.broadcast_to(t[:, 0][None, :], (128, L))).astype(np.float32)
    return zT, trow


def emit_tm_store(st, src16, b_src, identb, b_idb, tp_ps, b_tp, stg, b_stg, d_stg, dst_d, cc, kctr):
    sl = kctr[0] % 2
    kctr[0] += 1
    for q4 in range(4):
        p = q4 % 2

        def mm(e, q4=q4, p=p):
            ins = None
            for k in range(8):
                j = q4 * 8 + k
                par, jj = j // 16, j % 16
                ins = e.transpose(tp_ps[p][:, k * 128:(k + 1) * 128], src16[:, 256 * jj + par:256 * (jj + 1):2], identb[:])
            return ins
        st.op("tensor", mm, reads=[b_src, b_idb], writes=[b_tp[p]])
        if q4 % 2 == 0:
            st.op("vector", lambda e, q4=q4, p=p: e.tensor_copy(stg[sl][:, q4 * 8:(q4 + 1) * 8, :], tp_ps[p][:].rearrange("p (k c) -> p k c", c=128)),
                  reads=[b_tp[p]], writes=[b_stg[sl]])
        else:
            st.op("scalar", lambda e, q4=q4, p=p: e.activation(stg[sl][:, q4 * 8:(q4 + 1) * 8, :], tp_ps[p][:].rearrange("p (k c) -> p k c", c=128), AF.Copy),
                  reads=[b_tp[p]], writes=[b_stg[sl]])
    st.dma("sync", dst_d.rearrange("(tc p) c -> p tc c", p=128)[:, :, cc * 128:(cc + 1) * 128], stg[sl][:], d_stg[sl], reads=[b_stg[sl]])


def stage_hy_filter(nc, name, zT_d, trow_d, w1_d, w2_d, w3_d, wo_d, vecs_d, nv, vc, identb_d, ATM_d, BTM_d):
    st = Stage(nc, name)
    vecs = st.sb("vecs", [128, nv], F32)
    b_vecs = Buf()
    st.dma("sync", vecs[:], vecs_d, st.dsem(), writes=[b_vecs])
    identb = st.sb("identb", [128, 128], BF16)
    b_idb = Buf()
    st.dma("gpsimd", identb[:], identb_d, st.dsem(), writes=[b_idb])
    zT = st.sb("zT", [33, L], F32)
    b_z = Buf()
    st.dma("sync", zT[:], zT_d, st.dsem(), writes=[b_z])
    trow = st.sb("trow", [128, L], F32)
    b_tr = Buf()
    st.dma("sync", trow[:], trow_d, st.dsem(), writes=[b_tr])
    w1 = st.sb("w1", [33, 64], F32)
    w2 = st.sb("w2", [64, 64], F32)
    w3 = st.sb("w3", [64, 64], F32)
    wo = st.sb("wo", [64, 2048], F32)
    b_w = Buf()
    d_w = st.dsem()
    st.dma("sync", w1[:], w1_d, d_w, writes=[b_w])
    st.dma("sync", w2[:], w2_d, d_w, writes=[b_w])
    st.dma("sync", w3[:], w3_d, d_w, writes=[b_w])
    st.dma("sync", wo[:], wo_d, d_w, writes=[b_w])
    hid = [st.sb(f"hid{i}", [64, L], F32) for i in range(2)]
    b_hid = [Buf(), Buf()]
    tmp = [st.sb(f"tmp{i}", [128, 512], F32) for i in range(2)]
    b_tmp = [Buf(), Buf()]
    tq_ = [st.sb(f"tq{i}", [128, 512], F32) for i in range(2)]
    b_tq = [Buf(), Buf()]
    sm = st.sb("sm", [128, 32], F32)
    b_sm = Buf()
    ps = [st.ps(), st.ps()]
    b_ps = [Buf(), Buf()]
    tp_ps = [st.ps([128, 1024], BF16), st.ps([128, 1024], BF16)]
    b_tp = [Buf(), Buf()]
    fq = vc["freq"]
    for k, key in enumerate(("b1", "b2", "b3")):
        st.op("vector", lambda e, k=k, key=key: e.tensor_tensor(sm[0:64, k:k + 1], vecs[0:64, vc[key]:vc[key] + 1], vecs[0:64, fq:fq + 1], ALU.mult),
              reads=[b_vecs], writes=[b_sm])
    dc = vc["decay"]
    st.op("scalar", lambda e: e.activation(sm[:, 8:24], vecs[:, dc:dc + 16], AF.Abs), reads=[b_vecs], writes=[b_sm])
    st.op("vector", lambda e: e.tensor_scalar(sm[:, 8:24], sm[:, 8:24], -1.0, None, ALU.mult), reads=[b_sm], writes=[b_sm])
    kk = 0
    srcs = [(zT, b_z, 33), None, None]
    ws = [w1, w2, w3]
    for k in range(3):
        if k == 0:
            src, b_src, K = zT, b_z, 33
        else:
            src, b_src, K = hid[(k - 1) % 2], b_hid[(k - 1) % 2], 64
        dst, b_dst = hid[k % 2], b_hid[k % 2]
        for tl in range(8):
            p = kk % 2
            kk += 1
            st.op("tensor", lambda e, p=p, k=k, K=K, src=src, tl=tl: e.matmul(ps[p][0:64, :], ws[k][0:K, :], src[0:K, tl * 512:(tl + 1) * 512], start=True, stop=True),
                  reads=[b_src, b_w], writes=[b_ps[p]])
            st.op("vector", lambda e, p=p, k=k: e.tensor_scalar(tmp[p][0:64, :], ps[p][0:64, :], vecs[0:64, fq:fq + 1], sm[0:64, k:k + 1], ALU.mult, ALU.add),
                  reads=[b_ps[p], b_vecs, b_sm], writes=[b_tmp[p]])
            st.op("scalar", lambda e, p=p: e.activation(tmp[p][0:64, :], tmp[p][0:64, :], AF.Sin, scale=1.0 / 9.0),
                  reads=[b_tmp[p]], writes=[b_tmp[p]])
            for rep in range(2):
                st.op("vector", lambda e, p=p: e.tensor_tensor(tq_[p][0:64, :], tmp[p][0:64, :], tmp[p][0:64, :], ALU.mult),
                      reads=[b_tmp[p]], writes=[b_tq[p]])
                st.op("vector", lambda e, p=p: e.tensor_scalar(tq_[p][0:64, :], tq_[p][0:64, :], -4.0, 3.0, ALU.mult, ALU.add),
                      reads=[b_tq[p]], writes=[b_tq[p]])
                if rep == 0:
                    st.op("vector", lambda e, p=p: e.tensor_tensor(tmp[p][0:64, :], tmp[p][0:64, :], tq_[p][0:64, :], ALU.mult),
                          reads=[b_tmp[p], b_tq[p]], writes=[b_tmp[p]])
                else:
                    st.op("vector", lambda e, p=p, dst=dst, tl=tl: e.tensor_tensor(dst[:, tl * 512:(tl + 1) * 512], tmp[p][0:64, :], tq_[p][0:64, :], ALU.mult),
                          reads=[b_tmp[p], b_tq[p]], writes=[b_dst])
    hid3, b_h3 = hid[2 % 2], b_hid[2 % 2]
    hf = st.sb("hf", [128, L], F32)
    hb = st.sb("hb", [128, L], F32)
    b_hf, b_hb = Buf(), Buf()
    sc = st.sb("sc", [128, L], F32)
    b_sc = Buf()
    a16 = st.sb("a16", [128, L], BF16)
    b16 = st.sb("b16", [128, L], BF16)
    b_a16, b_b16 = Buf(), Buf()
    stg = [st.sb(f"stg{i}", [128, 32, 128], BF16) for i in range(2)]
    b_stg = [Buf(), Buf()]
    d_stg = [st.dsem(), st.dsem()]
    kctr = [0]
    for cc in range(8):
        for dr, (hh, b_hh) in enumerate(((hf, b_hf), (hb, b_hb))):
            chunk = dr * 8 + cc
            for tl in range(8):
                p = kk % 2
                kk += 1
                st.op("tensor", lambda e, p=p, chunk=chunk, tl=tl: e.matmul(ps[p][:], wo[:, chunk * 128:(chunk + 1) * 128], hid3[:, tl * 512:(tl + 1) * 512], start=True, stop=True),
                      reads=[b_h3, b_w], writes=[b_ps[p]])
                st.op("scalar", lambda e, p=p, chunk=chunk, tl=tl: e.activation(tmp[p][:], trow[:, tl * 512:(tl + 1) * 512], AF.Exp, scale=sm[:, 8 + chunk:9 + chunk]),
                      reads=[b_tr, b_sm], writes=[b_tmp[p]])
                st.op("vector", lambda e, p=p, hh=hh, tl=tl: e.tensor_tensor(hh[:, tl * 512:(tl + 1) * 512], ps[p][:], tmp[p][:], ALU.mult),
                      reads=[b_ps[p], b_tmp[p]], writes=[b_hh])
        st.op("vector", lambda e: e.memset(hb[:, 0:1], 0.0), writes=[b_hb], memset=True)
        for hh_ in range(2):
            st.op("scalar", lambda e, hh_=hh_: e.activation(sc[:, hh_ * 2048:(hh_ + 1) * 2048], hf[:, hh_ * 2048:(hh_ + 1) * 2048], AF.Abs), reads=[b_hf], writes=[b_sc])
        st.op("vector", lambda e: e.reduce_sum(sm[:, 24:25], sc[:], AX.X), reads=[b_sc], writes=[b_sm])
        for hh_ in range(2):
            st.op("scalar", lambda e, hh_=hh_: e.activation(sc[:, hh_ * 2048:(hh_ + 1) * 2048], hb[:, hh_ * 2048:(hh_ + 1) * 2048], AF.Abs), reads=[b_hb], writes=[b_sc])
        st.op("vector", lambda e: e.reduce_sum(sm[:, 25:26], sc[:], AX.X), reads=[b_sc], writes=[b_sm])
        st.op("vector", lambda e: e.tensor_tensor(sm[:, 26:27], sm[:, 24:25], sm[:, 25:26], ALU.add), reads=[b_sm], writes=[b_sm])
        st.op("vector", lambda e: e.reciprocal(sm[:, 27:28], sm[:, 26:27]), reads=[b_sm], writes=[b_sm])
        st.op("vector", lambda e: e.tensor_tensor(sc[:], hf[:], hb[:], ALU.add), reads=[b_hf, b_hb], writes=[b_sc])
        st.op("vector", lambda e: e.tensor_scalar(a16[:], sc[:], sm[:, 27:28], None, ALU.mult), reads=[b_sc, b_sm], writes=[b_a16])
        st.op("vector", lambda e: e.tensor_tensor(sc[:], hb[:], hf[:], ALU.subtract), reads=[b_hf, b_hb, b_a16], writes=[b_sc])
        st.op("vector", lambda e: e.tensor_scalar(b16[:], sc[:], sm[:, 27:28], None, ALU.mult), reads=[b_sc, b_sm], writes=[b_b16])
        emit_tm_store(st, a16, b_a16, identb, b_idb, tp_ps, b_tp, stg, b_stg, d_stg, ATM_d, cc, kctr)
        emit_tm_store(st, b16, b_b16, identb, b_idb, tp_ps, b_tp, stg, b_stg, d_stg, BTM_d, cc, kctr)
    st.finish()


def stage_hy_kdft(nc, name, ATM_d, BTM_d, FC_d, FS_d, KH_d):
    st = Stage(nc, name)
    a_tm = st.sb("a_tm", [128, NTJ, D], BF16)
    b_tm = st.sb("b_tm", [128, NTJ, D], BF16)
    b_a, b_b = Buf(), Buf()
    st.dma("sync", a_tm[:], ATM_d.rearrange("(tc p) c -> p tc c", p=128), st.dsem(), writes=[b_a])
    st.dma("sync", b_tm[:], BTM_d.rearrange("(tc p) c -> p tc c", p=128), st.dsem(), writes=[b_b])
    tb = [st.sb(f"tb{i}", [128, 2, NTJ, 128], BF16) for i in range(2)]
    b_tb = [Buf(), Buf()]
    d_tb = [st.dsem(), st.dsem()]
    kst = [st.sb(f"kst{i}", [128, 4, D], F32) for i in range(2)]
    b_kst = [Buf(), Buf()]
    d_k = [st.dsem(), st.dsem()]
    osb = [st.sb(f"osb{i}", [128, 2, 512], F32) for i in range(2)]
    b_osb = [Buf(), Buf()]
    acc = [[st.ps() for _ in range(4)] for _ in range(2)]
    b_acc = [[Buf() for _ in range(4)] for _ in range(2)]
    KHv = KH_d.rearrange("r (fc p) c -> fc p r c", p=128)

    def load(fc):
        s = fc % 2
        st.dma("sync", tb[s][:, 0], FC_d[fc], d_tb[s], writes=[b_tb[s]])
        st.dma("sync", tb[s][:, 1], FS_d[fc], d_tb[s], writes=[b_tb[s]])

    load(0)
    it = 0
    for fc in range(NFC):
        s = fc % 2
        if fc + 1 < NFC:
            load(fc + 1)
        for hc in range(2):
            q = it % 2
            it += 1
            cs = slice(hc * 512, (hc + 1) * 512)
            for which, (src, b_src) in enumerate(((a_tm, b_a), (b_tm, b_b))):
                for par in range(2):
                    k = which * 2 + par

                    def mm(e, which=which, par=par, k=k, src=src, s=s, q=q, cs=cs):
                        ins = None
                        for jj in range(16):
                            j = par * 16 + jj
                            ins = e.matmul(acc[q][k][:], tb[s][:, which, j, :], src[:, j, cs], start=(jj == 0), stop=(jj == 15))
                        return ins
                    st.op("tensor", mm, reads=[b_tb[s], b_src], writes=[b_acc[q][k]])
            st.op("scalar", lambda e, q=q: e.activation(osb[q][:, 0, :], acc[q][1][:], AF.Copy), reads=[b_acc[q][1]], writes=[b_osb[q]])
            st.op("scalar", lambda e, q=q: e.activation(osb[q][:, 1, :], acc[q][3][:], AF.Copy), reads=[b_acc[q][3]], writes=[b_osb[q]])
            st.op("vector", lambda e, q=q, s=s, cs=cs: e.tensor_tensor(kst[s][:, 0, cs], acc[q][0][:], osb[q][:, 0, :], ALU.add), reads=[b_acc[q][0], b_osb[q]], writes=[b_kst[s]])
            st.op("vector", lambda e, q=q, s=s, cs=cs: e.tensor_tensor(kst[s][:, 1, cs], acc[q][2][:], osb[q][:, 1, :], ALU.add), reads=[b_acc[q][2], b_osb[q]], writes=[b_kst[s]])
            st.op("vector", lambda e, q=q, s=s, cs=cs: e.tensor_tensor(kst[s][:, 2, cs], acc[q][0][:], osb[q][:, 0, :], ALU.subtract), reads=[b_acc[q][0], b_osb[q]], writes=[b_kst[s]])
            st.op("vector", lambda e, q=q, s=s, cs=cs: e.scalar_tensor_tensor(kst[s][:, 3, cs], acc[q][2][:], -1.0, osb[q][:, 1, :], ALU.mult, ALU.add), reads=[b_acc[q][2], b_osb[q]], writes=[b_kst[s]])
        if fc == NFC - 1:
            st.op("vector", lambda e, s=s: e.memset(kst[s][0:1, 2:4, :], 0.0), writes=[b_kst[s]], memset=True)
        st.dma("sync", KHv[fc], kst[s][:], d_k[s], reads=[b_kst[s]])
    st.finish()


def stage_hy_in(nc, name, xT_d, tok0, w_d, vecs_d, nv, gcol, vc, identb_d, VTM_d, X0TM_d):
    st = Stage(nc, name)
    vecs = st.sb("vecs", [128, nv], F32)
    b_vecs = Buf()
    st.dma("sync", vecs[:], vecs_d, st.dsem(), writes=[b_vecs])
    identb = st.sb("identb", [128, 128], BF16)
    b_idb = Buf()
    st.dma("gpsimd", identb[:], identb_d, st.dsem(), writes=[b_idb])
    ones = st.sb("ones", [128, 128], BF16)
    b_ones = Buf()
    st.op("vector", lambda e: e.memset(ones[:], 1.0), writes=[b_ones], memset=True)
    hT = st.sb("hT", [128, 8, L], BF16)
    b_h = Buf()
    xt = st.sb("xt", [128, 8, 512], F32)
    b_x = Buf()
    d_x = st.dsem()
    sq = [st.sb(f"sq{i}", [128, 2, 512], BF16) for i in range(2)]
    b_sq = [Buf(), Buf()]
    rs = st.sb("rs", [128, 512], F32)
    b_rs = Buf()
    wc = [st.sb(f"wc{i}", [128, 8, 3, 128], BF16) for i in range(2)]
    b_wc = [Buf(), Buf()]
    d_wc = [st.dsem(), st.dsem()]
    u2 = [st.sb(f"u{i}", [128, L], F32) for i in range(2)]
    b_u2 = [Buf(), Buf()]
    uc = [st.sb(f"uc{i}", [128, L], F32) for i in range(2)]
    b_uc = [Buf(), Buf()]
    g16 = st.sb("g16", [128, L], BF16)
    b_g16 = Buf()
    x16 = st.sb("x16", [128, L], BF16)
    b_x16 = Buf()
    stg = [st.sb(f"stg{i}", [128, 32, 128], BF16) for i in range(2)]
    b_stg = [Buf(), Buf()]
    d_stg = [st.dsem(), st.dsem()]
    ss_ps = st.ps()
    b_ss = Buf()
    ps = [st.ps() for _ in range(4)]
    b_ps = [Buf() for _ in range(4)]
    tp_ps = [st.ps([128, 1024], BF16), st.ps([128, 1024], BF16)]
    b_tp = [Buf(), Buf()]
    xTv = xT_d.rearrange("(c p) t -> p c t", p=128)
    wv_ = w_d.rearrange("(c p) (a n) -> p c a n", p=128, a=3)

    def loadw(cc):
        s = cc % 2
        for a in range(3):
            st.dma("gpsimd", wc[s][:, :, a, :], wv_[:, :, a, cc * 128:(cc + 1) * 128], d_wc[s], writes=[b_wc[s]])

    loadw(0)
    for i in range(8):
        t0 = tok0 + i * 512
        st.dma("sync", xt[:], xTv[:, :, t0:t0 + 512], d_x, writes=[b_x])
        emit_norm_stats(st, xt, b_x, sq, b_sq, ones, b_ones, ss_ps, b_ss, rs, b_rs, lnexp=True)
        for c in range(8):
            st.op("vector", lambda e, c=c, i=i: e.scalar_tensor_tensor(
                hT[:, c, i * 512:(i + 1) * 512], xt[:, c, :], vecs[:, gcol + c:gcol + c + 1], rs[:], ALU.mult, ALU.mult),
                reads=[b_x, b_rs, b_vecs], writes=[b_h])
    kk = 0
    npart = 0
    kctr = [0]
    pending = []
    bi, cw, cb = vc["b_in"], vc["conv_w"], vc["conv_b"]
    for cc in range(8):
        s = cc % 2
        if cc + 1 < 8:
            loadw(cc + 1)
        for part, dst_i in ((2, 0), (1, 1), (0, 1)):
            col = part * 8 + cc
            dst, b_dst = uc[dst_i], b_uc[dst_i]
            u, b_u = u2[npart % 2], b_u2[npart % 2]
            npart += 1
            for tl in range(8):
                p = kk % 4
                kk += 1

                def mm(e, p=p, part=part, tl=tl, s=s):
                    ins = None
                    for c in range(8):
                        ins = e.matmul(ps[p][:], wc[s][:, c, part, :], hT[:, c, tl * 512:(tl + 1) * 512], start=(c == 0), stop=(c == 7))
                    return ins
                st.op("tensor", mm, reads=[b_h, b_wc[s]], writes=[b_ps[p]])
                st.op("scalar", lambda e, p=p, tl=tl, col=col, u=u: e.activation(u[:, tl * 512:(tl + 1) * 512], ps[p][:], AF.Identity, bias=vecs[:, bi + col:bi + col + 1], scale=1.0),
                      reads=[b_ps[p], b_vecs], writes=[b_u])
            for fn_ in pending:
                fn_()
            pending.clear()
            for hh in range(2):
                st.op("scalar", lambda e, hh=hh, col=col, dst=dst, u=u: e.activation(
                    dst[:, hh * 2048:(hh + 1) * 2048], u[:, hh * 2048:(hh + 1) * 2048], AF.Identity,
                    bias=vecs[:, cb + col:cb + col + 1], scale=vecs[:, cw + 24 + col:cw + 24 + col + 1]),
                    reads=[b_u, b_vecs], writes=[b_dst])
            st.op("vector", lambda e, col=col, dst=dst, u=u: e.scalar_tensor_tensor(
                dst[:, 1:L], u[:, 0:L - 1], vecs[:, cw + col:cw + col + 1], dst[:, 1:L], ALU.mult, ALU.add),
                reads=[b_u, b_dst, b_vecs], writes=[b_dst])
            st.op("vector", lambda e, col=col, dst=dst, u=u: e.scalar_tensor_tensor(
                dst[:, 0:L - 1], u[:, 1:L], vecs[:, cw + 48 + col:cw + 48 + col + 1], dst[:, 0:L - 1], ALU.mult, ALU.add),
                reads=[b_u, b_dst, b_vecs], writes=[b_dst])
            if part == 1:
                st.op("vector", lambda e: e.tensor_tensor(g16[:], uc[0][:], uc[1][:], ALU.mult), reads=[b_uc[0], b_uc[1]], writes=[b_g16])
                pending.append(lambda cc=cc: emit_tm_store(st, g16, b_g16, identb, b_idb, tp_ps, b_tp, stg, b_stg, d_stg, VTM_d, cc, kctr))
            if part == 0:
                for hh in range(2):
                    st.op("scalar", lambda e, hh=hh: e.activation(x16[:, hh * 2048:(hh + 1) * 2048], uc[1][:, hh * 2048:(hh + 1) * 2048], AF.Copy), reads=[b_uc[1]], writes=[b_x16])
                pending.append(lambda cc=cc: emit_tm_store(st, x16, b_x16, identb, b_idb, tp_ps, b_tp, stg, b_stg, d_stg, X0TM_d, cc, kctr))
    for fn_ in pending:
        fn_()
    st.finish()


def stage_hy_fwd(nc, name, VTM_d, FC_d, FS_d, KH_d, YH_d, cast_jobs=None):
    st = Stage(nc, name)
    v_tm = st.sb("v_tm", [128, NTJ, D], BF16)
    b_v = Buf()
    st.dma("sync", v_tm[:], VTM_d.rearrange("(tc p) c -> p tc c", p=128), st.dsem(), writes=[b_v])
    if cast_jobs:
        emit_weight_cast(st, cast_jobs)
    tb = [st.sb(f"tb{i}", [128, 2, NTJ, 128], BF16) for i in range(2)]
    b_tb = [Buf(), Buf()]
    d_tb = [st.dsem(), st.dsem()]
    kh = [st.sb(f"kh{i}", [128, 4, D], F32) for i in range(2)]
    b_kh = [Buf(), Buf()]
    d_kh = [st.dsem(), st.dsem()]
    yst = [st.sb(f"yst{i}", [128, 4, D], BF16) for i in range(2)]
    b_yst = [Buf(), Buf()]
    d_y = [st.dsem(), st.dsem()]
    NB = 6
    bt = [[st.sb(f"bt{q}_{i}", [128, 512], F32) for i in range(NB)] for q in range(2)]
    b_bt = [[Buf() for _ in range(NB)] for _ in range(2)]
    mt = [st.sb(f"mt{i}", [128, 512], F32) for i in range(8)]
    b_mt = [Buf() for _ in range(8)]
    acc = [[st.ps() for _ in range(4)] for _ in range(2)]
    b_acc = [[Buf() for _ in range(4)] for _ in range(2)]
    KHv = KH_d.rearrange("r (fc p) c -> fc p r c", p=128)
    YHv = YH_d.rearrange("r (fc p) c -> fc p r c", p=128)

    def load(fc):
        s = fc % 2
        st.dma("sync", tb[s][:, 0], FC_d[fc], d_tb[s], writes=[b_tb[s]])
        st.dma("sync", tb[s][:, 1], FS_d[fc], d_tb[s], writes=[b_tb[s]])
        st.dma("sync", kh[s][:], KHv[fc], d_kh[s], writes=[b_kh[s]])

    def tt(eng, out, b_out, i0, b0, i1, b1, op):
        st.op(eng, lambda e: e.tensor_tensor(out, i0, i1, op), reads=[b0, b1], writes=[b_out])

    load(0)
    it = 0
    for fc in range(NFC):
        s = fc % 2
        if fc + 1 < NFC:
            load(fc + 1)
        for hc in range(2):
            q = it % 2
            it += 1
            cs = slice(hc * 512, (hc + 1) * 512)
            for which in range(2):
                for par in range(2):
                    k = which * 2 + par

                    def mm(e, which=which, par=par, k=k, s=s, q=q, cs=cs):
                        ins = None
                        for jj in range(16):
                            j = par * 16 + jj
                            ins = e.matmul(acc[q][k][:], tb[s][:, which, j, :], v_tm[:, j, cs], start=(jj == 0), stop=(jj == 15))
                        return ins
                    st.op("tensor", mm, reads=[b_tb[s], b_v], writes=[b_acc[q][k]])
            B, bB = bt[q], b_bt[q]
            A_, bA = acc[q], b_acc[q]
            st.op("scalar", lambda e, B=B, A_=A_: e.activation(B[0][:], A_[1][:], AF.Copy), reads=[bA[1]], writes=[bB[0]])
            st.op("scalar", lambda e, B=B, A_=A_: e.activation(B[1][:], A_[3][:], AF.Copy), reads=[bA[3]], writes=[bB[1]])
            tt("vector", B[2][:], bB[2], A_[0][:], bA[0], B[0][:], bB[0], ALU.add)
            tt("vector", B[3][:], bB[3], A_[0][:], bA[0], B[0][:], bB[0], ALU.subtract)
            tt("vector", B[4][:], bB[4], A_[2][:], bA[2], B[1][:], bB[1], ALU.add)
            tt("vector", B[5][:], bB[5], A_[2][:], bA[2], B[1][:], bB[1], ALU.subtract)
            K = kh[s]
            bK = b_kh[s]
            tt("vector", mt[0][:], b_mt[0], B[2][:], bB[2], K[:, 0, cs], bK, ALU.mult)
            tt("vector", mt[1][:], b_mt[1], B[4][:], bB[4], K[:, 1, cs], bK, ALU.mult)
            tt("vector", mt[0][:], b_mt[0], mt[0][:], b_mt[0], mt[1][:], b_mt[1], ALU.add)
            tt("vector", mt[2][:], b_mt[2], B[2][:], bB[2], K[:, 1, cs], bK, ALU.mult)
            tt("vector", mt[3][:], b_mt[3], B[4][:], bB[4], K[:, 0, cs], bK, ALU.mult)
            tt("vector", mt[2][:], b_mt[2], mt[2][:], b_mt[2], mt[3][:], b_mt[3], ALU.subtract)
            tt("gpsimd", mt[4][:], b_mt[4], B[3][:], bB[3], K[:, 2, cs], bK, ALU.mult)
            tt("gpsimd", mt[5][:], b_mt[5], B[5][:], bB[5], K[:, 3, cs], bK, ALU.mult)
            tt("gpsimd", mt[4][:], b_mt[4], mt[4][:], b_mt[4], mt[5][:], b_mt[5], ALU.subtract)
            tt("gpsimd", mt[6][:], b_mt[6], B[3][:], bB[3], K[:, 3, cs], bK, ALU.mult)
            tt("gpsimd", mt[7][:], b_mt[7], B[5][:], bB[5], K[:, 2, cs], bK, ALU.mult)
            tt("gpsimd", mt[6][:], b_mt[6], mt[6][:], b_mt[6], mt[7][:], b_mt[7], ALU.add)
            Y = yst[s]
            bY = b_yst[s]
            tt("vector", Y[:, 0, cs], bY, mt[0][:], b_mt[0], mt[4][:], b_mt[4], ALU.add)
            tt("vector", Y[:, 1, cs], bY, mt[2][:], b_mt[2], mt[6][:], b_mt[6], ALU.subtract)
            tt("gpsimd", Y[:, 2, cs], bY, mt[0][:], b_mt[0], mt[4][:], b_mt[4], ALU.subtract)
            tt("gpsimd", Y[:, 3, cs], bY, mt[2][:], b_mt[2], mt[6][:], b_mt[6], ALU.add)
        st.dma("sync", YHv[fc], yst[s][:], d_y[s], reads=[b_yst[s]])
    st.finish()


def stage_hy_inv(nc, name, YH_d, GC_d, GS_d, VTM_d, X0TM_d, dbc_d, identb_d, ZT_d):
    st = Stage(nc, name)
    identb = st.sb("identb", [128, 128], BF16)
    b_idb = Buf()
    st.dma("gpsimd", identb[:], identb_d, st.dsem(), writes=[b_idb])
    yh = st.sb("yh", [128, 4, NFC, D], BF16)
    b_yh = [Buf() for _ in range(4)]
    YHv = YH_d.rearrange("r (fc p) c -> r p fc c", p=128)
    for r in range(4):
        st.dma("sync", yh[:, r], YHv[r], st.dsem(), writes=[b_yh[r]])
    dbc = st.sb("dbc", [128, D], F32)
    b_dbc = Buf()
    st.dma("sync", dbc[:], dbc_d, st.dsem(), writes=[b_dbc])
    NS = 3
    tb = [st.sb(f"tb{i}", [128, 2, NFC, 128], BF16) for i in range(NS)]
    b_tb = [Buf() for _ in range(NS)]
    d_tb = [st.dsem() for _ in range(NS)]
    vx = [st.sb(f"vx{i}", [128, 2, D], BF16) for i in range(NS)]
    b_vx = [Buf() for _ in range(NS)]
    d_vx = [st.dsem() for _ in range(NS)]
    tm = [st.sb(f"tm{i}", [128, D], F32) for i in range(2)]
    b_tm = [Buf(), Buf()]
    z16 = [st.sb(f"z16{i}", [128, D], BF16) for i in range(2)]
    b_z = [Buf(), Buf()]
    zst = [st.sb(f"zst{i}", [128, 8, 256], BF16) for i in range(2)]
    b_zst = [Buf(), Buf()]
    d_z = [st.dsem(), st.dsem()]
    y_ps = [st.ps([128, 1024]) for _ in range(2)]
    b_y = [Buf(), Buf()]
    tp_ps = [st.ps([128, 1024], BF16), st.ps([128, 1024], BF16)]
    b_tp = [Buf(), Buf()]
    ZTv = ZT_d.rearrange("(c p) t -> p c t", p=128)
    order = [(jj, par) for jj in range(16) for par in range(2)]

    def load(n):
        jj, par = order[n]
        j = par * 16 + jj
        s3 = n % NS
        st.dma("sync", tb[s3][:, 0], GC_d[j], d_tb[s3], writes=[b_tb[s3]])
        st.dma("sync", tb[s3][:, 1], GS_d[j], d_tb[s3], writes=[b_tb[s3]])
        st.dma("sync", vx[s3][:, 0, :], VTM_d[j * 128:(j + 1) * 128, :], d_vx[s3], writes=[b_vx[s3]])
        st.dma("sync", vx[s3][:, 1, :], X0TM_d[j * 128:(j + 1) * 128, :], d_vx[s3], writes=[b_vx[s3]])

    load(0)
    load(1)
    deferred = None
    for n in range(32):
        jj, par = order[n]
        s = n % 2
        s3 = n % NS
        zs = jj % 2
        if n + 2 < 32:
            load(n + 2)

        def mm(e, s=s, par=par, s3=s3):
            ins = None
            for hc in range(2):
                k = 0
                for which in range(2):
                    for fc in range(NFC):
                        ins = e.matmul(y_ps[s][:, hc * 512:(hc + 1) * 512], tb[s3][:, which, fc, :], yh[:, 2 * par + which, fc, hc * 512:(hc + 1) * 512],
                                       start=(k == 0), stop=(k == 2 * NFC - 1))
                        k += 1
            return ins
        st.op("tensor", mm, reads=[b_tb[s3], b_yh[2 * par], b_yh[2 * par + 1]], writes=[b_y[s]])
        st.op("gpsimd", lambda e, s=s, s3=s3: e.tensor_tensor(tm[s][:], vx[s3][:, 0, :], dbc[:], ALU.mult), reads=[b_vx[s3], b_dbc], writes=[b_tm[s]])
        st.op("vector", lambda e, s=s: e.tensor_tensor(tm[s][:], y_ps[s][:], tm[s][:], ALU.add), reads=[b_y[s], b_tm[s]], writes=[b_tm[s]])
        st.op("gpsimd", lambda e, s=s, s3=s3: e.tensor_tensor(z16[s][:], tm[s][:], vx[s3][:, 1, :], ALU.mult), reads=[b_tm[s], b_vx[s3]], writes=[b_z[s]])

        def emit_tr(s=s, zs=zs, par=par, jj=jj):
            def tr(e):
                ins = None
                for c in range(8):
                    ins = e.transpose(tp_ps[s][:, c * 128:(c + 1) * 128], z16[s][:, c * 128:(c + 1) * 128], identb[:])
                return ins
            st.op("tensor", tr, reads=[b_z[s], b_idb], writes=[b_tp[s]])
            st.op("scalar", lambda e: e.activation(zst[zs][:, :, par:256:2], tp_ps[s][:].rearrange("p (c t) -> p c t", t=128), AF.Copy),
                  reads=[b_tp[s]], writes=[b_zst[zs]])
            if par == 1:
                st.dma("sync", ZTv[:, :, jj * 256:(jj + 1) * 256], zst[zs][:], d_z[zs], reads=[b_zst[zs]])
        if deferred is not None:
            deferred()
        deferred = emit_tr
    deferred()
    st.finish()


def stage_hy_out(nc, name, xT_d, tok0, ZT_d, wo_d, vecs_d, nv, bocol):
    st = Stage(nc, name)
    vecs = st.sb("vecs", [128, nv], F32)
    b_vecs = Buf()
    st.dma("sync", vecs[:], vecs_d, st.dsem(), writes=[b_vecs])
    wo = st.sb("wo", [128, 8, D], BF16)
    b_wo = Buf()
    st.dma("gpsimd", wo[:], wo_d.rearrange("(c p) n -> p c n", p=128), st.dsem(), writes=[b_wo])
    xt = [st.sb(f"xt{i}", [128, 8, 512], F32) for i in range(2)]
    b_x = [Buf(), Buf()]
    d_x = [st.dsem(), st.dsem()]
    d_o = [st.dsem(), st.dsem()]
    zt = [st.sb(f"zt{i}", [128, 8, 512], BF16) for i in range(2)]
    b_z = [Buf(), Buf()]
    d_zt = [st.dsem(), st.dsem()]
    y_ps = [st.ps(), st.ps()]
    b_y = [Buf(), Buf()]
    xTv = xT_d.rearrange("(c p) t -> p c t", p=128)
    ZTv = ZT_d.rearrange("(c p) t -> p c t", p=128)

    def load(i):
        s = i % 2
        st.dma("sync", xt[s][:], xTv[:, :, tok0 + i * 512:tok0 + (i + 1) * 512], d_x[s], writes=[b_x[s]])
        st.dma("sync", zt[s][:], ZTv[:, :, i * 512:(i + 1) * 512], d_zt[s], writes=[b_z[s]])

    load(0)
    for i in range(8):
        s = i % 2
        if i + 1 < 8:
            load(i + 1)
        for m in range(8):
            p = m % 2

            def mm(e, m=m, p=p, s=s):
                ins = None
                for c in range(8):
                    ins = e.matmul(y_ps[p][:], wo[:, c, m * 128:(m + 1) * 128], zt[s][:, c, :], start=(c == 0), stop=(c == 7))
                return ins
            st.op("tensor", mm, reads=[b_z[s], b_wo], writes=[b_y[p]])
            st.op("vector", lambda e, m=m, p=p, s=s: e.scalar_tensor_tensor(
                xt[s][:, m, :], y_ps[p][:], vecs[:, bocol + m:bocol + m + 1], xt[s][:, m, :], ALU.add, ALU.add),
                reads=[b_y[p], b_x[s], b_vecs], writes=[b_x[s]])
        st.dma("sync", xTv[:, :, tok0 + i * 512:tok0 + (i + 1) * 512], xt[s][:], d_o[s], reads=[b_x[s]])
    st.finish()


DEPTH = 4
_TABLE_CACHE = {}


def _tables():
    if not _TABLE_CACHE:
        FC, FS, GC, GS = dft_tables()
        rc, rsn = rope_tables()
        zT, trow = hyena_pos_tables()
        _TABLE_CACHE.update(FC=FC, FS=FS, GC=GC, GS=GS, rc=rc, rsn=rsn, zT=zT, trow=trow,
                            ident=np.eye(128, dtype=np.float32))
    return _TABLE_CACHE


def prep_shared(inp):
    f32 = lambda a: np.ascontiguousarray(np.asarray(a, np.float32))
    sh = {}
    vt = VecTable()
    for i in range(DEPTH):
        vt.add_feat(f"f{i}0", inp["ffn_norm_g"][i, 0])
        vt.add_feat(f"mix{i}", inp["mix_norm_g"][i])
        vt.add_feat(f"f{i}1", inp["ffn_norm_g"][i, 1])
    vt.add_feat("fin", inp["final_norm_g"])
    sw = np.arange(64) ^ 1
    na = inp["attn_w_in"].shape[0]
    nh = inp["hy_w_in"].shape[0]
    wq, wk, wv = [], [], []
    for j in range(na):
        qg = np.asarray(inp["attn_q_gain"][j], np.float32)
        kg = np.asarray(inp["attn_k_gain"][j], np.float32)
        vt.add(f"qg{j}", np.tile(qg, 2)[:, None])
        vt.add(f"qgs{j}", np.tile(qg[sw], 2)[:, None])
        vt.add(f"kg{j}", np.tile(kg, 2)[:, None])
        vt.add(f"kgs{j}", np.tile(kg[sw], 2)[:, None])
        a, b, c = prep_attn_weights(np.asarray(inp["attn_w_in"][j], np.float32))
        wq.append(a)
        wk.append(b)
        wv.append(c)
    for j in range(nh):
        vt.add_feat(f"b_in{j}", inp["hy_b_in"][j])
        vt.add_feat(f"conv_w{j}", np.asarray(inp["hy_conv_w"][j]).reshape(-1))
        vt.add_feat(f"conv_b{j}", inp["hy_conv_b"][j])
        vt.add_feat(f"decay{j}", np.asarray(inp["hy_decay"][j]).reshape(-1))
        vt.add_feat(f"b_out{j}", inp["hy_b_out"][j])
        for key, src in (("b1", "hy_f_b1"), ("b2", "hy_f_b2"), ("b3", "hy_f_b3"), ("freq", "hy_f_freq")):
            vt.add(f"{key}_{j}", np.asarray(inp[src][j], np.float32)[:, None])
    sh["vecs"] = vt.build()
    sh["wq"] = np.stack(wq)
    sh["wk"] = np.stack(wk)
    sh["wv"] = np.stack(wv)
    sh["hy_dbc"] = np.ascontiguousarray(np.broadcast_to(np.asarray(inp["hy_d_bias"], np.float32)[:, None, :], (nh, 128, D)))
    sh["attn_w_out"] = np.ascontiguousarray(np.asarray(inp["attn_w_out"], np.float32)[:, q_head_perm(), :])
    for k in ("ffn_w_in", "ffn_w_out", "hy_w_in", "hy_w_out", "hy_f_w1", "hy_f_w2", "hy_f_w3", "hy_f_w_out"):
        sh[k] = f32(inp[k])
    sh.update(_tables())
    return sh, vt


def build_program(sh, vt, nseq, plan=None):
    nc = bass.Bass("TRN2", target_bir_lowering=False)
    ntok = nseq * L
    ap = {}
    for k, a in sh.items():
        dt = BF16 if a.dtype == ml_dtypes.bfloat16 else F32
        ap[k] = nc.dram_tensor(k, list(a.shape), dt, kind="ExternalInput").ap()
    x_d = nc.dram_tensor("x", [ntok, D], F32, kind="ExternalInput").ap()
    out_d = nc.dram_tensor("out", [ntok, D], F32, kind="ExternalOutput").ap()
    xT = nc.dram_tensor("xT", [D, ntok], F32, kind="Internal").ap()
    KT = nc.dram_tensor("KT", [nseq, 2, 128, L], BF16, kind="Internal").ap()
    VA = nc.dram_tensor("VA", [nseq, 128, 32 * 512], BF16, kind="Internal").ap()
    ATM = nc.dram_tensor("ATM", [L, D], BF16, kind="Internal").ap()
    BTM = nc.dram_tensor("BTM", [L, D], BF16, kind="Internal").ap()
    KH = nc.dram_tensor("KH", [4, NFP, D], F32, kind="Internal").ap()
    VTM = nc.dram_tensor("VTM", [L, D], BF16, kind="Internal").ap()
    X0TM = nc.dram_tensor("X0TM", [L, D], BF16, kind="Internal").ap()
    YH = nc.dram_tensor("YH", [4, NFP, D], BF16, kind="Internal").ap()
    ZT = nc.dram_tensor("ZT", [D, L], BF16, kind="Internal").ap()
    WIB = nc.dram_tensor("WIB", [DEPTH, 2, D, 2 * DFF], BF16, kind="Internal").ap()
    WOB = nc.dram_tensor("WOB", [DEPTH, 2, DFF, D], BF16, kind="Internal").ap()
    full_plan = plan is None

    def jobs_for(i, k):
        if not full_plan or i >= DEPTH:
            return None
        return [(ap["ffn_w_in"][i, k], WIB[i, k]), (ap["ffn_w_out"][i, k], WOB[i, k])]
    vecs_d = ap["vecs"]
    nv = sh["vecs"].shape[1]
    col = lambda key: vt.idx[key][0]
    if plan is None:
        plan = ["tin"]
        for i in range(DEPTH):
            plan += [f"ffn{i}0", f"mix{i}", f"ffn{i}1"]
        plan += ["fin"]
    for item in plan:
        if item == "tin":
            stage_transpose_in(nc, x_d, xT, ap["ident"], ntok)
        elif item == "fin":
            stage_final(nc, xT, out_d, vecs_d, nv, col("fin"), ap["ident"], ntok)
        elif item.startswith("ffn"):
            i, k = int(item[3]), int(item[4])
            if full_plan and not (i == 0 and k == 0) and nseq == 2:
                stage_ffn(nc, item, xT, WIB[i, k], WOB[i, k], vecs_d, nv, col(f"f{i}{k}"), ntok, precast=True)
            else:
                stage_ffn(nc, item, xT, ap["ffn_w_in"][i, k], ap["ffn_w_out"][i, k], vecs_d, nv, col(f"f{i}{k}"), ntok)
        elif item.startswith("mix"):
            i = int(item[3])
            j = i // 2
            if i % 2 == 0:
                stage_attn_kv(nc, f"akv{i}", xT, ap["wk"][j], ap["wv"][j], vecs_d, nv, col(f"mix{i}"), col(f"kg{j}"), col(f"kgs{j}"),
                              ap["rc"], ap["rsn"], KT, VA, nseq)
                for s in range(nseq):
                    stage_attn_q(nc, f"aq{i}{s}", xT, s * L, ap["wq"][j], ap["attn_w_out"][j], vecs_d, nv, col(f"mix{i}"),
                                 col(f"qg{j}"), col(f"qgs{j}"), ap["rc"], ap["rsn"], KT[s], VA[s],
                                 cast_jobs=(jobs_for(i, 1) if s == 0 else jobs_for(i + 1, 0)) if nseq == 2 else None)
            else:
                vc = {k: col(f"{k}_{j}") for k in ("b1", "b2", "b3", "freq")}
                vc.update({k: col(f"{k}{j}") for k in ("b_in", "conv_w", "conv_b", "decay")})
                stage_hy_filter(nc, f"hf{i}", ap["zT"], ap["trow"], ap["hy_f_w1"][j], ap["hy_f_w2"][j], ap["hy_f_w3"][j], ap["hy_f_w_out"][j],
                                vecs_d, nv, vc, ap["ident"], ATM, BTM)
                stage_hy_kdft(nc, f"hk{i}", ATM, BTM, ap["FC"], ap["FS"], KH)
                for s in range(nseq):
                    stage_hy_in(nc, f"hi{i}{s}", xT, s * L, ap["hy_w_in"][j], vecs_d, nv, col(f"mix{i}"), vc, ap["ident"], VTM, X0TM)
                    stage_hy_fwd(nc, f"hw{i}{s}", VTM, ap["FC"], ap["FS"], KH, YH,
                                 cast_jobs=(jobs_for(i, 1) if s == 0 else jobs_for(i + 1, 0)) if nseq == 2 else None)
                    stage_hy_inv(nc, f"hv{i}{s}", YH, ap["GC"], ap["GS"], VTM, X0TM, ap["hy_dbc"][j], ap["ident"], ZT)
                    stage_hy_out(nc, f"ho{i}{s}", xT, s * L, ZT, ap["hy_w_out"][j], vecs_d, nv, col(f"b_out{j}"))
    return nc


def kernel(**inputs):
    x = np.asarray(inputs["x"], np.float32)
    B = x.shape[0]
    nseq = B // NCORES
    sh, vt = prep_shared(inputs)
    nc = build_program(sh, vt, nseq)
    in_maps = []
    for c in range(NCORES):
        m = dict(sh)
        m["x"] = np.ascontiguousarray(x[c * nseq:(c + 1) * nseq].reshape(nseq * L, D))
        in_maps.append(m)
    res = run_bass_kernel_spmd(nc, in_maps, core_ids=list(range(NCORES)))
    out = np.stack([np.asarray(r["out"], np.float32).reshape(nseq, L, D) for r in res.results], axis=0)
    return out.reshape(B, L, D)
```

```python
import math
import numpy as np
import ml_dtypes
import concourse.bass as bass
import concourse.mybir as mybir
from concourse.bass_utils import run_bass_kernel_spmd

F32 = mybir.dt.float32
BF16 = mybir.dt.bfloat16
ALU = mybir.AluOpType
AF = mybir.ActivationFunctionType
AX = mybir.AxisListType

ENGS = ("tensor", "vector", "scalar", "gpsimd", "sync")

D = 1024
DFF = 2816
NJ = DFF // 128
L = 4096
NCORES = 8
EPS = 1e-6
TWO_PI = 2.0 * math.pi


class Buf:
    __slots__ = ("w", "r", "strict")

    def __init__(self):
        self.w = None
        self.r = {}
        self.strict = False


class Stage:
    def __init__(self, nc, name):
        self.nc = nc
        self.name = name
        self.q = {e: [] for e in ENGS}
        self.cnt = {e: 0 for e in ENGS}
        self.seen = {e: {} for e in ENGS}
        self.sems = {}
        self.cleanup = nc.cleanup_on_exit()
        self.cleanup.__enter__()
        for e in ENGS:
            self.sems[e] = nc.alloc_semaphore(name=f"{name}_s_{e}")
        self.dsems = []
        self.nps = 0

    def sb(self, name, shape, dtype):
        return self.nc.alloc_sbuf_tensor(f"{self.name}_{name}", list(shape), dtype)

    def ps(self, shape=(128, 512), dtype=F32):
        self.nps += 1
        return self.nc.alloc_psum_tensor(f"{self.name}_ps{self.nps}", list(shape), dtype)

    def dsem(self):
        s = self.nc.alloc_semaphore(name=f"{self.name}_d{len(self.dsems)}")
        d = [s, 0]
        self.dsems.append(d)
        return d

    def _waits(self, eng, reads, writes, force_waw=False):
        need = {}

        def add(kv, raw, waw=False):
            if kv is None:
                return
            k, v = kv
            if isinstance(k, str) and k == eng and not raw and (eng == "tensor" or not waw):
                return
            kk = k if isinstance(k, str) else id(k)
            if kk not in need or need[kk][1] < v:
                need[kk] = (k, v)

        for b in reads:
            add(b.w, True)
        for b in writes:
            add(b.w, False, force_waw or b.strict)
            for kv in b.r.values():
                add(kv, False)
        out = []
        for kk, (k, v) in need.items():
            if self.seen[eng].get(kk, 0) < v:
                self.seen[eng][kk] = v
                out.append((self.sems[k] if isinstance(k, str) else k[0], v))
        return out

    @staticmethod
    def _note_read(b, k, v):
        kk = k if isinstance(k, str) else id(k)
        if kk not in b.r or b.r[kk][1] < v:
            b.r[kk] = (k, v)

    def op(self, eng, fn, reads=(), writes=(), memset=False):
        wl = self._waits(eng, reads, writes, force_waw=memset)
        self.cnt[eng] += 1
        v = self.cnt[eng]
        sem = self.sems[eng]

        def emit(e):
            for s, val in wl:
                e.wait_ge(s, val)
            fn(e).then_inc(sem, 1)

        self.q[eng].append(emit)
        for b in reads:
            self._note_read(b, eng, v)
        for b in writes:
            b.w = (eng, v)
            b.r = {}
            b.strict = memset

    def dma(self, queue, out_ap, in_ap, ds, reads=(), writes=()):
        wl = self._waits(queue, reads, writes)
        ds[1] += 16
        v = ds[1]
        sem = ds[0]

        def emit(e):
            for s, val in wl:
                e.wait_ge(s, val)
            e.dma_start(out=out_ap, in_=in_ap).then_inc(sem, 16)

        self.q[queue].append(emit)
        for b in reads:
            self._note_read(b, ds, v)
        for b in writes:
            b.w = (ds, v)
            b.r = {}

    def finish(self):
        nc = self.nc
        q = self.q
        fin = [(d[0], d[1]) for d in self.dsems if d[1] > 0]

        def emit_fin(e):
            for s_, v_ in fin:
                e.wait_ge(s_, v_)

        q["sync"].append(emit_fin)
        with nc.Block() as block:
            @block.tensor
            def _(e):
                for f in q["tensor"]:
                    f(e)

            @block.vector
            def _(e):
                for f in q["vector"]:
                    f(e)

            @block.scalar
            def _(e):
                for f in q["scalar"]:
                    f(e)

            @block.gpsimd
            def _(e):
                for f in q["gpsimd"]:
                    f(e)

            @block.sync
            def _(e):
                for f in q["sync"]:
                    f(e)
        self.cleanup.__exit__(None, None, None)


class VecTable:
    def __init__(self):
        self.cols = []
        self.idx = {}

    def add(self, key, arr2d):
        a = np.zeros((128, arr2d.shape[1]), np.float32)
        a[:arr2d.shape[0]] = arr2d
        self.idx[key] = (sum(c.shape[1] for c in self.cols), a.shape[1])
        self.cols.append(a)

    def add_feat(self, key, vec):
        v = np.asarray(vec, np.float32)
        self.add(key, np.ascontiguousarray(v.reshape(-1, 128).T))

    def build(self):
        return np.ascontiguousarray(np.concatenate(self.cols, axis=1))


def stage_transpose_in(nc, x_d, xT_d, ident_d, ntok):
    st = Stage(nc, "tin")
    ident = st.sb("ident", [128, 128], F32)
    b_id = Buf()
    d_id = st.dsem()
    st.dma("sync", ident[:], ident_d, d_id, writes=[b_id])
    xin = [st.sb(f"xin{i}", [128, 4, D], F32) for i in range(2)]
    b_xin = [Buf(), Buf()]
    xo = [st.sb(f"xo{i}", [128, 8, 512], F32) for i in range(2)]
    b_xo = [Buf(), Buf()]
    d_in = [st.dsem(), st.dsem()]
    d_out = [st.dsem(), st.dsem()]
    pss = [st.ps() for _ in range(8)]
    b_ps = [Buf() for _ in range(8)]
    xv = x_d.rearrange("(n s p) f -> n p s f", p=128, s=4)
    xTv = xT_d.rearrange("(c p) t -> p c t", p=128)
    NT = ntok // 512
    for i in range(NT):
        sl = i % 2
        st.dma("sync", xin[sl][:], xv[i], d_in[sl], writes=[b_xin[sl]])
        for c in range(8):
            def mm(e, c=c, sl=sl):
                ins = None
                for s in range(4):
                    ins = e.transpose(pss[c][:, s * 128:(s + 1) * 128], xin[sl][:, s, c * 128:(c + 1) * 128], ident[:])
                return ins
            st.op("tensor", mm, reads=[b_xin[sl], b_id], writes=[b_ps[c]])
            if c % 2 == 0:
                st.op("vector", lambda e, c=c, sl=sl: e.tensor_copy(xo[sl][:, c, :], pss[c][:]),
                      reads=[b_ps[c]], writes=[b_xo[sl]])
            else:
                st.op("scalar", lambda e, c=c, sl=sl: e.activation(xo[sl][:, c, :], pss[c][:], AF.Copy),
                      reads=[b_ps[c]], writes=[b_xo[sl]])
        st.dma("sync", xTv[:, :, i * 512:(i + 1) * 512], xo[sl][:], d_out[sl], reads=[b_xo[sl]])
    st.finish()


def emit_norm_stats(st, xt, b_x, sq, b_sq, ones, b_ones, ss_ps, b_ss, rs, b_rs, width=512, lnexp=False):
    for h in range(4):
        s2 = h % 2
        st.op("gpsimd", lambda e, h=h, s2=s2: e.tensor_tensor(sq[s2][:], xt[:, 2 * h:2 * h + 2, :], xt[:, 2 * h:2 * h + 2, :], ALU.mult),
              reads=[b_x], writes=[b_sq[s2]])

        def mm(e, h=h, s2=s2):
            ins = None
            for k in range(2):
                ins = e.matmul(ss_ps[:, 0:width], ones[:], sq[s2][:, k, :], start=(h == 0 and k == 0), stop=(h == 3 and k == 1))
            return ins
        st.op("tensor", mm, reads=[b_sq[s2], b_ones], writes=[b_ss])
    if lnexp:
        st.op("scalar", lambda e: e.activation(rs[:, 0:width], ss_ps[:, 0:width], AF.Ln, bias=EPS, scale=1.0 / D), reads=[b_ss], writes=[b_rs])
        st.op("scalar", lambda e: e.activation(rs[:, 0:width], rs[:, 0:width], AF.Exp, scale=-0.5), reads=[b_rs], writes=[b_rs])
        return
    st.op("scalar", lambda e: e.activation(rs[:, 0:width], ss_ps[:, 0:width], AF.Sqrt, bias=EPS, scale=1.0 / D), reads=[b_ss], writes=[b_rs])
    st.op("vector", lambda e: e.reciprocal(rs[:, 0:width], rs[:, 0:width]), reads=[b_rs], writes=[b_rs])


def stage_ffn(nc, name, xT_d, w_in_d, w_out_d, vecs_d, nv, gcol, ntok):
    st = Stage(nc, name)
    NT = ntok // 512
    w_in = st.sb("w_in", [128, 8, 2 * DFF], BF16)
    w_out = st.sb("w_out", [128, NJ, D], BF16)
    JB = [0, 6, 12, 17, 22]
    b_win = [Buf() for _ in range(4)]
    d_wq = [st.dsem() for _ in range(4)]
    jq = [max(q for q in range(4) if JB[q] <= j) for j in range(NJ)]
    b_wout = [Buf() for _ in range(2)]
    d_w = st.dsem()
    d_w2 = st.dsem()
    vecs = st.sb("vecs", [128, nv], F32)
    b_vecs = Buf()
    d_v = st.dsem()
    st.dma("sync", vecs[:], vecs_d, d_v, writes=[b_vecs])
    ones = st.sb("ones", [128, 128], BF16)
    b_ones = Buf()
    st.op("vector", lambda e: e.memset(ones[:], 1.0), writes=[b_ones], memset=True)
    xt = [st.sb(f"xt{i}", [128, 8, 512], F32) for i in range(2)]
    b_x = [Buf(), Buf()]
    d_x = [st.dsem(), st.dsem()]
    d_o = [st.dsem(), st.dsem()]
    hT = st.sb("hT", [128, 8, 512], BF16)
    b_h = Buf()
    aT = st.sb("aT", [128, NJ, 512], BF16)
    b_a = [Buf() for _ in range(NJ)]
    sq = [st.sb(f"sq{i}", [128, 2, 512], BF16) for i in range(2)]
    b_sq = [Buf(), Buf()]
    rs = st.sb("rs", [128, 512], F32)
    b_rs = Buf()
    sl_t = [st.sb(f"sl{i}", [128, 512], F32) for i in range(2)]
    b_sl = [Buf(), Buf()]
    ss_ps = st.ps()
    b_ss = Buf()
    g_ps = [st.ps(), st.ps()]
    u_ps = [st.ps(), st.ps()]
    b_g = [Buf(), Buf()]
    b_u = [Buf(), Buf()]
    y_ps = [st.ps(), st.ps()]
    b_y = [Buf(), Buf()]
    xTv = xT_d.rearrange("(c p) t -> p c t", p=128)
    w_in_v = w_in_d.rearrange("(c p) n -> p c n", p=128)
    w_out_v = w_out_d.rearrange("(j p) n -> p j n", p=128)

    def load_x(i):
        s = i % 2
        st.dma("sync", xt[s][:], xTv[:, :, i * 512:(i + 1) * 512], d_x[s], writes=[b_x[s]])

    load_x(0)
    if NT > 1:
        load_x(1)
    def pro_a(i):
        s = i % 2
        emit_norm_stats(st, xt[s], b_x[s], sq, b_sq, ones, b_ones, ss_ps, b_ss, rs, b_rs)

    def pro_b(i):
        s = i % 2
        for c in range(8):
            st.op("vector", lambda e, c=c, s=s: e.scalar_tensor_tensor(
                hT[:, c, :], xt[s][:, c, :], vecs[:, gcol + c:gcol + c + 1], rs[:], ALU.mult, ALU.mult),
                reads=[b_x[s], b_rs, b_vecs], writes=[b_h])

    def up(i):
        for j in range(NJ):
            p = j % 2

            def mmg(e, j=j, p=p):
                ins = None
                for c in range(8):
                    ins = e.matmul(g_ps[p][:], w_in[:, c, j * 128:(j + 1) * 128], hT[:, c, :], start=(c == 0), stop=(c == 7))
                return ins

            def mmu(e, j=j, p=p):
                ins = None
                for c in range(8):
                    ins = e.matmul(u_ps[p][:], w_in[:, c, DFF + j * 128:DFF + (j + 1) * 128], hT[:, c, :], start=(c == 0), stop=(c == 7))
                return ins
            st.op("tensor", mmg, reads=[b_h, b_win[jq[j]]], writes=[b_g[p]])
            st.op("tensor", mmu, reads=[b_h, b_win[jq[j]]], writes=[b_u[p]])
            st.op("scalar", lambda e, p=p: e.activation(sl_t[p][:], g_ps[p][:], AF.Silu), reads=[b_g[p]], writes=[b_sl[p]])
            st.op("vector", lambda e, p=p, j=j: e.tensor_tensor(aT[:, j, :], u_ps[p][:], sl_t[p][:], ALU.mult),
                  reads=[b_u[p], b_sl[p]], writes=[b_a[j]])

    def down(i):
        s = i % 2
        for m in range(8):
            p = m % 2

            def mmy(e, m=m, p=p):
                ins = None
                for j in range(NJ):
                    ins = e.matmul(y_ps[p][:], w_out[:, j, m * 128:(m + 1) * 128], aT[:, j, :], start=(j == 0), stop=(j == NJ - 1))
                return ins
            st.op("tensor", mmy, reads=b_a + b_wout, writes=[b_y[p]])
            st.op("vector", lambda e, m=m, p=p, s=s: e.scalar_tensor_tensor(
                xt[s][:, m, :], y_ps[p][:], 0.5, xt[s][:, m, :], ALU.mult, ALU.add),
                reads=[b_y[p], b_x[s]], writes=[b_x[s]])
        st.dma("sync", xTv[:, :, i * 512:(i + 1) * 512], xt[s][:], d_o[s], reads=[b_x[s]])

    pro_a(0)
    for q in range(4):
        for off in (0, DFF):
            ca, cb_ = off + JB[q] * 128, off + JB[q + 1] * 128
            st.dma("gpsimd", w_in[:, :, ca:cb_], w_in_v[:, :, ca:cb_], d_wq[q], writes=[b_win[q]])
    for hh in range(2):
        st.dma("gpsimd", w_out[:, hh * 11:(hh + 1) * 11, :], w_out_v[:, hh * 11:(hh + 1) * 11, :], d_w2, writes=[b_wout[hh]])

    pro_b(0)
    for i in range(NT):
        up(i)
        if i + 1 < NT:
            pro_a(i + 1)
            pro_b(i + 1)
        down(i)
        if i + 2 < NT:
            load_x(i + 2)
    st.finish()


def stage_final(nc, xT_d, out_d, vecs_d, nv, gcol, ident_d, ntok):
    st = Stage(nc, "fin")
    NT = ntok // 512
    vecs = st.sb("vecs", [128, nv], F32)
    b_vecs = Buf()
    d_v = st.dsem()
    st.dma("sync", vecs[:], vecs_d, d_v, writes=[b_vecs])
    ident = st.sb("ident", [128, 128], F32)
    b_id = Buf()
    d_id = st.dsem()
    st.dma("sync", ident[:], ident_d, d_id, writes=[b_id])
    ones = st.sb("ones", [128, 128], BF16)
    b_ones = Buf()
    st.op("vector", lambda e: e.memset(ones[:], 1.0), writes=[b_ones], memset=True)
    xt = [st.sb(f"xt{i}", [128, 8, 512], F32) for i in range(2)]
    b_x = [Buf(), Buf()]
    d_x = [st.dsem(), st.dsem()]
    d_o = [st.dsem(), st.dsem()]
    sq = [st.sb(f"sq{i}", [128, 2, 512], BF16) for i in range(2)]
    b_sq = [Buf(), Buf()]
    rs = st.sb("rs", [128, 512], F32)
    b_rs = Buf()
    ot = [st.sb(f"ot{i}", [128, 4, D], F32) for i in range(2)]
    b_ot = [Buf(), Buf()]
    ss_ps = st.ps()
    b_ss = Buf()
    pss = [st.ps() for _ in range(6)]
    b_ps = [Buf() for _ in range(6)]
    xTv = xT_d.rearrange("(c p) t -> p c t", p=128)
    ov = out_d.rearrange("(n s p) f -> n p s f", p=128, s=4)
    kk = 0
    for i in range(NT):
        s = i % 2
        st.dma("sync", xt[s][:], xTv[:, :, i * 512:(i + 1) * 512], d_x[s], writes=[b_x[s]])
        emit_norm_stats(st, xt[s], b_x[s], sq, b_sq, ones, b_ones, ss_ps, b_ss, rs, b_rs)
        for c in range(8):
            st.op("vector", lambda e, c=c, s=s: e.scalar_tensor_tensor(
                xt[s][:, c, :], xt[s][:, c, :], vecs[:, gcol + c:gcol + c + 1], rs[:], ALU.mult, ALU.mult),
                reads=[b_x[s], b_rs, b_vecs], writes=[b_x[s]])
        for sb_ in range(4):
            for hf in range(2):
                p = kk % 6
                kk += 1

                def mm(e, sb_=sb_, hf=hf, p=p, s=s):
                    ins = None
                    for c4 in range(4):
                        c = hf * 4 + c4
                        ins = e.transpose(pss[p][:, c4 * 128:(c4 + 1) * 128], xt[s][:, c, sb_ * 128:(sb_ + 1) * 128], ident[:])
                    return ins
                st.op("tensor", mm, reads=[b_x[s], b_id], writes=[b_ps[p]])
                if kk % 2 == 0:
                    st.op("vector", lambda e, sb_=sb_, hf=hf, p=p, s=s: e.tensor_copy(ot[s][:, sb_, hf * 512:(hf + 1) * 512], pss[p][:]),
                          reads=[b_ps[p]], writes=[b_ot[s]])
                else:
                    st.op("scalar", lambda e, sb_=sb_, hf=hf, p=p, s=s: e.activation(ot[s][:, sb_, hf * 512:(hf + 1) * 512], pss[p][:], AF.Copy),
                          reads=[b_ps[p]], writes=[b_ot[s]])
        st.dma("sync", ov[i], ot[s][:], d_o[s], reads=[b_ot[s]])
    st.finish()


NQX = 2048
NKX = 512
NVX = 256
HEAD_A = [0, 1, 2, 3, 8, 9, 10, 11]
HEAD_B = [4, 5, 6, 7, 12, 13, 14, 15]


def emit_headnorm_rope(st, src_ps, b_src, bones, b_bones, ssq_ps, b_ssq, sqh, b_sqh, rsh, b_rsh, t1, b_t1, t2, b_t2,
                       vecs, b_vecs, gc, gsc, rc, rsn, b_rope, outs):
    st.op("scalar", lambda e: e.activation(sqh[:], src_ps[:, 0:512], AF.Square), reads=[b_src], writes=[b_sqh])
    st.op("tensor", lambda e: e.matmul(ssq_ps[:], bones[:], sqh[:], start=True, stop=True), reads=[b_sqh, b_bones], writes=[b_ssq])
    st.op("scalar", lambda e: e.activation(rsh[:], ssq_ps[:], AF.Ln, bias=EPS, scale=1.0 / 64), reads=[b_ssq], writes=[b_rsh])
    st.op("scalar", lambda e: e.activation(rsh[:], rsh[:], AF.Exp, scale=-0.5), reads=[b_rsh], writes=[b_rsh])
    st.op("vector", lambda e: e.scalar_tensor_tensor(t1[:], src_ps[:, 0:512], vecs[:, gc:gc + 1], rsh[:], ALU.mult, ALU.mult),
          reads=[b_src, b_rsh, b_vecs], writes=[b_t1])
    st.op("vector", lambda e: e.scalar_tensor_tensor(t2[:], src_ps[:, 512:1024], vecs[:, gsc:gsc + 1], rsh[:], ALU.mult, ALU.mult),
          reads=[b_src, b_rsh, b_vecs], writes=[b_t2])
    st.op("gpsimd", lambda e: e.tensor_tensor(t1[:], t1[:], rc, ALU.mult), reads=[b_t1, b_rope], writes=[b_t1])
    st.op("vector", lambda e: e.tensor_tensor(t2[:], t2[:], rsn, ALU.mult), reads=[b_t2, b_rope], writes=[b_t2])
    for out_ap, rows, b_out in outs:
        st.op("gpsimd", lambda e, out_ap=out_ap, rows=rows: e.tensor_tensor(out_ap, t1[rows, :], t2[rows, :], ALU.add),
              reads=[b_t1, b_t2], writes=[b_out])


def make_bones(st):
    bones = st.sb("bones", [128, 128], BF16)
    b = Buf()
    st.op("vector", lambda e: e.memset(bones[:], 0.0), writes=[b], memset=True)
    st.op("vector", lambda e: e.memset(bones[0:64, 0:64], 1.0), writes=[b], memset=True)
    st.op("vector", lambda e: e.memset(bones[64:128, 64:128], 1.0), writes=[b], memset=True)
    return bones, b


def stage_attn_kv(nc, name, xT_d, wk_d, wv_d, vecs_d, nv, gcol, kgc, kgsc, ropec_d, ropes_d, KT_d, VA_d, nseq):
    st = Stage(nc, name)
    vecs = st.sb("vecs", [128, nv], F32)
    b_vecs = Buf()
    d_v = st.dsem()
    st.dma("sync", vecs[:], vecs_d, d_v, writes=[b_vecs])
    wk = st.sb("wk", [128, 8, NKX], BF16)
    wv = st.sb("wv", [128, 8, NVX], BF16)
    b_wk, b_wv = Buf(), Buf()
    d_w = st.dsem()
    st.dma("gpsimd", wk[:], wk_d.rearrange("(c p) n -> p c n", p=128), d_w, writes=[b_wk])
    d_w2 = st.dsem()
    st.dma("gpsimd", wv[:], wv_d.rearrange("(c p) n -> p c n", p=128), d_w2, writes=[b_wv])
    ones = st.sb("ones", [128, 128], BF16)
    b_ones = Buf()
    st.op("vector", lambda e: e.memset(ones[:], 1.0), writes=[b_ones], memset=True)
    bones, b_bones = make_bones(st)
    xt = [st.sb(f"xt{i}", [128, 8, 512], F32) for i in range(2)]
    b_x = [Buf(), Buf()]
    d_x = [st.dsem(), st.dsem()]
    rope = [st.sb(f"rope{i}", [128, 2, 512], F32) for i in range(2)]
    b_rope = [Buf(), Buf()]
    d_r = [st.dsem(), st.dsem()]
    hTs = [st.sb(f"hT{i}", [128, 8, 512], BF16) for i in range(2)]
    b_hs = [Buf(), Buf()]
    sq = [st.sb(f"sq{i}", [128, 2, 512], BF16) for i in range(2)]
    b_sq = [Buf(), Buf()]
    rs = st.sb("rs", [128, 512], F32)
    b_rs = Buf()
    sqh = st.sb("sqh", [128, 512], BF16)
    b_sqh = Buf()
    rsh = st.sb("rsh", [128, 512], F32)
    b_rsh = Buf()
    t1 = st.sb("t1", [128, 512], F32)
    t2 = st.sb("t2", [128, 512], F32)
    b_t1, b_t2 = Buf(), Buf()
    kst = [st.sb(f"kst{i}", [128, 2, 512], BF16) for i in range(2)]
    b_kst = [Buf(), Buf()]
    d_k = [st.dsem(), st.dsem()]
    vst = [st.sb(f"vst{i}", [128, 4, 4, 128], BF16) for i in range(2)]
    b_vst = [Buf(), Buf()]
    d_vs = [st.dsem(), st.dsem()]
    for i in range(2):
        st.op("vector", lambda e, i=i: e.memset(vst[i][:], 1.0), writes=[b_vst[i]], memset=True)
    ss_ps = st.ps()
    b_ss = Buf()
    ssq_ps = st.ps()
    b_ssq = Buf()
    kps = [st.ps([128, 1024]) for _ in range(2)]
    b_kps = [Buf(), Buf()]
    v_ps = [st.ps(), st.ps()]
    b_vps = [Buf(), Buf()]
    xTv = xT_d.rearrange("(c p) t -> p c t", p=128)
    NT = nseq * 8
    KTv = KT_d.rearrange("s g p t -> s p g t")

    def load(i):
        s = i % 2
        st.dma("sync", xt[s][:], xTv[:, :, i * 512:(i + 1) * 512], d_x[s], writes=[b_x[s]])
        tl = (i % 8) * 512
        st.dma("sync", rope[s][:, 0, :], ropec_d[:, tl:tl + 512], d_r[s], writes=[b_rope[s]])
        st.dma("sync", rope[s][:, 1, :], ropes_d[:, tl:tl + 512], d_r[s], writes=[b_rope[s]])

    def pro(i):
        s = i % 2
        emit_norm_stats(st, xt[s], b_x[s], sq, b_sq, ones, b_ones, ss_ps, b_ss, rs, b_rs, lnexp=True)
        for c in range(8):
            st.op("vector", lambda e, c=c, s=s: e.scalar_tensor_tensor(
                hTs[s][:, c, :], xt[s][:, c, :], vecs[:, gcol + c:gcol + c + 1], rs[:], ALU.mult, ALU.mult),
                reads=[b_x[s], b_rs, b_vecs], writes=[b_hs[s]])

    def kv(i):
        s = i % 2
        hT, b_h = hTs[s], b_hs[s]
        for kc in range(2):
            p = kc % 2

            def mm(e, kc=kc, p=p):
                ins = None
                for sw in range(2):
                    col = sw * 256 + kc * 128
                    for c in range(8):
                        ins = e.matmul(kps[p][:, sw * 512:(sw + 1) * 512], wk[:, c, col:col + 128], hT[:, c, :], start=(c == 0), stop=(c == 7))
                return ins
            st.op("tensor", mm, reads=[b_h, b_wk], writes=[b_kps[p]])
            emit_headnorm_rope(st, kps[p], b_kps[p], bones, b_bones, ssq_ps, b_ssq, sqh, b_sqh, rsh, b_rsh, t1, b_t1, t2, b_t2,
                               vecs, b_vecs, kgc, kgsc, rope[s][:, 0, :], rope[s][:, 1, :], b_rope[s],
                               [(kst[s][:, kc, :], slice(0, 128), b_kst[s])])
        seq, tl = i // 8, (i % 8) * 512
        st.dma("sync", KTv[seq][:, :, tl:tl + 512], kst[s][:], d_k[s], reads=[b_kst[s]])
        for sb_ in range(4):
            p = sb_ % 2

            def mmv(e, sb_=sb_, p=p):
                ins = None
                for c in range(8):
                    ins = e.matmul(v_ps[p][:, 0:256], hT[:, c, sb_ * 128:(sb_ + 1) * 128], wv[:, c, :], start=(c == 0), stop=(c == 7))
                return ins
            st.op("tensor", mmv, reads=[b_h, b_wv], writes=[b_vps[p]])
            st.op("vector", lambda e, sb_=sb_, p=p, s=s: e.tensor_copy(
                vst[s][:, sb_, :, 0:64], v_ps[p][:, 0:256].rearrange("p (g d) -> p g d", d=64)),
                reads=[b_vps[p]], writes=[b_vst[s]])
        st.dma("sync", VA_d[seq][:, (i % 8) * 2048:(i % 8 + 1) * 2048], vst[s][:].rearrange("p a g d -> p (a g d)"), d_vs[s], reads=[b_vst[s]])

    load(0)
    if NT > 1:
        load(1)
    pro(0)
    for i in range(NT):
        if i + 1 < NT:
            pro(i + 1)
        kv(i)
        if i + 2 < NT:
            load(i + 2)
    st.finish()


def stage_attn_q(nc, name, xT_d, tok0, wq_d, wo_d, vecs_d, nv, gcol, qgc, qgsc, ropec_d, ropes_d, KT_d, VA_d):
    st = Stage(nc, name)
    vecs = st.sb("vecs", [128, nv], F32)
    b_vecs = Buf()
    d_v = st.dsem()
    st.dma("sync", vecs[:], vecs_d, d_v, writes=[b_vecs])
    wq = st.sb("wq", [128, 8, NQX], BF16)
    wo = st.sb("wo", [128, 8, D], BF16)
    b_wq, b_wo = Buf(), Buf()
    d_w = st.dsem()
    st.dma("gpsimd", wq[:], wq_d.rearrange("(c p) n -> p c n", p=128), d_w, writes=[b_wq])
    d_w2 = st.dsem()
    st.dma("gpsimd", wo[:], wo_d.rearrange("(c p) n -> p c n", p=128), d_w2, writes=[b_wo])
    kT2 = st.sb("kT2", [128, 2, L], BF16)
    b_k = Buf()
    va = st.sb("va", [128, 32, 4, 128], BF16)
    b_va = Buf()
    d_kv = st.dsem()
    st.dma("sync", kT2[:], KT_d.rearrange("g p t -> p g t"), d_kv, writes=[b_k])
    d_kv2 = st.dsem()
    st.dma("sync", va[:].rearrange("p a g d -> p (a g d)"), VA_d, d_kv2, writes=[b_va])
    ones = st.sb("ones", [128, 128], BF16)
    b_ones = Buf()
    st.op("vector", lambda e: e.memset(ones[:], 1.0), writes=[b_ones], memset=True)
    bones, b_bones = make_bones(st)
    xts = [st.sb(f"xt{i}", [128, 8, 512], F32) for i in range(2)]
    b_xs = [Buf(), Buf()]
    d_xs = [st.dsem(), st.dsem()]
    d_os = [st.dsem(), st.dsem()]
    ropes = [st.sb(f"rope{i}", [128, 2, 512], F32) for i in range(2)]
    b_ropes = [Buf(), Buf()]
    d_rs = [st.dsem(), st.dsem()]
    hT = st.sb("hT", [128, 8, 512], BF16)
    b_h = Buf()
    qT = st.sb("qT", [128, 16, 512], BF16)
    b_q = [Buf() for _ in range(8)]
    st.op("gpsimd", lambda e: e.memset(qT[:], 0.0), writes=b_q, memset=True)
    oT = st.sb("oT", [128, 8, 512], BF16)
    b_o = [Buf() for _ in range(8)]
    sq = [st.sb(f"sq{i}", [128, 2, 512], BF16) for i in range(2)]
    b_sq = [Buf(), Buf()]
    rs = st.sb("rs", [128, 512], F32)
    b_rs = Buf()
    sqh = st.sb("sqh", [128, 512], BF16)
    b_sqh = Buf()
    rsh = st.sb("rsh", [128, 512], F32)
    b_rsh = Buf()
    t1 = st.sb("t1", [128, 512], F32)
    t2 = st.sb("t2", [128, 512], F32)
    b_t1, b_t2 = Buf(), Buf()
    NPT = 4
    pT = [st.sb(f"pT{i}", [128, 1024], BF16) for i in range(NPT)]
    b_p = [Buf() for _ in range(NPT)]
    rcp = [st.sb(f"rcp{i}", [128, 512], F32) for i in range(2)]
    b_rcp = [Buf(), Buf()]
    s_ps = [st.ps([128, 1024]) for _ in range(3)]
    b_s = [Buf(), Buf(), Buf()]
    o_ps = [st.ps(), st.ps()]
    b_ops = [Buf(), Buf()]
    m_ps = [s_ps[2][:, 0:512], s_ps[2][:, 512:1024]]
    b_m = [b_s[2], b_s[2]]
    xTv = xT_d.rearrange("(c p) t -> p c t", p=128)

    def load(i):
        sl_ = i % 2
        ta = tok0 + i * 512
        st.dma("sync", xts[sl_][:], xTv[:, :, ta:ta + 512], d_xs[sl_], writes=[b_xs[sl_]])
        st.dma("sync", ropes[sl_][:, 0, :], ropec_d[:, i * 512:(i + 1) * 512], d_rs[sl_], writes=[b_ropes[sl_]])
        st.dma("sync", ropes[sl_][:, 1, :], ropes_d[:, i * 512:(i + 1) * 512], d_rs[sl_], writes=[b_ropes[sl_]])

    def pro_stats(i):
        xt, b_x = xts[i % 2], b_xs[i % 2]
        emit_norm_stats(st, xt, b_x, sq, b_sq, ones, b_ones, m_ps[1], b_m[1], rs, b_rs, lnexp=True)
        for c in range(8):
            st.op("vector", lambda e, c=c, xt=xt: e.scalar_tensor_tensor(
                hT[:, c, :], xt[:, c, :], vecs[:, gcol + c:gcol + c + 1], rs[:], ALU.mult, ALU.mult),
                reads=[b_x, b_rs, b_vecs], writes=[b_h])

    def pro_q(i, c):
        rope, b_rope = ropes[i % 2], b_ropes[i % 2]
        p = c % 2

        def mm(e):
            ins = None
            for sw in range(2):
                col = sw * 1024 + c * 128
                for cc in range(8):
                    ins = e.matmul(s_ps[p][:, sw * 512:(sw + 1) * 512], wq[:, cc, col:col + 128], hT[:, cc, :], start=(cc == 0), stop=(cc == 7))
            return ins
        st.op("tensor", mm, reads=[b_h, b_wq], writes=[b_s[p]])
        emit_headnorm_rope(st, s_ps[p], b_s[p], bones, b_bones, m_ps[0], b_m[0], sqh, b_sqh, rsh, b_rsh, t1, b_t1, t2, b_t2,
                           vecs, b_vecs, qgc, qgsc, rope[:, 0, :], rope[:, 1, :], b_rope,
                           [(qT[0:64, 2 * c, :], slice(0, 64), b_q[c]), (qT[64:128, 2 * c + 1, :], slice(64, 128), b_q[c])])

    def outproj(i, m):
        xt, b_x = xts[i % 2], b_xs[i % 2]
        p = m % 2

        def mmy(e):
            ins = None
            for c in range(8):
                ins = e.matmul(o_ps[p][:], wo[:, c, m * 128:(m + 1) * 128], oT[:, c, :], start=(c == 0), stop=(c == 7))
            return ins
        st.op("tensor", mmy, reads=b_o + [b_wo], writes=[b_ops[p]])
        st.op("vector", lambda e: e.tensor_tensor(xt[:, m, :], o_ps[p][:], xt[:, m, :], ALU.add),
              reads=[b_ops[p], b_x], writes=[b_x])

    NP = 16
    seqn = [(h, kp) for h in range(16) for kp in range(NP)]
    NN = len(seqn)

    def S(n):
        hh, kp = seqn[n]
        c, half = hh // 2, hh % 2
        g = (HEAD_A[c] if half == 0 else HEAD_B[c]) // 4
        sl = n % 3

        def mm(e):
            ins = None
            for k2 in range(2):
                kc = 2 * kp + k2
                ins = e.matmul(s_ps[sl][:, k2 * 512:(k2 + 1) * 512], kT2[:, g // 2, kc * 128:(kc + 1) * 128], qT[:, hh, :], start=True, stop=True)
            return ins
        st.op("tensor", mm, reads=[b_k, b_q[c]], writes=[b_s[sl]])
        st.op("scalar", lambda e: e.activation(pT[n % NPT][:], s_ps[sl][:], AF.Exp, scale=0.125), reads=[b_s[sl]], writes=[b_p[n % NPT]])

    def PV(n):
        h, kp = seqn[n]
        g = (HEAD_A[h // 2] if h % 2 == 0 else HEAD_B[h // 2]) // 4
        os_ = h % 2

        def mm(e):
            ins = None
            for k2 in range(2):
                kc = 2 * kp + k2
                ins = e.matmul(o_ps[os_][:], va[:, kc, g, :], pT[n % NPT][:, k2 * 512:(k2 + 1) * 512],
                               start=(kp == 0 and k2 == 0), stop=(kp == NP - 1 and k2 == 1))
            return ins
        st.op("tensor", mm, reads=[b_va, b_p[n % NPT]], writes=[b_ops[os_]])
        if kp == NP - 1:
            c, half = h // 2, h % 2
            rows = slice(64 * half, 64 * half + 64)
            st.op("vector", lambda e: e.reciprocal(rcp[os_][64:128, :], o_ps[os_][64:128, :]), reads=[b_ops[os_]], writes=[b_rcp[os_]])
            st.op("vector", lambda e: e.tensor_tensor(oT[rows, c, :], o_ps[os_][0:64, :], rcp[os_][64:128, :], ALU.mult),
                  reads=[b_ops[os_], b_rcp[os_]], writes=[b_o[c]])

    load(0)
    pro_stats(0)
    for c in range(8):
        pro_q(0, c)
    for i in range(8):
        t0 = tok0 + i * 512
        if i + 1 < 8:
            load(i + 1)
        S(0)
        S(1)
        S(2)
        for n in range(NN):
            PV(n)
            if n + 3 < NN:
                S(n + 3)
            if n + 3 == NN - 1 and i + 1 < 8:
                pro_stats(i + 1)
        for k in range(8):
            if i + 1 < 8:
                pro_q(i + 1, k)
            outproj(i, k)
        st.dma("sync", xTv[:, :, t0:t0 + 512], xts[i % 2][:], d_os[i % 2], reads=[b_xs[i % 2]])
    st.finish()


def _pair_swap_perm(n):
    p = np.arange(n)
    return p ^ 1


def q_head_perm():
    cols = []
    for c in range(8):
        for h in (HEAD_A[c], HEAD_B[c]):
            cols.append(np.arange(h * 64, (h + 1) * 64))
    return np.concatenate(cols)


def prep_attn_weights(w_in):
    q = w_in[:, :1024][:, q_head_perm()]
    k = w_in[:, 1024:1280]
    v = w_in[:, 1280:1536]
    wq = np.concatenate([q, q[:, _pair_swap_perm(1024)]], axis=1)
    wk = np.concatenate([k, k[:, _pair_swap_perm(256)]], axis=1)
    return np.ascontiguousarray(wq), np.ascontiguousarray(wk), np.ascontiguousarray(v)


def rope_tables():
    t = np.arange(L)
    row = (t // 64).astype(np.float32)
    col = (t % 64).astype(np.float32)
    inv = (10000.0 ** (-np.arange(0, 32, 2, dtype=np.float32) / 32)).astype(np.float32)
    ang = np.concatenate([row[:, None] * inv, col[:, None] * inv], axis=-1).astype(np.float32)
    c, s = np.cos(ang), np.sin(ang)
    p = np.arange(128)
    pi = (p % 64) // 2
    sign = np.where(p % 2 == 0, -1.0, 1.0).astype(np.float32)
    rc = np.ascontiguousarray(c[:, pi].T.astype(np.float32))
    rsn = np.ascontiguousarray((s[:, pi] * sign[None, :]).T.astype(np.float32))
    return rc, rsn


NFC = 17
NFP = NFC * 128
NTJ = 32


def _chunk_t():
    j = np.arange(NTJ)
    par, jj = j // 16, j % 16
    p = np.arange(128)
    return (2 * (128 * jj[:, None] + p[None, :]) + par[:, None])


def dft_tables():
    N = 2 * L
    k = np.arange(N)
    ct = np.cos(2 * np.pi * k / N)
    sn = np.sin(2 * np.pi * k / N)
    f = np.arange(NFP)
    tt = _chunk_t().reshape(-1)
    ft = (f[:, None] * tt[None, :]) % N
    C = ct[ft].astype(np.float32)
    S = sn[ft].astype(np.float32)
    wf = np.full(NFP, 2.0, np.float32)
    wf[0] = 1.0
    wf[2049:] = 0.0
    bf = ml_dtypes.bfloat16
    FC = np.ascontiguousarray(C.reshape(NFC, 128, NTJ, 128).transpose(0, 3, 2, 1)).astype(bf)
    FS = np.ascontiguousarray(S.reshape(NFC, 128, NTJ, 128).transpose(0, 3, 2, 1)).astype(bf)
    Gc = (C * (wf / N)[:, None]).reshape(NFC, 128, NTJ, 128)
    Gs = (-S * (wf / N)[:, None]).reshape(NFC, 128, NTJ, 128)
    GC = np.ascontiguousarray(Gc.transpose(2, 1, 0, 3)).astype(bf)
    GS = np.ascontiguousarray(Gs.transpose(2, 1, 0, 3)).astype(bf)
    return FC, FS, GC, GS


def hyena_pos_tables():
    t = np.linspace(0.0, 1.0, L, dtype=np.float32)[:, None]
    bands = 16
    fr = np.linspace(1e-4, bands - 1, bands, dtype=np.float32)
    w = (2.0 * math.pi * np.arange(L, dtype=np.float32)[:, None] / L).astype(np.float32)
    z = np.concatenate([t, np.cos(fr * w), -np.sin(fr * w)], axis=-1).astype(np.float32)
    zT = np.ascontiguousarray(z.T)
    trow = np.ascontiguousarray(np.broadcast_to(t[:, 0][None, :], (128, L))).astype(np.float32)
    return zT, trow


def emit_tm_store(st, src16, b_src, identb, b_idb, tp_ps, b_tp, stg, b_stg, d_stg, dst_d, cc, kctr):
    sl = kctr[0] % 2
    kctr[0] += 1
    for q4 in range(4):
        p = q4 % 2

        def mm(e, q4=q4, p=p):
            ins = None
            for k in range(8):
                j = q4 * 8 + k
                par, jj = j // 16, j % 16
                ins = e.transpose(tp_ps[p][:, k * 128:(k + 1) * 128], src16[:, 256 * jj + par:256 * (jj + 1):2], identb[:])
            return ins
        st.op("tensor", mm, reads=[b_src, b_idb], writes=[b_tp[p]])
        if q4 % 2 == 0:
            st.op("vector", lambda e, q4=q4, p=p: e.tensor_copy(stg[sl][:, q4 * 8:(q4 + 1) * 8, :], tp_ps[p][:].rearrange("p (k c) -> p k c", c=128)),
                  reads=[b_tp[p]], writes=[b_stg[sl]])
        else:
            st.op("scalar", lambda e, q4=q4, p=p: e.activation(stg[sl][:, q4 * 8:(q4 + 1) * 8, :], tp_ps[p][:].rearrange("p (k c) -> p k c", c=128), AF.Copy),
                  reads=[b_tp[p]], writes=[b_stg[sl]])
    st.dma("sync", dst_d.rearrange("(tc p) c -> p tc c", p=128)[:, :, cc * 128:(cc + 1) * 128], stg[sl][:], d_stg[sl], reads=[b_stg[sl]])


def stage_hy_filter(nc, name, zT_d, trow_d, w1_d, w2_d, w3_d, wo_d, vecs_d, nv, vc, identb_d, ATM_d, BTM_d):
    st = Stage(nc, name)
    vecs = st.sb("vecs", [128, nv], F32)
    b_vecs = Buf()
    st.dma("sync", vecs[:], vecs_d, st.dsem(), writes=[b_vecs])
    identb = st.sb("identb", [128, 128], BF16)
    b_idb = Buf()
    st.dma("gpsimd", identb[:], identb_d, st.dsem(), writes=[b_idb])
    zT = st.sb("zT", [33, L], F32)
    b_z = Buf()
    st.dma("sync", zT[:], zT_d, st.dsem(), writes=[b_z])
    trow = st.sb("trow", [128, L], F32)
    b_tr = Buf()
    st.dma("sync", trow[:], trow_d, st.dsem(), writes=[b_tr])
    w1 = st.sb("w1", [33, 64], F32)
    w2 = st.sb("w2", [64, 64], F32)
    w3 = st.sb("w3", [64, 64], F32)
    wo = st.sb("wo", [64, 2048], F32)
    b_w = Buf()
    d_w = st.dsem()
    st.dma("sync", w1[:], w1_d, d_w, writes=[b_w])
    st.dma("sync", w2[:], w2_d, d_w, writes=[b_w])
    st.dma("sync", w3[:], w3_d, d_w, writes=[b_w])
    st.dma("sync", wo[:], wo_d, d_w, writes=[b_w])
    hid = [st.sb(f"hid{i}", [64, L], F32) for i in range(2)]
    b_hid = [Buf(), Buf()]
    tmp = [st.sb(f"tmp{i}", [128, 512], F32) for i in range(2)]
    b_tmp = [Buf(), Buf()]
    tq_ = [st.sb(f"tq{i}", [128, 512], F32) for i in range(2)]
    b_tq = [Buf(), Buf()]
    sm = st.sb("sm", [128, 32], F32)
    b_sm = Buf()
    ps = [st.ps(), st.ps()]
    b_ps = [Buf(), Buf()]
    tp_ps = [st.ps([128, 1024], BF16), st.ps([128, 1024], BF16)]
    b_tp = [Buf(), Buf()]
    fq = vc["freq"]
    for k, key in enumerate(("b1", "b2", "b3")):
        st.op("vector", lambda e, k=k, key=key: e.tensor_tensor(sm[0:64, k:k + 1], vecs[0:64, vc[key]:vc[key] + 1], vecs[0:64, fq:fq + 1], ALU.mult),
              reads=[b_vecs], writes=[b_sm])
    dc = vc["decay"]
    st.op("scalar", lambda e: e.activation(sm[:, 8:24], vecs[:, dc:dc + 16], AF.Abs), reads=[b_vecs], writes=[b_sm])
    st.op("vector", lambda e: e.tensor_scalar(sm[:, 8:24], sm[:, 8:24], -1.0, None, ALU.mult), reads=[b_sm], writes=[b_sm])
    kk = 0
    srcs = [(zT, b_z, 33), None, None]
    ws = [w1, w2, w3]
    for k in range(3):
        if k == 0:
            src, b_src, K = zT, b_z, 33
        else:
            src, b_src, K = hid[(k - 1) % 2], b_hid[(k - 1) % 2], 64
        dst, b_dst = hid[k % 2], b_hid[k % 2]
        for tl in range(8):
            p = kk % 2
            kk += 1
            st.op("tensor", lambda e, p=p, k=k, K=K, src=src, tl=tl: e.matmul(ps[p][0:64, :], ws[k][0:K, :], src[0:K, tl * 512:(tl + 1) * 512], start=True, stop=True),
                  reads=[b_src, b_w], writes=[b_ps[p]])
            st.op("vector", lambda e, p=p, k=k: e.tensor_scalar(tmp[p][0:64, :], ps[p][0:64, :], vecs[0:64, fq:fq + 1], sm[0:64, k:k + 1], ALU.mult, ALU.add),
                  reads=[b_ps[p], b_vecs, b_sm], writes=[b_tmp[p]])
            st.op("scalar", lambda e, p=p: e.activation(tmp[p][0:64, :], tmp[p][0:64, :], AF.Sin, scale=1.0 / 9.0),
                  reads=[b_tmp[p]], writes=[b_tmp[p]])
            for rep in range(2):
                st.op("vector", lambda e, p=p: e.tensor_tensor(tq_[p][0:64, :], tmp[p][0:64, :], tmp[p][0:64, :], ALU.mult),
                      reads=[b_tmp[p]], writes=[b_tq[p]])
                st.op("vector", lambda e, p=p: e.tensor_scalar(tq_[p][0:64, :], tq_[p][0:64, :], -4.0, 3.0, ALU.mult, ALU.add),
                      reads=[b_tq[p]], writes=[b_tq[p]])
                if rep == 0:
                    st.op("vector", lambda e, p=p: e.tensor_tensor(tmp[p][0:64, :], tmp[p][0:64, :], tq_[p][0:64, :], ALU.mult),
                          reads=[b_tmp[p], b_tq[p]], writes=[b_tmp[p]])
                else:
                    st.op("vector", lambda e, p=p, dst=dst, tl=tl: e.tensor_tensor(dst[:, tl * 512:(tl + 1) * 512], tmp[p][0:64, :], tq_[p][0:64, :], ALU.mult),
                          reads=[b_tmp[p], b_tq[p]], writes=[b_dst])
    hid3, b_h3 = hid[2 % 2], b_hid[2 % 2]
    hf = st.sb("hf", [128, L], F32)
    hb = st.sb("hb", [128, L], F32)
    b_hf, b_hb = Buf(), Buf()
    sc = st.sb("sc", [128, L], F32)
    b_sc = Buf()
    a16 = st.sb("a16", [128, L], BF16)
    b16 = st.sb("b16", [128, L], BF16)
    b_a16, b_b16 = Buf(), Buf()
    stg = [st.sb(f"stg{i}", [128, 32, 128], BF16) for i in range(2)]
    b_stg = [Buf(), Buf()]
    d_stg = [st.dsem(), st.dsem()]
    kctr = [0]
    for cc in range(8):
        for dr, (hh, b_hh) in enumerate(((hf, b_hf), (hb, b_hb))):
            chunk = dr * 8 + cc
            for tl in range(8):
                p = kk % 2
                kk += 1
                st.op("tensor", lambda e, p=p, chunk=chunk, tl=tl: e.matmul(ps[p][:], wo[:, chunk * 128:(chunk + 1) * 128], hid3[:, tl * 512:(tl + 1) * 512], start=True, stop=True),
                      reads=[b_h3, b_w], writes=[b_ps[p]])
                st.op("scalar", lambda e, p=p, chunk=chunk, tl=tl: e.activation(tmp[p][:], trow[:, tl * 512:(tl + 1) * 512], AF.Exp, scale=sm[:, 8 + chunk:9 + chunk]),
                      reads=[b_tr, b_sm], writes=[b_tmp[p]])
                st.op("vector", lambda e, p=p, hh=hh, tl=tl: e.tensor_tensor(hh[:, tl * 512:(tl + 1) * 512], ps[p][:], tmp[p][:], ALU.mult),
                      reads=[b_ps[p], b_tmp[p]], writes=[b_hh])
        st.op("vector", lambda e: e.memset(hb[:, 0:1], 0.0), writes=[b_hb], memset=True)
        for hh_ in range(2):
            st.op("scalar", lambda e, hh_=hh_: e.activation(sc[:, hh_ * 2048:(hh_ + 1) * 2048], hf[:, hh_ * 2048:(hh_ + 1) * 2048], AF.Abs), reads=[b_hf], writes=[b_sc])
        st.op("vector", lambda e: e.reduce_sum(sm[:, 24:25], sc[:], AX.X), reads=[b_sc], writes=[b_sm])
        for hh_ in range(2):
            st.op("scalar", lambda e, hh_=hh_: e.activation(sc[:, hh_ * 2048:(hh_ + 1) * 2048], hb[:, hh_ * 2048:(hh_ + 1) * 2048], AF.Abs), reads=[b_hb], writes=[b_sc])
        st.op("vector", lambda e: e.reduce_sum(sm[:, 25:26], sc[:], AX.X), reads=[b_sc], writes=[b_sm])
        st.op("vector", lambda e: e.tensor_tensor(sm[:, 26:27], sm[:, 24:25], sm[:, 25:26], ALU.add), reads=[b_sm], writes=[b_sm])
        st.op("vector", lambda e: e.reciprocal(sm[:, 27:28], sm[:, 26:27]), reads=[b_sm], writes=[b_sm])
        st.op("vector", lambda e: e.tensor_tensor(sc[:], hf[:], hb[:], ALU.add), reads=[b_hf, b_hb], writes=[b_sc])
        st.op("vector", lambda e: e.tensor_scalar(a16[:], sc[:], sm[:, 27:28], None, ALU.mult), reads=[b_sc, b_sm], writes=[b_a16])
        st.op("vector", lambda e: e.tensor_tensor(sc[:], hb[:], hf[:], ALU.subtract), reads=[b_hf, b_hb, b_a16], writes=[b_sc])
        st.op("vector", lambda e: e.tensor_scalar(b16[:], sc[:], sm[:, 27:28], None, ALU.mult), reads=[b_sc, b_sm], writes=[b_b16])
        emit_tm_store(st, a16, b_a16, identb, b_idb, tp_ps, b_tp, stg, b_stg, d_stg, ATM_d, cc, kctr)
        emit_tm_store(st, b16, b_b16, identb, b_idb, tp_ps, b_tp, stg, b_stg, d_stg, BTM_d, cc, kctr)
    st.finish()


def stage_hy_kdft(nc, name, ATM_d, BTM_d, FC_d, FS_d, KH_d):
    st = Stage(nc, name)
    a_tm = st.sb("a_tm", [128, NTJ, D], BF16)
    b_tm = st.sb("b_tm", [128, NTJ, D], BF16)
    b_a, b_b = [Buf(), Buf()], [Buf(), Buf()]
    ATv = ATM_d.rearrange("(tc p) c -> p tc c", p=128)
    BTv = BTM_d.rearrange("(tc p) c -> p tc c", p=128)
    for h_ in range(2):
        st.dma("sync", a_tm[:, 16 * h_:16 * h_ + 16], ATv[:, 16 * h_:16 * h_ + 16], st.dsem(), writes=[b_a[h_]])
        st.dma("scalar", b_tm[:, 16 * h_:16 * h_ + 16], BTv[:, 16 * h_:16 * h_ + 16], st.dsem(), writes=[b_b[h_]])
    tb = [st.sb(f"tb{i}", [128, 2, NTJ, 128], BF16) for i in range(2)]
    b_tb = [Buf(), Buf()]
    d_tb = [st.dsem(), st.dsem()]
    kst = [st.sb(f"kst{i}", [128, 4, D], F32) for i in range(2)]
    b_kst = [Buf(), Buf()]
    d_k = [st.dsem(), st.dsem()]
    osb = [st.sb(f"osb{i}", [128, 2, 512], F32) for i in range(2)]
    b_osb = [Buf(), Buf()]
    acc = [[st.ps() for _ in range(4)] for _ in range(2)]
    b_acc = [[Buf() for _ in range(4)] for _ in range(2)]
    KHv = KH_d.rearrange("r (fc p) c -> fc p r c", p=128)

    def load(fc):
        s = fc % 2
        st.dma("sync", tb[s][:, 0], FC_d[fc], d_tb[s], writes=[b_tb[s]])
        st.dma("sync", tb[s][:, 1], FS_d[fc], d_tb[s], writes=[b_tb[s]])

    load(0)
    it = 0
    for fc in range(NFC):
        s = fc % 2
        if fc + 1 < NFC:
            load(fc + 1)
        for hc in range(2):
            q = it % 2
            it += 1
            cs = slice(hc * 512, (hc + 1) * 512)
            for which, (src, b_src) in enumerate(((a_tm, b_a), (b_tm, b_b))):
                for par in range(2):
                    k = which * 2 + par

                    def mm(e, which=which, par=par, k=k, src=src, s=s, q=q, cs=cs):
                        ins = None
                        for jj in range(16):
                            j = par * 16 + jj
                            ins = e.matmul(acc[q][k][:], tb[s][:, which, j, :], src[:, j, cs], start=(jj == 0), stop=(jj == 15))
                        return ins
                    st.op("tensor", mm, reads=[b_tb[s], b_src[par]], writes=[b_acc[q][k]])
            st.op("scalar", lambda e, q=q: e.activation(osb[q][:, 0, :], acc[q][1][:], AF.Copy), reads=[b_acc[q][1]], writes=[b_osb[q]])
            st.op("scalar", lambda e, q=q: e.activation(osb[q][:, 1, :], acc[q][3][:], AF.Copy), reads=[b_acc[q][3]], writes=[b_osb[q]])
            st.op("vector", lambda e, q=q, s=s, cs=cs: e.tensor_tensor(kst[s][:, 0, cs], acc[q][0][:], osb[q][:, 0, :], ALU.add), reads=[b_acc[q][0], b_osb[q]], writes=[b_kst[s]])
            st.op("vector", lambda e, q=q, s=s, cs=cs: e.tensor_tensor(kst[s][:, 1, cs], acc[q][2][:], osb[q][:, 1, :], ALU.add), reads=[b_acc[q][2], b_osb[q]], writes=[b_kst[s]])
            st.op("vector", lambda e, q=q, s=s, cs=cs: e.tensor_tensor(kst[s][:, 2, cs], acc[q][0][:], osb[q][:, 0, :], ALU.subtract), reads=[b_acc[q][0], b_osb[q]], writes=[b_kst[s]])
            st.op("vector", lambda e, q=q, s=s, cs=cs: e.scalar_tensor_tensor(kst[s][:, 3, cs], acc[q][2][:], -1.0, osb[q][:, 1, :], ALU.mult, ALU.add), reads=[b_acc[q][2], b_osb[q]], writes=[b_kst[s]])
        if fc == NFC - 1:
            st.op("vector", lambda e, s=s: e.memset(kst[s][0:1, 2:4, :], 0.0), writes=[b_kst[s]], memset=True)
        st.dma("sync", KHv[fc], kst[s][:], d_k[s], reads=[b_kst[s]])
    st.finish()


def stage_hy_in(nc, name, xT_d, tok0, w_d, vecs_d, nv, gcol, vc, identb_d, VTM_d, X0TM_d):
    st = Stage(nc, name)
    vecs = st.sb("vecs", [128, nv], F32)
    b_vecs = Buf()
    st.dma("sync", vecs[:], vecs_d, st.dsem(), writes=[b_vecs])
    identb = st.sb("identb", [128, 128], BF16)
    b_idb = Buf()
    st.dma("gpsimd", identb[:], identb_d, st.dsem(), writes=[b_idb])
    ones = st.sb("ones", [128, 128], BF16)
    b_ones = Buf()
    st.op("vector", lambda e: e.memset(ones[:], 1.0), writes=[b_ones], memset=True)
    hT = st.sb("hT", [128, 8, L], BF16)
    b_h = Buf()
    xt = st.sb("xt", [128, 8, 512], F32)
    b_x = Buf()
    d_x = st.dsem()
    sq = [st.sb(f"sq{i}", [128, 2, 512], BF16) for i in range(2)]
    b_sq = [Buf(), Buf()]
    rs = st.sb("rs", [128, 512], F32)
    b_rs = Buf()
    wc = [st.sb(f"wc{i}", [128, 8, 3, 128], BF16) for i in range(2)]
    b_wc = [Buf(), Buf()]
    d_wc = [st.dsem(), st.dsem()]
    u2 = [st.sb(f"u{i}", [128, L], F32) for i in range(2)]
    b_u2 = [Buf(), Buf()]
    uc = [st.sb(f"uc{i}", [128, L], F32) for i in range(2)]
    b_uc = [Buf(), Buf()]
    g16 = st.sb("g16", [128, L], BF16)
    b_g16 = Buf()
    x16 = st.sb("x16", [128, L], BF16)
    b_x16 = Buf()
    stg = [st.sb(f"stg{i}", [128, 32, 128], BF16) for i in range(2)]
    b_stg = [Buf(), Buf()]
    d_stg = [st.dsem(), st.dsem()]
    ss_ps = st.ps()
    b_ss = Buf()
    ps = [st.ps() for _ in range(4)]
    b_ps = [Buf() for _ in range(4)]
    tp_ps = [st.ps([128, 1024], BF16), st.ps([128, 1024], BF16)]
    b_tp = [Buf(), Buf()]
    xTv = xT_d.rearrange("(c p) t -> p c t", p=128)
    wv_ = w_d.rearrange("(c p) (a n) -> p c a n", p=128, a=3)

    def loadw(cc):
        s = cc % 2
        for a in range(3):
            st.dma("gpsimd", wc[s][:, :, a, :], wv_[:, :, a, cc * 128:(cc + 1) * 128], d_wc[s], writes=[b_wc[s]])

    loadw(0)
    for i in range(8):
        t0 = tok0 + i * 512
        st.dma("sync", xt[:], xTv[:, :, t0:t0 + 512], d_x, writes=[b_x])
        emit_norm_stats(st, xt, b_x, sq, b_sq, ones, b_ones, ss_ps, b_ss, rs, b_rs, lnexp=True)
        for c in range(8):
            st.op("vector", lambda e, c=c, i=i: e.scalar_tensor_tensor(
                hT[:, c, i * 512:(i + 1) * 512], xt[:, c, :], vecs[:, gcol + c:gcol + c + 1], rs[:], ALU.mult, ALU.mult),
                reads=[b_x, b_rs, b_vecs], writes=[b_h])
    kk = 0
    npart = 0
    kctr = [0]
    pending = []
    bi, cw, cb = vc["b_in"], vc["conv_w"], vc["conv_b"]
    for cc in range(8):
        s = cc % 2
        if cc + 1 < 8:
            loadw(cc + 1)
        for part, dst_i in ((2, 0), (1, 1), (0, 1)):
            col = part * 8 + cc
            dst, b_dst = uc[dst_i], b_uc[dst_i]
            u, b_u = u2[npart % 2], b_u2[npart % 2]
            npart += 1
            for tl in range(8):
                p = kk % 4
                kk += 1

                def mm(e, p=p, part=part, tl=tl, s=s):
                    ins = None
                    for c in range(8):
                        ins = e.matmul(ps[p][:], wc[s][:, c, part, :], hT[:, c, tl * 512:(tl + 1) * 512], start=(c == 0), stop=(c == 7))
                    return ins
                st.op("tensor", mm, reads=[b_h, b_wc[s]], writes=[b_ps[p]])
                st.op("scalar", lambda e, p=p, tl=tl, col=col, u=u: e.activation(u[:, tl * 512:(tl + 1) * 512], ps[p][:], AF.Identity, bias=vecs[:, bi + col:bi + col + 1], scale=1.0),
                      reads=[b_ps[p], b_vecs], writes=[b_u])
            for fn_ in pending:
                fn_()
            pending.clear()
            for hh in range(2):
                st.op("scalar", lambda e, hh=hh, col=col, dst=dst, u=u: e.activation(
                    dst[:, hh * 2048:(hh + 1) * 2048], u[:, hh * 2048:(hh + 1) * 2048], AF.Identity,
                    bias=vecs[:, cb + col:cb + col + 1], scale=vecs[:, cw + 24 + col:cw + 24 + col + 1]),
                    reads=[b_u, b_vecs], writes=[b_dst])
            st.op("vector", lambda e, col=col, dst=dst, u=u: e.scalar_tensor_tensor(
                dst[:, 1:L], u[:, 0:L - 1], vecs[:, cw + col:cw + col + 1], dst[:, 1:L], ALU.mult, ALU.add),
                reads=[b_u, b_dst, b_vecs], writes=[b_dst])
            st.op("vector", lambda e, col=col, dst=dst, u=u: e.scalar_tensor_tensor(
                dst[:, 0:L - 1], u[:, 1:L], vecs[:, cw + 48 + col:cw + 48 + col + 1], dst[:, 0:L - 1], ALU.mult, ALU.add),
                reads=[b_u, b_dst, b_vecs], writes=[b_dst])
            if part == 1:
                st.op("vector", lambda e: e.tensor_tensor(g16[:], uc[0][:], uc[1][:], ALU.mult), reads=[b_uc[0], b_uc[1]], writes=[b_g16])
                pending.append(lambda cc=cc: emit_tm_store(st, g16, b_g16, identb, b_idb, tp_ps, b_tp, stg, b_stg, d_stg, VTM_d, cc, kctr))
            if part == 0:
                for hh in range(2):
                    st.op("scalar", lambda e, hh=hh: e.activation(x16[:, hh * 2048:(hh + 1) * 2048], uc[1][:, hh * 2048:(hh + 1) * 2048], AF.Copy), reads=[b_uc[1]], writes=[b_x16])
                pending.append(lambda cc=cc: emit_tm_store(st, x16, b_x16, identb, b_idb, tp_ps, b_tp, stg, b_stg, d_stg, X0TM_d, cc, kctr))
    for fn_ in pending:
        fn_()
    st.finish()


def stage_hy_fwd(nc, name, VTM_d, FC_d, FS_d, KH_d, YH_d):
    st = Stage(nc, name)
    v_tm = st.sb("v_tm", [128, NTJ, D], BF16)
    b_vh = [Buf(), Buf()]
    VTv = VTM_d.rearrange("(tc p) c -> p tc c", p=128)
    st.dma("sync", v_tm[:, 0:16], VTv[:, 0:16], st.dsem(), writes=[b_vh[0]])
    st.dma("scalar", v_tm[:, 16:32], VTv[:, 16:32], st.dsem(), writes=[b_vh[1]])
    tb = [st.sb(f"tb{i}", [128, 2, NTJ, 128], BF16) for i in range(2)]
    b_tb = [Buf(), Buf()]
    d_tb = [st.dsem(), st.dsem()]
    kh = [st.sb(f"kh{i}", [128, 4, D], F32) for i in range(2)]
    b_kh = [Buf(), Buf()]
    d_kh = [st.dsem(), st.dsem()]
    yst = [st.sb(f"yst{i}", [128, 4, D], BF16) for i in range(2)]
    b_yst = [Buf(), Buf()]
    d_y = [st.dsem(), st.dsem()]
    NB = 6
    bt = [[st.sb(f"bt{q}_{i}", [128, 512], F32) for i in range(NB)] for q in range(2)]
    b_bt = [[Buf() for _ in range(NB)] for _ in range(2)]
    mt = [st.sb(f"mt{i}", [128, 512], F32) for i in range(8)]
    b_mt = [Buf() for _ in range(8)]
    acc = [[st.ps() for _ in range(4)] for _ in range(2)]
    b_acc = [[Buf() for _ in range(4)] for _ in range(2)]
    KHv = KH_d.rearrange("r (fc p) c -> fc p r c", p=128)
    YHv = YH_d.rearrange("r (fc p) c -> fc p r c", p=128)

    def load(fc):
        s = fc % 2
        st.dma("sync", tb[s][:, 0], FC_d[fc], d_tb[s], writes=[b_tb[s]])
        st.dma("sync", tb[s][:, 1], FS_d[fc], d_tb[s], writes=[b_tb[s]])
        st.dma("sync", kh[s][:], KHv[fc], d_kh[s], writes=[b_kh[s]])

    def tt(eng, out, b_out, i0, b0, i1, b1, op):
        st.op(eng, lambda e: e.tensor_tensor(out, i0, i1, op), reads=[b0, b1], writes=[b_out])

    load(0)
    it = 0
    for fc in range(NFC):
        s = fc % 2
        if fc + 1 < NFC:
            load(fc + 1)
        for hc in range(2):
            q = it % 2
            it += 1
            cs = slice(hc * 512, (hc + 1) * 512)
            for which in range(2):
                for par in range(2):
                    k = which * 2 + par

                    def mm(e, which=which, par=par, k=k, s=s, q=q, cs=cs):
                        ins = None
                        for jj in range(16):
                            j = par * 16 + jj
                            ins = e.matmul(acc[q][k][:], tb[s][:, which, j, :], v_tm[:, j, cs], start=(jj == 0), stop=(jj == 15))
                        return ins
                    st.op("tensor", mm, reads=[b_tb[s], b_vh[par]], writes=[b_acc[q][k]])
            B, bB = bt[q], b_bt[q]
            A_, bA = acc[q], b_acc[q]
            st.op("scalar", lambda e, B=B, A_=A_: e.activation(B[0][:], A_[1][:], AF.Copy), reads=[bA[1]], writes=[bB[0]])
            st.op("scalar", lambda e, B=B, A_=A_: e.activation(B[1][:], A_[3][:], AF.Copy), reads=[bA[3]], writes=[bB[1]])
            tt("vector", B[2][:], bB[2], A_[0][:], bA[0], B[0][:], bB[0], ALU.add)
            tt("vector", B[3][:], bB[3], A_[0][:], bA[0], B[0][:], bB[0], ALU.subtract)
            tt("vector", B[4][:], bB[4], A_[2][:], bA[2], B[1][:], bB[1], ALU.add)
            tt("vector", B[5][:], bB[5], A_[2][:], bA[2], B[1][:], bB[1], ALU.subtract)
            K = kh[s]
            bK = b_kh[s]
            tt("vector", mt[0][:], b_mt[0], B[2][:], bB[2], K[:, 0, cs], bK, ALU.mult)
            tt("vector", mt[1][:], b_mt[1], B[4][:], bB[4], K[:, 1, cs], bK, ALU.mult)
            tt("vector", mt[0][:], b_mt[0], mt[0][:], b_mt[0], mt[1][:], b_mt[1], ALU.add)
            tt("vector", mt[2][:], b_mt[2], B[2][:], bB[2], K[:, 1, cs], bK, ALU.mult)
            tt("vector", mt[3][:], b_mt[3], B[4][:], bB[4], K[:, 0, cs], bK, ALU.mult)
            tt("vector", mt[2][:], b_mt[2], mt[2][:], b_mt[2], mt[3][:], b_mt[3], ALU.subtract)
            tt("gpsimd", mt[4][:], b_mt[4], B[3][:], bB[3], K[:, 2, cs], bK, ALU.mult)
            tt("gpsimd", mt[5][:], b_mt[5], B[5][:], bB[5], K[:, 3, cs], bK, ALU.mult)
            tt("gpsimd", mt[4][:], b_mt[4], mt[4][:], b_mt[4], mt[5][:], b_mt[5], ALU.subtract)
            tt("gpsimd", mt[6][:], b_mt[6], B[3][:], bB[3], K[:, 3, cs], bK, ALU.mult)
            tt("gpsimd", mt[7][:], b_mt[7], B[5][:], bB[5], K[:, 2, cs], bK, ALU.mult)
            tt("gpsimd", mt[6][:], b_mt[6], mt[6][:], b_mt[6], mt[7][:], b_mt[7], ALU.add)
            Y = yst[s]
            bY = b_yst[s]
            tt("vector", Y[:, 0, cs], bY, mt[0][:], b_mt[0], mt[4][:], b_mt[4], ALU.add)
            tt("vector", Y[:, 1, cs], bY, mt[2][:], b_mt[2], mt[6][:], b_mt[6], ALU.subtract)
            tt("gpsimd", Y[:, 2, cs], bY, mt[0][:], b_mt[0], mt[4][:], b_mt[4], ALU.subtract)
            tt("gpsimd", Y[:, 3, cs], bY, mt[2][:], b_mt[2], mt[6][:], b_mt[6], ALU.add)
        st.dma("sync", YHv[fc], yst[s][:], d_y[s], reads=[b_yst[s]])
    st.finish()


def stage_hy_inv(nc, name, YH_d, GC_d, GS_d, VTM_d, X0TM_d, dbc_d, identb_d, ZT_d):
    st = Stage(nc, name)
    identb = st.sb("identb", [128, 128], BF16)
    b_idb = Buf()
    st.dma("gpsimd", identb[:], identb_d, st.dsem(), writes=[b_idb])
    yh = st.sb("yh", [128, 4, NFC, D], BF16)
    b_yh = [Buf() for _ in range(4)]
    YHv = YH_d.rearrange("r (fc p) c -> r p fc c", p=128)
    for r in range(4):
        st.dma("sync" if r < 2 else "scalar", yh[:, r], YHv[r], st.dsem(), writes=[b_yh[r]])
    dbc = st.sb("dbc", [128, D], F32)
    b_dbc = Buf()
    st.dma("sync", dbc[:], dbc_d, st.dsem(), writes=[b_dbc])
    NS = 3
    tb = [st.sb(f"tb{i}", [128, 2, NFC, 128], BF16) for i in range(NS)]
    b_tb = [Buf() for _ in range(NS)]
    d_tb = [st.dsem() for _ in range(NS)]
    vx = [st.sb(f"vx{i}", [128, 2, D], BF16) for i in range(NS)]
    b_vx = [Buf() for _ in range(NS)]
    d_vx = [st.dsem() for _ in range(NS)]
    tm = [st.sb(f"tm{i}", [128, D], F32) for i in range(2)]
    b_tm = [Buf(), Buf()]
    z16 = [st.sb(f"z16{i}", [128, D], BF16) for i in range(2)]
    b_z = [Buf(), Buf()]
    zst = [st.sb(f"zst{i}", [128, 8, 256], BF16) for i in range(2)]
    b_zst = [Buf(), Buf()]
    d_z = [st.dsem(), st.dsem()]
    y_ps = [st.ps([128, 1024]) for _ in range(2)]
    b_y = [Buf(), Buf()]
    tp_ps = [st.ps([128, 1024], BF16), st.ps([128, 1024], BF16)]
    b_tp = [Buf(), Buf()]
    ZTv = ZT_d.rearrange("(c p) t -> p c t", p=128)
    order = [(jj, par) for jj in range(16) for par in range(2)]

    def load(n):
        jj, par = order[n]
        j = par * 16 + jj
        s3 = n % NS
        st.dma("sync", tb[s3][:, 0], GC_d[j], d_tb[s3], writes=[b_tb[s3]])
        st.dma("sync", tb[s3][:, 1], GS_d[j], d_tb[s3], writes=[b_tb[s3]])
        st.dma("sync", vx[s3][:, 0, :], VTM_d[j * 128:(j + 1) * 128, :], d_vx[s3], writes=[b_vx[s3]])
        st.dma("sync", vx[s3][:, 1, :], X0TM_d[j * 128:(j + 1) * 128, :], d_vx[s3], writes=[b_vx[s3]])

    load(0)
    load(1)
    deferred = None
    for n in range(32):
        jj, par = order[n]
        s = n % 2
        s3 = n % NS
        zs = jj % 2
        if n + 2 < 32:
            load(n + 2)

        def mm(e, s=s, par=par, s3=s3):
            ins = None
            for hc in range(2):
                k = 0
                for which in range(2):
                    for fc in range(NFC):
                        ins = e.matmul(y_ps[s][:, hc * 512:(hc + 1) * 512], tb[s3][:, which, fc, :], yh[:, 2 * par + which, fc, hc * 512:(hc + 1) * 512],
                                       start=(k == 0), stop=(k == 2 * NFC - 1))
                        k += 1
            return ins
        st.op("tensor", mm, reads=[b_tb[s3], b_yh[2 * par], b_yh[2 * par + 1]], writes=[b_y[s]])
        st.op("gpsimd", lambda e, s=s, s3=s3: e.tensor_tensor(tm[s][:], vx[s3][:, 0, :], dbc[:], ALU.mult), reads=[b_vx[s3], b_dbc], writes=[b_tm[s]])
        st.op("vector", lambda e, s=s: e.tensor_tensor(tm[s][:], y_ps[s][:], tm[s][:], ALU.add), reads=[b_y[s], b_tm[s]], writes=[b_tm[s]])
        st.op("gpsimd", lambda e, s=s, s3=s3: e.tensor_tensor(z16[s][:], tm[s][:], vx[s3][:, 1, :], ALU.mult), reads=[b_tm[s], b_vx[s3]], writes=[b_z[s]])

        def emit_tr(s=s, zs=zs, par=par, jj=jj):
            def tr(e):
                ins = None
                for c in range(8):
                    ins = e.transpose(tp_ps[s][:, c * 128:(c + 1) * 128], z16[s][:, c * 128:(c + 1) * 128], identb[:])
                return ins
            st.op("tensor", tr, reads=[b_z[s], b_idb], writes=[b_tp[s]])
            st.op("scalar", lambda e: e.activation(zst[zs][:, :, par:256:2], tp_ps[s][:].rearrange("p (c t) -> p c t", t=128), AF.Copy),
                  reads=[b_tp[s]], writes=[b_zst[zs]])
            if par == 1:
                st.dma("sync", ZTv[:, :, jj * 256:(jj + 1) * 256], zst[zs][:], d_z[zs], reads=[b_zst[zs]])
        if deferred is not None:
            deferred()
        deferred = emit_tr
    deferred()
    st.finish()


def stage_hy_out(nc, name, xT_d, tok0, ZT_d, wo_d, vecs_d, nv, bocol):
    st = Stage(nc, name)
    vecs = st.sb("vecs", [128, nv], F32)
    b_vecs = Buf()
    st.dma("sync", vecs[:], vecs_d, st.dsem(), writes=[b_vecs])
    wo = st.sb("wo", [128, 8, D], BF16)
    b_wo = Buf()
    st.dma("gpsimd", wo[:], wo_d.rearrange("(c p) n -> p c n", p=128), st.dsem(), writes=[b_wo])
    xt = [st.sb(f"xt{i}", [128, 8, 512], F32) for i in range(2)]
    b_x = [Buf(), Buf()]
    d_x = [st.dsem(), st.dsem()]
    d_o = [st.dsem(), st.dsem()]
    zt = [st.sb(f"zt{i}", [128, 8, 512], BF16) for i in range(2)]
    b_z = [Buf(), Buf()]
    d_zt = [st.dsem(), st.dsem()]
    y_ps = [st.ps(), st.ps()]
    b_y = [Buf(), Buf()]
    xTv = xT_d.rearrange("(c p) t -> p c t", p=128)
    ZTv = ZT_d.rearrange("(c p) t -> p c t", p=128)

    def load(i):
        s = i % 2
        st.dma("sync", xt[s][:], xTv[:, :, tok0 + i * 512:tok0 + (i + 1) * 512], d_x[s], writes=[b_x[s]])
        st.dma("sync", zt[s][:], ZTv[:, :, i * 512:(i + 1) * 512], d_zt[s], writes=[b_z[s]])

    load(0)
    for i in range(8):
        s = i % 2
        if i + 1 < 8:
            load(i + 1)
        for m in range(8):
            p = m % 2

            def mm(e, m=m, p=p, s=s):
                ins = None
                for c in range(8):
                    ins = e.matmul(y_ps[p][:], wo[:, c, m * 128:(m + 1) * 128], zt[s][:, c, :], start=(c == 0), stop=(c == 7))
                return ins
            st.op("tensor", mm, reads=[b_z[s], b_wo], writes=[b_y[p]])
            st.op("vector", lambda e, m=m, p=p, s=s: e.scalar_tensor_tensor(
                xt[s][:, m, :], y_ps[p][:], vecs[:, bocol + m:bocol + m + 1], xt[s][:, m, :], ALU.add, ALU.add),
                reads=[b_y[p], b_x[s], b_vecs], writes=[b_x[s]])
        st.dma("sync", xTv[:, :, tok0 + i * 512:tok0 + (i + 1) * 512], xt[s][:], d_o[s], reads=[b_x[s]])
    st.finish()


DEPTH = 4
_TABLE_CACHE = {}


def _tables():
    if not _TABLE_CACHE:
        FC, FS, GC, GS = dft_tables()
        rc, rsn = rope_tables()
        zT, trow = hyena_pos_tables()
        _TABLE_CACHE.update(FC=FC, FS=FS, GC=GC, GS=GS, rc=rc, rsn=rsn, zT=zT, trow=trow,
                            ident=np.eye(128, dtype=np.float32))
    return _TABLE_CACHE


def prep_shared(inp):
    f32 = lambda a: np.ascontiguousarray(np.asarray(a, np.float32))
    sh = {}
    vt = VecTable()
    for i in range(DEPTH):
        vt.add_feat(f"f{i}0", inp["ffn_norm_g"][i, 0])
        vt.add_feat(f"mix{i}", inp["mix_norm_g"][i])
        vt.add_feat(f"f{i}1", inp["ffn_norm_g"][i, 1])
    vt.add_feat("fin", inp["final_norm_g"])
    sw = np.arange(64) ^ 1
    na = inp["attn_w_in"].shape[0]
    nh = inp["hy_w_in"].shape[0]
    wq, wk, wv = [], [], []
    for j in range(na):
        qg = np.asarray(inp["attn_q_gain"][j], np.float32)
        kg = np.asarray(inp["attn_k_gain"][j], np.float32)
        vt.add(f"qg{j}", np.tile(qg, 2)[:, None])
        vt.add(f"qgs{j}", np.tile(qg[sw], 2)[:, None])
        vt.add(f"kg{j}", np.tile(kg, 2)[:, None])
        vt.add(f"kgs{j}", np.tile(kg[sw], 2)[:, None])
        a, b, c = prep_attn_weights(np.asarray(inp["attn_w_in"][j], np.float32))
        wq.append(a)
        wk.append(b)
        wv.append(c)
    for j in range(nh):
        vt.add_feat(f"b_in{j}", inp["hy_b_in"][j])
        vt.add_feat(f"conv_w{j}", np.asarray(inp["hy_conv_w"][j]).reshape(-1))
        vt.add_feat(f"conv_b{j}", inp["hy_conv_b"][j])
        vt.add_feat(f"decay{j}", np.asarray(inp["hy_decay"][j]).reshape(-1))
        vt.add_feat(f"b_out{j}", inp["hy_b_out"][j])
        for key, src in (("b1", "hy_f_b1"), ("b2", "hy_f_b2"), ("b3", "hy_f_b3"), ("freq", "hy_f_freq")):
            vt.add(f"{key}_{j}", np.asarray(inp[src][j], np.float32)[:, None])
    sh["vecs"] = vt.build()
    sh["wq"] = np.stack(wq)
    sh["wk"] = np.stack(wk)
    sh["wv"] = np.stack(wv)
    sh["hy_dbc"] = np.ascontiguousarray(np.broadcast_to(np.asarray(inp["hy_d_bias"], np.float32)[:, None, :], (nh, 128, D)))
    sh["attn_w_out"] = np.ascontiguousarray(np.asarray(inp["attn_w_out"], np.float32)[:, q_head_perm(), :])
    for k in ("ffn_w_in", "ffn_w_out", "hy_w_in", "hy_w_out", "hy_f_w1", "hy_f_w2", "hy_f_w3", "hy_f_w_out"):
        sh[k] = f32(inp[k])
    sh.update(_tables())
    return sh, vt


def build_program(sh, vt, nseq, plan=None):
    nc = bass.Bass("TRN2", target_bir_lowering=False)
    ntok = nseq * L
    ap = {}
    for k, a in sh.items():
        dt = BF16 if a.dtype == ml_dtypes.bfloat16 else F32
        ap[k] = nc.dram_tensor(k, list(a.shape), dt, kind="ExternalInput").ap()
    x_d = nc.dram_tensor("x", [ntok, D], F32, kind="ExternalInput").ap()
    out_d = nc.dram_tensor("out", [ntok, D], F32, kind="ExternalOutput").ap()
    xT = nc.dram_tensor("xT", [D, ntok], F32, kind="Internal").ap()
    KT = nc.dram_tensor("KT", [nseq, 2, 128, L], BF16, kind="Internal").ap()
    VA = nc.dram_tensor("VA", [nseq, 128, 32 * 512], BF16, kind="Internal").ap()
    ATM = nc.dram_tensor("ATM", [L, D], BF16, kind="Internal").ap()
    BTM = nc.dram_tensor("BTM", [L, D], BF16, kind="Internal").ap()
    KH = nc.dram_tensor("KH", [4, NFP, D], F32, kind="Internal").ap()
    VTM = nc.dram_tensor("VTM", [L, D], BF16, kind="Internal").ap()
    X0TM = nc.dram_tensor("X0TM", [L, D], BF16, kind="Internal").ap()
    YH = nc.dram_tensor("YH", [4, NFP, D], BF16, kind="Internal").ap()
    ZT = nc.dram_tensor("ZT", [D, L], BF16, kind="Internal").ap()
    vecs_d = ap["vecs"]
    nv = sh["vecs"].shape[1]
    col = lambda key: vt.idx[key][0]
    if plan is None:
        plan = ["tin"]
        for i in range(DEPTH):
            plan += [f"ffn{i}0", f"mix{i}", f"ffn{i}1"]
        plan += ["fin"]
    for item in plan:
        if item == "tin":
            stage_transpose_in(nc, x_d, xT, ap["ident"], ntok)
        elif item == "fin":
            stage_final(nc, xT, out_d, vecs_d, nv, col("fin"), ap["ident"], ntok)
        elif item.startswith("ffn"):
            i, k = int(item[3]), int(item[4])
            stage_ffn(nc, item, xT, ap["ffn_w_in"][i, k], ap["ffn_w_out"][i, k], vecs_d, nv, col(f"f{i}{k}"), ntok)
        elif item.startswith("mix"):
            i = int(item[3])
            j = i // 2
            if i % 2 == 0:
                stage_attn_kv(nc, f"akv{i}", xT, ap["wk"][j], ap["wv"][j], vecs_d, nv, col(f"mix{i}"), col(f"kg{j}"), col(f"kgs{j}"),
                              ap["rc"], ap["rsn"], KT, VA, nseq)
                for s in range(nseq):
                    stage_attn_q(nc, f"aq{i}{s}", xT, s * L, ap["wq"][j], ap["attn_w_out"][j], vecs_d, nv, col(f"mix{i}"),
                                 col(f"qg{j}"), col(f"qgs{j}"), ap["rc"], ap["rsn"], KT[s], VA[s])
            else:
                vc = {k: col(f"{k}_{j}") for k in ("b1", "b2", "b3", "freq")}
                vc.update({k: col(f"{k}{j}") for k in ("b_in", "conv_w", "conv_b", "decay")})
                stage_hy_filter(nc, f"hf{i}", ap["zT"], ap["trow"], ap["hy_f_w1"][j], ap["hy_f_w2"][j], ap["hy_f_w3"][j], ap["hy_f_w_out"][j],
                                vecs_d, nv, vc, ap["ident"], ATM, BTM)
                stage_hy_kdft(nc, f"hk{i}", ATM, BTM, ap["FC"], ap["FS"], KH)
                for s in range(nseq):
                    stage_hy_in(nc, f"hi{i}{s}", xT, s * L, ap["hy_w_in"][j], vecs_d, nv, col(f"mix{i}"), vc, ap["ident"], VTM, X0TM)
                    stage_hy_fwd(nc, f"hw{i}{s}", VTM, ap["FC"], ap["FS"], KH, YH)
                    stage_hy_inv(nc, f"hv{i}{s}", YH, ap["GC"], ap["GS"], VTM, X0TM, ap["hy_dbc"][j], ap["ident"], ZT)
                    stage_hy_out(nc, f"ho{i}{s}", xT, s * L, ZT, ap["hy_w_out"][j], vecs_d, nv, col(f"b_out{j}"))
    return nc


def kernel(**inputs):
    x = np.asarray(inputs["x"], np.float32)
    B = x.shape[0]
    nseq = B // NCORES
    sh, vt = prep_shared(inputs)
    nc = build_program(sh, vt, nseq)
    in_maps = []
    for c in range(NCORES):
        m = dict(sh)
        m["x"] = np.ascontiguousarray(x[c * nseq:(c + 1) * nseq].reshape(nseq * L, D))
        in_maps.append(m)
    res = run_bass_kernel_spmd(nc, in_maps, core_ids=list(range(NCORES)))
    out = np.stack([np.asarray(r["out"], np.float32).reshape(nseq, L, D) for r in res.results], axis=0)
    return out.reshape(B, L, D)
```

```python
import math
import numpy as np
import ml_dtypes
import concourse.bass as bass
import concourse.mybir as mybir
from concourse.bass_utils import run_bass_kernel_spmd

F32 = mybir.dt.float32
BF16 = mybir.dt.bfloat16
ALU = mybir.AluOpType
AF = mybir.ActivationFunctionType
AX = mybir.AxisListType

ENGS = ("tensor", "vector", "scalar", "gpsimd", "sync")

D = 1024
DFF = 2816
NJ = DFF // 128
L = 4096
NCORES = 8
EPS = 1e-6
TWO_PI = 2.0 * math.pi


class Buf:
    __slots__ = ("w", "r", "strict")

    def __init__(self):
        self.w = None
        self.r = {}
        self.strict = False


class Stage:
    def __init__(self, nc, name):
        self.nc = nc
        self.name = name
        self.q = {e: [] for e in ENGS}
        self.cnt = {e: 0 for e in ENGS}
        self.seen = {e: {} for e in ENGS}
        self.sems = {}
        self.cleanup = nc.cleanup_on_exit()
        self.cleanup.__enter__()
        for e in ENGS:
            self.sems[e] = nc.alloc_semaphore(name=f"{name}_s_{e}")
        self.dsems = []
        self.nps = 0

    def sb(self, name, shape, dtype):
        return self.nc.alloc_sbuf_tensor(f"{self.name}_{name}", list(shape), dtype)

    def ps(self, shape=(128, 512), dtype=F32):
        self.nps += 1
        return self.nc.alloc_psum_tensor(f"{self.name}_ps{self.nps}", list(shape), dtype)

    def dsem(self):
        s = self.nc.alloc_semaphore(name=f"{self.name}_d{len(self.dsems)}")
        d = [s, 0]
        self.dsems.append(d)
        return d

    def _waits(self, eng, reads, writes, force_waw=False):
        need = {}

        def add(kv, raw, waw=False):
            if kv is None:
                return
            k, v = kv
            if isinstance(k, str) and k == eng and not raw and (eng == "tensor" or not waw):
                return
            kk = k if isinstance(k, str) else id(k)
            if kk not in need or need[kk][1] < v:
                need[kk] = (k, v)

        for b in reads:
            add(b.w, True)
        for b in writes:
            add(b.w, False, force_waw or b.strict)
            for kv in b.r.values():
                add(kv, False)
        out = []
        for kk, (k, v) in need.items():
            if self.seen[eng].get(kk, 0) < v:
                self.seen[eng][kk] = v
                out.append((self.sems[k] if isinstance(k, str) else k[0], v))
        return out

    @staticmethod
    def _note_read(b, k, v):
        kk = k if isinstance(k, str) else id(k)
        if kk not in b.r or b.r[kk][1] < v:
            b.r[kk] = (k, v)

    def op(self, eng, fn, reads=(), writes=(), memset=False):
        wl = self._waits(eng, reads, writes, force_waw=memset)
        self.cnt[eng] += 1
        v = self.cnt[eng]
        sem = self.sems[eng]

        def emit(e):
            for s, val in wl:
                e.wait_ge(s, val)
            fn(e).then_inc(sem, 1)

        self.q[eng].append(emit)
        for b in reads:
            self._note_read(b, eng, v)
        for b in writes:
            b.w = (eng, v)
            b.r = {}
            b.strict = memset

    def dma(self, queue, out_ap, in_ap, ds, reads=(), writes=()):
        wl = self._waits(queue, reads, writes)
        ds[1] += 16
        v = ds[1]
        sem = ds[0]

        def emit(e):
            for s, val in wl:
                e.wait_ge(s, val)
            e.dma_start(out=out_ap, in_=in_ap).then_inc(sem, 16)

        self.q[queue].append(emit)
        for b in reads:
            self._note_read(b, ds, v)
        for b in writes:
            b.w = (ds, v)
            b.r = {}

    def finish(self):
        nc = self.nc
        q = self.q
        fin = [(d[0], d[1]) for d in self.dsems if d[1] > 0]

        def emit_fin(e):
            for s_, v_ in fin:
                e.wait_ge(s_, v_)

        q["sync"].append(emit_fin)
        with nc.Block() as block:
            @block.tensor
            def _(e):
                for f in q["tensor"]:
                    f(e)

            @block.vector
            def _(e):
                for f in q["vector"]:
                    f(e)

            @block.scalar
            def _(e):
                for f in q["scalar"]:
                    f(e)

            @block.gpsimd
            def _(e):
                for f in q["gpsimd"]:
                    f(e)

            @block.sync
            def _(e):
                for f in q["sync"]:
                    f(e)
        self.cleanup.__exit__(None, None, None)


class VecTable:
    def __init__(self):
        self.cols = []
        self.idx = {}

    def add(self, key, arr2d):
        a = np.zeros((128, arr2d.shape[1]), np.float32)
        a[:arr2d.shape[0]] = arr2d
        self.idx[key] = (sum(c.shape[1] for c in self.cols), a.shape[1])
        self.cols.append(a)

    def add_feat(self, key, vec):
        v = np.asarray(vec, np.float32)
        self.add(key, np.ascontiguousarray(v.reshape(-1, 128).T))

    def build(self):
        return np.ascontiguousarray(np.concatenate(self.cols, axis=1))


def stage_transpose_in(nc, x_d, xT_d, ident_d, ntok):
    st = Stage(nc, "tin")
    ident = st.sb("ident", [128, 128], F32)
    b_id = Buf()
    d_id = st.dsem()
    st.dma("sync", ident[:], ident_d, d_id, writes=[b_id])
    xin = [st.sb(f"xin{i}", [128, 4, D], F32) for i in range(2)]
    b_xin = [Buf(), Buf()]
    xo = [st.sb(f"xo{i}", [128, 8, 512], F32) for i in range(2)]
    b_xo = [Buf(), Buf()]
    d_in = [st.dsem(), st.dsem()]
    d_out = [st.dsem(), st.dsem()]
    pss = [st.ps() for _ in range(8)]
    b_ps = [Buf() for _ in range(8)]
    xv = x_d.rearrange("(n s p) f -> n p s f", p=128, s=4)
    xTv = xT_d.rearrange("(c p) t -> p c t", p=128)
    NT = ntok // 512
    for i in range(NT):
        sl = i % 2
        st.dma("sync", xin[sl][:], xv[i], d_in[sl], writes=[b_xin[sl]])
        for c in range(8):
            def mm(e, c=c, sl=sl):
                ins = None
                for s in range(4):
                    ins = e.transpose(pss[c][:, s * 128:(s + 1) * 128], xin[sl][:, s, c * 128:(c + 1) * 128], ident[:])
                return ins
            st.op("tensor", mm, reads=[b_xin[sl], b_id], writes=[b_ps[c]])
            if c % 2 == 0:
                st.op("vector", lambda e, c=c, sl=sl: e.tensor_copy(xo[sl][:, c, :], pss[c][:]),
                      reads=[b_ps[c]], writes=[b_xo[sl]])
            else:
                st.op("scalar", lambda e, c=c, sl=sl: e.activation(xo[sl][:, c, :], pss[c][:], AF.Copy),
                      reads=[b_ps[c]], writes=[b_xo[sl]])
        st.dma("sync", xTv[:, :, i * 512:(i + 1) * 512], xo[sl][:], d_out[sl], reads=[b_xo[sl]])
    st.finish()


def emit_norm_stats(st, xt, b_x, sq, b_sq, ones, b_ones, ss_ps, b_ss, rs, b_rs, width=512, lnexp=False):
    for h in range(4):
        s2 = h % 2
        st.op("gpsimd", lambda e, h=h, s2=s2: e.tensor_tensor(sq[s2][:], xt[:, 2 * h:2 * h + 2, :], xt[:, 2 * h:2 * h + 2, :], ALU.mult),
              reads=[b_x], writes=[b_sq[s2]])

        def mm(e, h=h, s2=s2):
            ins = None
            for k in range(2):
                ins = e.matmul(ss_ps[:, 0:width], ones[:], sq[s2][:, k, :], start=(h == 0 and k == 0), stop=(h == 3 and k == 1))
            return ins
        st.op("tensor", mm, reads=[b_sq[s2], b_ones], writes=[b_ss])
    if lnexp:
        st.op("scalar", lambda e: e.activation(rs[:, 0:width], ss_ps[:, 0:width], AF.Ln, bias=EPS, scale=1.0 / D), reads=[b_ss], writes=[b_rs])
        st.op("scalar", lambda e: e.activation(rs[:, 0:width], rs[:, 0:width], AF.Exp, scale=-0.5), reads=[b_rs], writes=[b_rs])
        return
    st.op("scalar", lambda e: e.activation(rs[:, 0:width], ss_ps[:, 0:width], AF.Sqrt, bias=EPS, scale=1.0 / D), reads=[b_ss], writes=[b_rs])
    st.op("vector", lambda e: e.reciprocal(rs[:, 0:width], rs[:, 0:width]), reads=[b_rs], writes=[b_rs])


def stage_ffn(nc, name, xT_d, w_in_d, w_out_d, vecs_d, nv, gcol, ntok):
    st = Stage(nc, name)
    NT = ntok // 512
    w_in = st.sb("w_in", [128, 8, 2 * DFF], BF16)
    w_out = st.sb("w_out", [128, NJ, D], BF16)
    JB = [0, 6, 12, 17, 22]
    b_win = [Buf() for _ in range(4)]
    d_wq = [st.dsem() for _ in range(4)]
    jq = [max(q for q in range(4) if JB[q] <= j) for j in range(NJ)]
    b_wout = [Buf() for _ in range(2)]
    d_w = st.dsem()
    d_w2 = st.dsem()
    vecs = st.sb("vecs", [128, nv], F32)
    b_vecs = Buf()
    d_v = st.dsem()
    st.dma("sync", vecs[:], vecs_d, d_v, writes=[b_vecs])
    ones = st.sb("ones", [128, 128], BF16)
    b_ones = Buf()
    st.op("vector", lambda e: e.memset(ones[:], 1.0), writes=[b_ones], memset=True)
    xt = [st.sb(f"xt{i}", [128, 8, 512], F32) for i in range(2)]
    b_x = [Buf(), Buf()]
    d_x = [st.dsem(), st.dsem()]
    d_o = [st.dsem(), st.dsem()]
    hT = st.sb("hT", [128, 8, 512], BF16)
    b_h = Buf()
    aT = st.sb("aT", [128, NJ, 512], BF16)
    b_a = [Buf() for _ in range(NJ)]
    sq = [st.sb(f"sq{i}", [128, 2, 512], BF16) for i in range(2)]
    b_sq = [Buf(), Buf()]
    rs = st.sb("rs", [128, 512], F32)
    b_rs = Buf()
    sl_t = [st.sb(f"sl{i}", [128, 512], F32) for i in range(2)]
    b_sl = [Buf(), Buf()]
    ss_ps = st.ps()
    b_ss = Buf()
    g_ps = [st.ps(), st.ps()]
    u_ps = [st.ps(), st.ps()]
    b_g = [Buf(), Buf()]
    b_u = [Buf(), Buf()]
    y_ps = [st.ps(), st.ps()]
    b_y = [Buf(), Buf()]
    xTv = xT_d.rearrange("(c p) t -> p c t", p=128)
    w_in_v = w_in_d.rearrange("(c p) n -> p c n", p=128)
    w_out_v = w_out_d.rearrange("(j p) n -> p j n", p=128)

    def load_x(i):
        s = i % 2
        st.dma("sync", xt[s][:], xTv[:, :, i * 512:(i + 1) * 512], d_x[s], writes=[b_x[s]])

    load_x(0)
    if NT > 1:
        load_x(1)
    def pro_a(i):
        s = i % 2
        emit_norm_stats(st, xt[s], b_x[s], sq, b_sq, ones, b_ones, ss_ps, b_ss, rs, b_rs)

    def pro_b(i):
        s = i % 2
        for c in range(8):
            st.op("vector", lambda e, c=c, s=s: e.scalar_tensor_tensor(
                hT[:, c, :], xt[s][:, c, :], vecs[:, gcol + c:gcol + c + 1], rs[:], ALU.mult, ALU.mult),
                reads=[b_x[s], b_rs, b_vecs], writes=[b_h])

    def up(i):
        for j in range(NJ):
            p = j % 2

            def mmg(e, j=j, p=p):
                ins = None
                for c in range(8):
                    ins = e.matmul(g_ps[p][:], w_in[:, c, j * 128:(j + 1) * 128], hT[:, c, :], start=(c == 0), stop=(c == 7))
                return ins

            def mmu(e, j=j, p=p):
                ins = None
                for c in range(8):
                    ins = e.matmul(u_ps[p][:], w_in[:, c, DFF + j * 128:DFF + (j + 1) * 128], hT[:, c, :], start=(c == 0), stop=(c == 7))
                return ins
            st.op("tensor", mmg, reads=[b_h, b_win[jq[j]]], writes=[b_g[p]])
            st.op("tensor", mmu, reads=[b_h, b_win[jq[j]]], writes=[b_u[p]])
            st.op("scalar", lambda e, p=p: e.activation(sl_t[p][:], g_ps[p][:], AF.Silu), reads=[b_g[p]], writes=[b_sl[p]])
            st.op("vector", lambda e, p=p, j=j: e.tensor_tensor(aT[:, j, :], u_ps[p][:], sl_t[p][:], ALU.mult),
                  reads=[b_u[p], b_sl[p]], writes=[b_a[j]])

    def down(i):
        s = i % 2
        for m in range(8):
            p = m % 2

            def mmy(e, m=m, p=p):
                ins = None
                for j in range(NJ):
                    ins = e.matmul(y_ps[p][:], w_out[:, j, m * 128:(m + 1) * 128], aT[:, j, :], start=(j == 0), stop=(j == NJ - 1))
                return ins
            st.op("tensor", mmy, reads=b_a + b_wout, writes=[b_y[p]])
            st.op("vector", lambda e, m=m, p=p, s=s: e.scalar_tensor_tensor(
                xt[s][:, m, :], y_ps[p][:], 0.5, xt[s][:, m, :], ALU.mult, ALU.add),
                reads=[b_y[p], b_x[s]], writes=[b_x[s]])
        st.dma("sync", xTv[:, :, i * 512:(i + 1) * 512], xt[s][:], d_o[s], reads=[b_x[s]])

    pro_a(0)
    for q in range(4):
        for off in (0, DFF):
            ca, cb_ = off + JB[q] * 128, off + JB[q + 1] * 128
            st.dma("gpsimd", w_in[:, :, ca:cb_], w_in_v[:, :, ca:cb_], d_wq[q], writes=[b_win[q]])
    for hh in range(2):
        st.dma("gpsimd", w_out[:, hh * 11:(hh + 1) * 11, :], w_out_v[:, hh * 11:(hh + 1) * 11, :], d_w2, writes=[b_wout[hh]])

    pro_b(0)
    for i in range(NT):
        up(i)
        if i + 1 < NT:
            pro_a(i + 1)
            pro_b(i + 1)
        down(i)
        if i + 2 < NT:
            load_x(i + 2)
    st.finish()


def stage_final(nc, xT_d, out_d, vecs_d, nv, gcol, ident_d, ntok):
    st = Stage(nc, "fin")
    NT = ntok // 512
    vecs = st.sb("vecs", [128, nv], F32)
    b_vecs = Buf()
    d_v = st.dsem()
    st.dma("sync", vecs[:], vecs_d, d_v, writes=[b_vecs])
    ident = st.sb("ident", [128, 128], F32)
    b_id = Buf()
    d_id = st.dsem()
    st.dma("sync", ident[:], ident_d, d_id, writes=[b_id])
    ones = st.sb("ones", [128, 128], BF16)
    b_ones = Buf()
    st.op("vector", lambda e: e.memset(ones[:], 1.0), writes=[b_ones], memset=True)
    xt = [st.sb(f"xt{i}", [128, 8, 512], F32) for i in range(2)]
    b_x = [Buf(), Buf()]
    d_x = [st.dsem(), st.dsem()]
    d_o = [st.dsem(), st.dsem()]
    sq = [st.sb(f"sq{i}", [128, 2, 512], BF16) for i in range(2)]
    b_sq = [Buf(), Buf()]
    rs = st.sb("rs", [128, 512], F32)
    b_rs = Buf()
    ot = [st.sb(f"ot{i}", [128, 4, D], F32) for i in range(2)]
    b_ot = [Buf(), Buf()]
    ss_ps = st.ps()
    b_ss = Buf()
    pss = [st.ps() for _ in range(6)]
    b_ps = [Buf() for _ in range(6)]
    xTv = xT_d.rearrange("(c p) t -> p c t", p=128)
    ov = out_d.rearrange("(n s p) f -> n p s f", p=128, s=4)
    def load(i):
        s = i % 2
        st.dma("sync", xt[s][:], xTv[:, :, i * 512:(i + 1) * 512], d_x[s], writes=[b_x[s]])

    def pro(i):
        s = i % 2
        emit_norm_stats(st, xt[s], b_x[s], sq, b_sq, ones, b_ones, ss_ps, b_ss, rs, b_rs)
        for c in range(8):
            st.op("vector", lambda e, c=c, s=s: e.scalar_tensor_tensor(
                xt[s][:, c, :], xt[s][:, c, :], vecs[:, gcol + c:gcol + c + 1], rs[:], ALU.mult, ALU.mult),
                reads=[b_x[s], b_rs, b_vecs], writes=[b_x[s]])

    kk = 0
    load(0)
    if NT > 1:
        load(1)
    pro(0)
    for i in range(NT):
        s = i % 2
        if i + 1 < NT:
            pro(i + 1)
        for sb_ in range(4):
            for hf in range(2):
                p = kk % 6
                kk += 1

                def mm(e, sb_=sb_, hf=hf, p=p, s=s):
                    ins = None
                    for c4 in range(4):
                        c = hf * 4 + c4
                        ins = e.transpose(pss[p][:, c4 * 128:(c4 + 1) * 128], xt[s][:, c, sb_ * 128:(sb_ + 1) * 128], ident[:])
                    return ins
                st.op("tensor", mm, reads=[b_x[s], b_id], writes=[b_ps[p]])
                if kk % 2 == 0:
                    st.op("gpsimd" if False else "vector", lambda e, sb_=sb_, hf=hf, p=p, s=s: e.tensor_copy(ot[s][:, sb_, hf * 512:(hf + 1) * 512], pss[p][:]),
                          reads=[b_ps[p]], writes=[b_ot[s]])
                else:
                    st.op("scalar", lambda e, sb_=sb_, hf=hf, p=p, s=s: e.activation(ot[s][:, sb_, hf * 512:(hf + 1) * 512], pss[p][:], AF.Copy),
                          reads=[b_ps[p]], writes=[b_ot[s]])
        st.dma("sync", ov[i], ot[s][:], d_o[s], reads=[b_ot[s]])
        if i + 2 < NT:
            load(i + 2)
    st.finish()


NQX = 2048
NKX = 512
NVX = 256
HEAD_A = [0, 1, 2, 3, 8, 9, 10, 11]
HEAD_B = [4, 5, 6, 7, 12, 13, 14, 15]


def emit_headnorm_rope(st, src_ps, b_src, bones, b_bones, ssq_ps, b_ssq, sqh, b_sqh, rsh, b_rsh, t1, b_t1, t2, b_t2,
                       vecs, b_vecs, gc, gsc, rc, rsn, b_rope, outs):
    st.op("scalar", lambda e: e.activation(sqh[:], src_ps[:, 0:512], AF.Square), reads=[b_src], writes=[b_sqh])
    st.op("tensor", lambda e: e.matmul(ssq_ps[:], bones[:], sqh[:], start=True, stop=True), reads=[b_sqh, b_bones], writes=[b_ssq])
    st.op("scalar", lambda e: e.activation(rsh[:], ssq_ps[:], AF.Ln, bias=EPS, scale=1.0 / 64), reads=[b_ssq], writes=[b_rsh])
    st.op("scalar", lambda e: e.activation(rsh[:], rsh[:], AF.Exp, scale=-0.5), reads=[b_rsh], writes=[b_rsh])
    st.op("vector", lambda e: e.scalar_tensor_tensor(t1[:], src_ps[:, 0:512], vecs[:, gc:gc + 1], rsh[:], ALU.mult, ALU.mult),
          reads=[b_src, b_rsh, b_vecs], writes=[b_t1])
    st.op("vector", lambda e: e.scalar_tensor_tensor(t2[:], src_ps[:, 512:1024], vecs[:, gsc:gsc + 1], rsh[:], ALU.mult, ALU.mult),
          reads=[b_src, b_rsh, b_vecs], writes=[b_t2])
    st.op("gpsimd", lambda e: e.tensor_tensor(t1[:], t1[:], rc, ALU.mult), reads=[b_t1, b_rope], writes=[b_t1])
    st.op("vector", lambda e: e.tensor_tensor(t2[:], t2[:], rsn, ALU.mult), reads=[b_t2, b_rope], writes=[b_t2])
    for out_ap, rows, b_out in outs:
        st.op("gpsimd", lambda e, out_ap=out_ap, rows=rows: e.tensor_tensor(out_ap, t1[rows, :], t2[rows, :], ALU.add),
              reads=[b_t1, b_t2], writes=[b_out])


def make_bones(st):
    bones = st.sb("bones", [128, 128], BF16)
    b = Buf()
    st.op("vector", lambda e: e.memset(bones[:], 0.0), writes=[b], memset=True)
    st.op("vector", lambda e: e.memset(bones[0:64, 0:64], 1.0), writes=[b], memset=True)
    st.op("vector", lambda e: e.memset(bones[64:128, 64:128], 1.0), writes=[b], memset=True)
    return bones, b


def stage_attn_kv(nc, name, xT_d, wk_d, wv_d, vecs_d, nv, gcol, kgc, kgsc, ropec_d, ropes_d, KT_d, VA_d, nseq):
    st = Stage(nc, name)
    vecs = st.sb("vecs", [128, nv], F32)
    b_vecs = Buf()
    d_v = st.dsem()
    st.dma("sync", vecs[:], vecs_d, d_v, writes=[b_vecs])
    wk = st.sb("wk", [128, 8, NKX], BF16)
    wv = st.sb("wv", [128, 8, NVX], BF16)
    b_wk, b_wv = Buf(), Buf()
    d_w = st.dsem()
    st.dma("gpsimd", wk[:], wk_d.rearrange("(c p) n -> p c n", p=128), d_w, writes=[b_wk])
    d_w2 = st.dsem()
    st.dma("gpsimd", wv[:], wv_d.rearrange("(c p) n -> p c n", p=128), d_w2, writes=[b_wv])
    ones = st.sb("ones", [128, 128], BF16)
    b_ones = Buf()
    st.op("vector", lambda e: e.memset(ones[:], 1.0), writes=[b_ones], memset=True)
    bones, b_bones = make_bones(st)
    xt = [st.sb(f"xt{i}", [128, 8, 512], F32) for i in range(2)]
    b_x = [Buf(), Buf()]
    d_x = [st.dsem(), st.dsem()]
    rope = [st.sb(f"rope{i}", [128, 2, 512], F32) for i in range(2)]
    b_rope = [Buf(), Buf()]
    d_r = [st.dsem(), st.dsem()]
    hTs = [st.sb(f"hT{i}", [128, 8, 512], BF16) for i in range(2)]
    b_hs = [Buf(), Buf()]
    sq = [st.sb(f"sq{i}", [128, 2, 512], BF16) for i in range(2)]
    b_sq = [Buf(), Buf()]
    rs = st.sb("rs", [128, 512], F32)
    b_rs = Buf()
    sqh = st.sb("sqh", [128, 512], BF16)
    b_sqh = Buf()
    rsh = st.sb("rsh", [128, 512], F32)
    b_rsh = Buf()
    t1 = st.sb("t1", [128, 512], F32)
    t2 = st.sb("t2", [128, 512], F32)
    b_t1, b_t2 = Buf(), Buf()
    kst = [st.sb(f"kst{i}", [128, 2, 512], BF16) for i in range(2)]
    b_kst = [Buf(), Buf()]
    d_k = [st.dsem(), st.dsem()]
    vst = [st.sb(f"vst{i}", [128, 4, 4, 128], BF16) for i in range(2)]
    b_vst = [Buf(), Buf()]
    d_vs = [st.dsem(), st.dsem()]
    for i in range(2):
        st.op("vector", lambda e, i=i: e.memset(vst[i][:], 1.0), writes=[b_vst[i]], memset=True)
    ss_ps = st.ps()
    b_ss = Buf()
    ssq_ps = st.ps()
    b_ssq = Buf()
    kps = [st.ps([128, 1024]) for _ in range(2)]
    b_kps = [Buf(), Buf()]
    v_ps = [st.ps(), st.ps()]
    b_vps = [Buf(), Buf()]
    xTv = xT_d.rearrange("(c p) t -> p c t", p=128)
    NT = nseq * 8
    KTv = KT_d.rearrange("s g p t -> s p g t")

    def load(i):
        s = i % 2
        st.dma("sync", xt[s][:], xTv[:, :, i * 512:(i + 1) * 512], d_x[s], writes=[b_x[s]])
        tl = (i % 8) * 512
        st.dma("sync", rope[s][:, 0, :], ropec_d[:, tl:tl + 512], d_r[s], writes=[b_rope[s]])
        st.dma("sync", rope[s][:, 1, :], ropes_d[:, tl:tl + 512], d_r[s], writes=[b_rope[s]])

    def pro(i):
        s = i % 2
        emit_norm_stats(st, xt[s], b_x[s], sq, b_sq, ones, b_ones, ss_ps, b_ss, rs, b_rs, lnexp=True)
        for c in range(8):
            st.op("vector", lambda e, c=c, s=s: e.scalar_tensor_tensor(
                hTs[s][:, c, :], xt[s][:, c, :], vecs[:, gcol + c:gcol + c + 1], rs[:], ALU.mult, ALU.mult),
                reads=[b_x[s], b_rs, b_vecs], writes=[b_hs[s]])

    def kv(i):
        s = i % 2
        hT, b_h = hTs[s], b_hs[s]
        for kc in range(2):
            p = kc % 2

            def mm(e, kc=kc, p=p):
                ins = None
                for sw in range(2):
                    col = sw * 256 + kc * 128
                    for c in range(8):
                        ins = e.matmul(kps[p][:, sw * 512:(sw + 1) * 512], wk[:, c, col:col + 128], hT[:, c, :], start=(c == 0), stop=(c == 7))
                return ins
            st.op("tensor", mm, reads=[b_h, b_wk], writes=[b_kps[p]])
            emit_headnorm_rope(st, kps[p], b_kps[p], bones, b_bones, ssq_ps, b_ssq, sqh, b_sqh, rsh, b_rsh, t1, b_t1, t2, b_t2,
                               vecs, b_vecs, kgc, kgsc, rope[s][:, 0, :], rope[s][:, 1, :], b_rope[s],
                               [(kst[s][:, kc, :], slice(0, 128), b_kst[s])])
        seq, tl = i // 8, (i % 8) * 512
        st.dma("sync", KTv[seq][:, :, tl:tl + 512], kst[s][:], d_k[s], reads=[b_kst[s]])
        for sb_ in range(4):
            p = sb_ % 2

            def mmv(e, sb_=sb_, p=p):
                ins = None
                for c in range(8):
                    ins = e.matmul(v_ps[p][:, 0:256], hT[:, c, sb_ * 128:(sb_ + 1) * 128], wv[:, c, :], start=(c == 0), stop=(c == 7))
                return ins
            st.op("tensor", mmv, reads=[b_h, b_wv], writes=[b_vps[p]])
            st.op("vector", lambda e, sb_=sb_, p=p, s=s: e.tensor_copy(
                vst[s][:, sb_, :, 0:64], v_ps[p][:, 0:256].rearrange("p (g d) -> p g d", d=64)),
                reads=[b_vps[p]], writes=[b_vst[s]])
        st.dma("sync", VA_d[seq][:, (i % 8) * 2048:(i % 8 + 1) * 2048], vst[s][:].rearrange("p a g d -> p (a g d)"), d_vs[s], reads=[b_vst[s]])

    load(0)
    if NT > 1:
        load(1)
    pro(0)
    for i in range(NT):
        if i + 1 < NT:
            pro(i + 1)
        kv(i)
        if i + 2 < NT:
            load(i + 2)
    st.finish()


def stage_attn_q(nc, name, xT_d, tok0, wq_d, wo_d, vecs_d, nv, gcol, qgc, qgsc, ropec_d, ropes_d, KT_d, VA_d):
    st = Stage(nc, name)
    vecs = st.sb("vecs", [128, nv], F32)
    b_vecs = Buf()
    d_v = st.dsem()
    st.dma("sync", vecs[:], vecs_d, d_v, writes=[b_vecs])
    wq = st.sb("wq", [128, 8, NQX], BF16)
    wo = st.sb("wo", [128, 8, D], BF16)
    b_wq, b_wo = Buf(), Buf()
    d_w = st.dsem()
    st.dma("gpsimd", wq[:], wq_d.rearrange("(c p) n -> p c n", p=128), d_w, writes=[b_wq])
    d_w2 = st.dsem()
    st.dma("gpsimd", wo[:], wo_d.rearrange("(c p) n -> p c n", p=128), d_w2, writes=[b_wo])
    kT2 = st.sb("kT2", [128, 2, L], BF16)
    b_k = Buf()
    va = st.sb("va", [128, 32, 4, 128], BF16)
    b_va = Buf()
    d_kv = st.dsem()
    st.dma("sync", kT2[:], KT_d.rearrange("g p t -> p g t"), d_kv, writes=[b_k])
    d_kv2 = st.dsem()
    st.dma("sync", va[:].rearrange("p a g d -> p (a g d)"), VA_d, d_kv2, writes=[b_va])
    ones = st.sb("ones", [128, 128], BF16)
    b_ones = Buf()
    st.op("vector", lambda e: e.memset(ones[:], 1.0), writes=[b_ones], memset=True)
    bones, b_bones = make_bones(st)
    xts = [st.sb(f"xt{i}", [128, 8, 512], F32) for i in range(2)]
    b_xs = [Buf(), Buf()]
    d_xs = [st.dsem(), st.dsem()]
    d_os = [st.dsem(), st.dsem()]
    ropes = [st.sb(f"rope{i}", [128, 2, 512], F32) for i in range(2)]
    b_ropes = [Buf(), Buf()]
    d_rs = [st.dsem(), st.dsem()]
    hT = st.sb("hT", [128, 8, 512], BF16)
    b_h = Buf()
    qT = st.sb("qT", [128, 16, 512], BF16)
    b_q = [Buf() for _ in range(8)]
    st.op("gpsimd", lambda e: e.memset(qT[:], 0.0), writes=b_q, memset=True)
    oT = st.sb("oT", [128, 8, 512], BF16)
    b_o = [Buf() for _ in range(8)]
    sq = [st.sb(f"sq{i}", [128, 2, 512], BF16) for i in range(2)]
    b_sq = [Buf(), Buf()]
    rs = st.sb("rs", [128, 512], F32)
    b_rs = Buf()
    sqh = st.sb("sqh", [128, 512], BF16)
    b_sqh = Buf()
    rsh = st.sb("rsh", [128, 512], F32)
    b_rsh = Buf()
    t1 = st.sb("t1", [128, 512], F32)
    t2 = st.sb("t2", [128, 512], F32)
    b_t1, b_t2 = Buf(), Buf()
    NPT = 4
    pT = [st.sb(f"pT{i}", [128, 1024], BF16) for i in range(NPT)]
    b_p = [Buf() for _ in range(NPT)]
    rcp = [st.sb(f"rcp{i}", [128, 512], F32) for i in range(2)]
    b_rcp = [Buf(), Buf()]
    s_ps = [st.ps([128, 1024]) for _ in range(3)]
    b_s = [Buf(), Buf(), Buf()]
    o_ps = [st.ps(), st.ps()]
    b_ops = [Buf(), Buf()]
    m_ps = [s_ps[2][:, 0:512], s_ps[2][:, 512:1024]]
    b_m = [b_s[2], b_s[2]]
    xTv = xT_d.rearrange("(c p) t -> p c t", p=128)

    def load(i):
        sl_ = i % 2
        ta = tok0 + i * 512
        st.dma("sync", xts[sl_][:], xTv[:, :, ta:ta + 512], d_xs[sl_], writes=[b_xs[sl_]])
        st.dma("sync", ropes[sl_][:, 0, :], ropec_d[:, i * 512:(i + 1) * 512], d_rs[sl_], writes=[b_ropes[sl_]])
        st.dma("sync", ropes[sl_][:, 1, :], ropes_d[:, i * 512:(i + 1) * 512], d_rs[sl_], writes=[b_ropes[sl_]])

    def pro_stats(i):
        xt, b_x = xts[i % 2], b_xs[i % 2]
        emit_norm_stats(st, xt, b_x, sq, b_sq, ones, b_ones, m_ps[1], b_m[1], rs, b_rs, lnexp=True)
        for c in range(8):
            st.op("vector", lambda e, c=c, xt=xt: e.scalar_tensor_tensor(
                hT[:, c, :], xt[:, c, :], vecs[:, gcol + c:gcol + c + 1], rs[:], ALU.mult, ALU.mult),
                reads=[b_x, b_rs, b_vecs], writes=[b_h])

    def pro_q(i, c):
        rope, b_rope = ropes[i % 2], b_ropes[i % 2]
        p = c % 2

        def mm(e):
            ins = None
            for sw in range(2):
                col = sw * 1024 + c * 128
                for cc in range(8):
                    ins = e.matmul(s_ps[p][:, sw * 512:(sw + 1) * 512], wq[:, cc, col:col + 128], hT[:, cc, :], start=(cc == 0), stop=(cc == 7))
            return ins
        st.op("tensor", mm, reads=[b_h, b_wq], writes=[b_s[p]])
        emit_headnorm_rope(st, s_ps[p], b_s[p], bones, b_bones, m_ps[0], b_m[0], sqh, b_sqh, rsh, b_rsh, t1, b_t1, t2, b_t2,
                           vecs, b_vecs, qgc, qgsc, rope[:, 0, :], rope[:, 1, :], b_rope,
                           [(qT[0:64, 2 * c, :], slice(0, 64), b_q[c]), (qT[64:128, 2 * c + 1, :], slice(64, 128), b_q[c])])

    def outproj(i, m):
        xt, b_x = xts[i % 2], b_xs[i % 2]
        p = m % 2

        def mmy(e):
            ins = None
            for c in range(8):
                ins = e.matmul(o_ps[p][:], wo[:, c, m * 128:(m + 1) * 128], oT[:, c, :], start=(c == 0), stop=(c == 7))
            return ins
        st.op("tensor", mmy, reads=b_o + [b_wo], writes=[b_ops[p]])
        st.op("vector", lambda e: e.tensor_tensor(xt[:, m, :], o_ps[p][:], xt[:, m, :], ALU.add),
              reads=[b_ops[p], b_x], writes=[b_x])

    NP = 16
    seqn = [(h, kp) for h in range(16) for kp in range(NP)]
    NN = len(seqn)

    def S(n):
        hh, kp = seqn[n]
        c, half = hh // 2, hh % 2
        g = (HEAD_A[c] if half == 0 else HEAD_B[c]) // 4
        sl = n % 3

        def mm(e):
            ins = None
            for k2 in range(2):
                kc = 2 * kp + k2
                ins = e.matmul(s_ps[sl][:, k2 * 512:(k2 + 1) * 512], kT2[:, g // 2, kc * 128:(kc + 1) * 128], qT[:, hh, :], start=True, stop=True)
            return ins
        st.op("tensor", mm, reads=[b_k, b_q[c]], writes=[b_s[sl]])
        st.op("scalar", lambda e: e.activation(pT[n % NPT][:], s_ps[sl][:], AF.Exp, scale=0.125), reads=[b_s[sl]], writes=[b_p[n % NPT]])

    def PV(n):
        h, kp = seqn[n]
        g = (HEAD_A[h // 2] if h % 2 == 0 else HEAD_B[h // 2]) // 4
        os_ = h % 2

        def mm(e):
            ins = None
            for k2 in range(2):
                kc = 2 * kp + k2
                ins = e.matmul(o_ps[os_][:], va[:, kc, g, :], pT[n % NPT][:, k2 * 512:(k2 + 1) * 512],
                               start=(kp == 0 and k2 == 0), stop=(kp == NP - 1 and k2 == 1))
            return ins
        st.op("tensor", mm, reads=[b_va, b_p[n % NPT]], writes=[b_ops[os_]])
        if kp == NP - 1:
            c, half = h // 2, h % 2
            rows = slice(64 * half, 64 * half + 64)
            st.op("vector", lambda e: e.reciprocal(rcp[os_][64:128, :], o_ps[os_][64:128, :]), reads=[b_ops[os_]], writes=[b_rcp[os_]])
            st.op("vector", lambda e: e.tensor_tensor(oT[rows, c, :], o_ps[os_][0:64, :], rcp[os_][64:128, :], ALU.mult),
                  reads=[b_ops[os_], b_rcp[os_]], writes=[b_o[c]])

    load(0)
    pro_stats(0)
    for c in range(8):
        pro_q(0, c)
    for i in range(8):
        t0 = tok0 + i * 512
        if i + 1 < 8:
            load(i + 1)
        S(0)
        S(1)
        S(2)
        for n in range(NN):
            PV(n)
            if n + 3 < NN:
                S(n + 3)
            if n + 3 == NN - 1 and i + 1 < 8:
                pro_stats(i + 1)
        for k in range(8):
            if i + 1 < 8:
                pro_q(i + 1, k)
            outproj(i, k)
        st.dma("sync", xTv[:, :, t0:t0 + 512], xts[i % 2][:], d_os[i % 2], reads=[b_xs[i % 2]])
    st.finish()


def _pair_swap_perm(n):
    p = np.arange(n)
    return p ^ 1


def q_head_perm():
    cols = []
    for c in range(8):
        for h in (HEAD_A[c], HEAD_B[c]):
            cols.append(np.arange(h * 64, (h + 1) * 64))
    return np.concatenate(cols)


def prep_attn_weights(w_in):
    q = w_in[:, :1024][:, q_head_perm()]
    k = w_in[:, 1024:1280]
    v = w_in[:, 1280:1536]
    wq = np.concatenate([q, q[:, _pair_swap_perm(1024)]], axis=1)
    wk = np.concatenate([k, k[:, _pair_swap_perm(256)]], axis=1)
    return np.ascontiguousarray(wq), np.ascontiguousarray(wk), np.ascontiguousarray(v)


def rope_tables():
    t = np.arange(L)
    row = (t // 64).astype(np.float32)
    col = (t % 64).astype(np.float32)
    inv = (10000.0 ** (-np.arange(0, 32, 2, dtype=np.float32) / 32)).astype(np.float32)
    ang = np.concatenate([row[:, None] * inv, col[:, None] * inv], axis=-1).astype(np.float32)
    c, s = np.cos(ang), np.sin(ang)
    p = np.arange(128)
    pi = (p % 64) // 2
    sign = np.where(p % 2 == 0, -1.0, 1.0).astype(np.float32)
    rc = np.ascontiguousarray(c[:, pi].T.astype(np.float32))
    rsn = np.ascontiguousarray((s[:, pi] * sign[None, :]).T.astype(np.float32))
    return rc, rsn


NFC = 17
NFP = NFC * 128
NTJ = 32


def _chunk_t():
    j = np.arange(NTJ)
    par, jj = j // 16, j % 16
    p = np.arange(128)
    return (2 * (128 * jj[:, None] + p[None, :]) + par[:, None])


def dft_tables():
    N = 2 * L
    k = np.arange(N)
    ct = np.cos(2 * np.pi * k / N)
    sn = np.sin(2 * np.pi * k / N)
    f = np.arange(NFP)
    tt = _chunk_t().reshape(-1)
    ft = (f[:, None] * tt[None, :]) % N
    C = ct[ft].astype(np.float32)
    S = sn[ft].astype(np.float32)
    wf = np.full(NFP, 2.0, np.float32)
    wf[0] = 1.0
    wf[2049:] = 0.0
    bf = ml_dtypes.bfloat16
    FC = np.ascontiguousarray(C.reshape(NFC, 128, NTJ, 128).transpose(0, 3, 2, 1)).astype(bf)
    FS = np.ascontiguousarray(S.reshape(NFC, 128, NTJ, 128).transpose(0, 3, 2, 1)).astype(bf)
    Gc = (C * (wf / N)[:, None]).reshape(NFC, 128, NTJ, 128)
    Gs = (-S * (wf / N)[:, None]).reshape(NFC, 128, NTJ, 128)
    GC = np.ascontiguousarray(Gc.transpose(2, 1, 0, 3)).astype(bf)
    GS = np.ascontiguousarray(Gs.transpose(2, 1, 0, 3)).astype(bf)
    return FC, FS, GC, GS


def hyena_pos_tables():
    t = np.linspace(0.0, 1.0, L, dtype=np.float32)[:, None]
    bands = 16
    fr = np.linspace(1e-4, bands - 1, bands, dtype=np.float32)
    w = (2.0 * math.pi * np.arange(L, dtype=np.float32)[:, None] / L).astype(np.float32)
    z = np.concatenate([t, np.cos(fr * w), -np.sin(fr * w)], axis=-1).astype(np.float32)
    zT = np.ascontiguousarray(z.T)
    trow = np.ascontiguousarray(np.broadcast_to(t[:, 0][None, :], (128, L))).astype(np.float32)
    return zT, trow


def emit_tm_store(st, src16, b_src, identb, b_idb, tp_ps, b_tp, stg, b_stg, d_stg, dst_d, cc, kctr):
    sl = kctr[0] % 2
    kctr[0] += 1
    for q4 in range(4):
        p = q4 % 2

        def mm(e, q4=q4, p=p):
            ins = None
            for k in range(8):
                j = q4 * 8 + k
                par, jj = j // 16, j % 16
                ins = e.transpose(tp_ps[p][:, k * 128:(k + 1) * 128], src16[:, 256 * jj + par:256 * (jj + 1):2], identb[:])
            return ins
        st.op("tensor", mm, reads=[b_src, b_idb], writes=[b_tp[p]])
        if q4 % 2 == 0:
            st.op("vector", lambda e, q4=q4, p=p: e.tensor_copy(stg[sl][:, q4 * 8:(q4 + 1) * 8, :], tp_ps[p][:].rearrange("p (k c) -> p k c", c=128)),
                  reads=[b_tp[p]], writes=[b_stg[sl]])
        else:
            st.op("scalar", lambda e, q4=q4, p=p: e.activation(stg[sl][:, q4 * 8:(q4 + 1) * 8, :], tp_ps[p][:].rearrange("p (k c) -> p k c", c=128), AF.Copy),
                  reads=[b_tp[p]], writes=[b_stg[sl]])
    st.dma("sync", dst_d.rearrange("(tc p) c -> p tc c", p=128)[:, :, cc * 128:(cc + 1) * 128], stg[sl][:], d_stg[sl], reads=[b_stg[sl]])


def stage_hy_filter(nc, name, zT_d, trow_d, w1_d, w2_d, w3_d, wo_d, vecs_d, nv, vc, identb_d, ATM_d, BTM_d):
    st = Stage(nc, name)
    vecs = st.sb("vecs", [128, nv], F32)
    b_vecs = Buf()
    st.dma("sync", vecs[:], vecs_d, st.dsem(), writes=[b_vecs])
    identb = st.sb("identb", [128, 128], BF16)
    b_idb = Buf()
    st.dma("gpsimd", identb[:], identb_d, st.dsem(), writes=[b_idb])
    zT = st.sb("zT", [33, L], F32)
    b_z = Buf()
    st.dma("sync", zT[:], zT_d, st.dsem(), writes=[b_z])
    trow = st.sb("trow", [128, L], F32)
    b_tr = Buf()
    st.dma("sync", trow[:], trow_d, st.dsem(), writes=[b_tr])
    w1 = st.sb("w1", [33, 64], F32)
    w2 = st.sb("w2", [64, 64], F32)
    w3 = st.sb("w3", [64, 64], F32)
    wo = st.sb("wo", [64, 2048], F32)
    b_w = Buf()
    d_w = st.dsem()
    st.dma("sync", w1[:], w1_d, d_w, writes=[b_w])
    st.dma("sync", w2[:], w2_d, d_w, writes=[b_w])
    st.dma("sync", w3[:], w3_d, d_w, writes=[b_w])
    st.dma("sync", wo[:], wo_d, d_w, writes=[b_w])
    hid = [st.sb(f"hid{i}", [64, L], F32) for i in range(2)]
    b_hid = [Buf(), Buf()]
    tmp = [st.sb(f"tmp{i}", [128, 512], F32) for i in range(2)]
    b_tmp = [Buf(), Buf()]
    tq_ = [st.sb(f"tq{i}", [128, 512], F32) for i in range(2)]
    b_tq = [Buf(), Buf()]
    sm = st.sb("sm", [128, 32], F32)
    b_sm = Buf()
    ps = [st.ps(), st.ps()]
    b_ps = [Buf(), Buf()]
    tp_ps = [st.ps([128, 1024], BF16), st.ps([128, 1024], BF16)]
    b_tp = [Buf(), Buf()]
    fq = vc["freq"]
    for k, key in enumerate(("b1", "b2", "b3")):
        st.op("vector", lambda e, k=k, key=key: e.tensor_tensor(sm[0:64, k:k + 1], vecs[0:64, vc[key]:vc[key] + 1], vecs[0:64, fq:fq + 1], ALU.mult),
              reads=[b_vecs], writes=[b_sm])
    dc = vc["decay"]
    st.op("scalar", lambda e: e.activation(sm[:, 8:24], vecs[:, dc:dc + 16], AF.Abs), reads=[b_vecs], writes=[b_sm])
    st.op("vector", lambda e: e.tensor_scalar(sm[:, 8:24], sm[:, 8:24], -1.0, None, ALU.mult), reads=[b_sm], writes=[b_sm])
    kk = 0
    srcs = [(zT, b_z, 33), None, None]
    ws = [w1, w2, w3]
    for k in range(3):
        if k == 0:
            src, b_src, K = zT, b_z, 33
        else:
            src, b_src, K = hid[(k - 1) % 2], b_hid[(k - 1) % 2], 64
        dst, b_dst = hid[k % 2], b_hid[k % 2]
        for tl in range(8):
            p = kk % 2
            kk += 1
            st.op("tensor", lambda e, p=p, k=k, K=K, src=src, tl=tl: e.matmul(ps[p][0:64, :], ws[k][0:K, :], src[0:K, tl * 512:(tl + 1) * 512], start=True, stop=True),
                  reads=[b_src, b_w], writes=[b_ps[p]])
            st.op("vector", lambda e, p=p, k=k: e.tensor_scalar(tmp[p][0:64, :], ps[p][0:64, :], vecs[0:64, fq:fq + 1], sm[0:64, k:k + 1], ALU.mult, ALU.add),
                  reads=[b_ps[p], b_vecs, b_sm], writes=[b_tmp[p]])
            st.op("scalar", lambda e, p=p: e.activation(tmp[p][0:64, :], tmp[p][0:64, :], AF.Sin, scale=1.0 / 9.0),
                  reads=[b_tmp[p]], writes=[b_tmp[p]])
            for rep in range(2):
                st.op("vector", lambda e, p=p: e.tensor_tensor(tq_[p][0:64, :], tmp[p][0:64, :], tmp[p][0:64, :], ALU.mult),
                      reads=[b_tmp[p]], writes=[b_tq[p]])
                st.op("vector", lambda e, p=p: e.tensor_scalar(tq_[p][0:64, :], tq_[p][0:64, :], -4.0, 3.0, ALU.mult, ALU.add),
                      reads=[b_tq[p]], writes=[b_tq[p]])
                if rep == 0:
                    st.op("vector", lambda e, p=p: e.tensor_tensor(tmp[p][0:64, :], tmp[p][0:64, :], tq_[p][0:64, :], ALU.mult),
                          reads=[b_tmp[p], b_tq[p]], writes=[b_tmp[p]])
                else:
                    st.op("vector", lambda e, p=p, dst=dst, tl=tl: e.tensor_tensor(dst[:, tl * 512:(tl + 1) * 512], tmp[p][0:64, :], tq_[p][0:64, :], ALU.mult),
                          reads=[b_tmp[p], b_tq[p]], writes=[b_dst])
    hid3, b_h3 = hid[2 % 2], b_hid[2 % 2]
    hf = st.sb("hf", [128, L], F32)
    hb = st.sb("hb", [128, L], F32)
    b_hf, b_hb = Buf(), Buf()
    sc = st.sb("sc", [128, L], F32)
    b_sc = Buf()
    a16 = st.sb("a16", [128, L], BF16)
    b16 = st.sb("b16", [128, L], BF16)
    b_a16, b_b16 = Buf(), Buf()
    stg = [st.sb(f"stg{i}", [128, 32, 128], BF16) for i in range(2)]
    b_stg = [Buf(), Buf()]
    d_stg = [st.dsem(), st.dsem()]
    kctr = [0]
    for cc in range(8):
        for dr, (hh, b_hh) in enumerate(((hf, b_hf), (hb, b_hb))):
            chunk = dr * 8 + cc
            for tl in range(8):
                p = kk % 2
                kk += 1
                st.op("tensor", lambda e, p=p, chunk=chunk, tl=tl: e.matmul(ps[p][:], wo[:, chunk * 128:(chunk + 1) * 128], hid3[:, tl * 512:(tl + 1) * 512], start=True, stop=True),
                      reads=[b_h3, b_w], writes=[b_ps[p]])
                st.op("scalar", lambda e, p=p, chunk=chunk, tl=tl: e.activation(tmp[p][:], trow[:, tl * 512:(tl + 1) * 512], AF.Exp, scale=sm[:, 8 + chunk:9 + chunk]),
                      reads=[b_tr, b_sm], writes=[b_tmp[p]])
                st.op("vector", lambda e, p=p, hh=hh, tl=tl: e.tensor_tensor(hh[:, tl * 512:(tl + 1) * 512], ps[p][:], tmp[p][:], ALU.mult),
                      reads=[b_ps[p], b_tmp[p]], writes=[b_hh])
        st.op("vector", lambda e: e.memset(hb[:, 0:1], 0.0), writes=[b_hb], memset=True)
        for hh_ in range(2):
            st.op("scalar", lambda e, hh_=hh_: e.activation(sc[:, hh_ * 2048:(hh_ + 1) * 2048], hf[:, hh_ * 2048:(hh_ + 1) * 2048], AF.Abs), reads=[b_hf], writes=[b_sc])
        st.op("vector", lambda e: e.reduce_sum(sm[:, 24:25], sc[:], AX.X), reads=[b_sc], writes=[b_sm])
        for hh_ in range(2):
            st.op("scalar", lambda e, hh_=hh_: e.activation(sc[:, hh_ * 2048:(hh_ + 1) * 2048], hb[:, hh_ * 2048:(hh_ + 1) * 2048], AF.Abs), reads=[b_hb], writes=[b_sc])
        st.op("vector", lambda e: e.reduce_sum(sm[:, 25:26], sc[:], AX.X), reads=[b_sc], writes=[b_sm])
        st.op("vector", lambda e: e.tensor_tensor(sm[:, 26:27], sm[:, 24:25], sm[:, 25:26], ALU.add), reads=[b_sm], writes=[b_sm])
        st.op("vector", lambda e: e.reciprocal(sm[:, 27:28], sm[:, 26:27]), reads=[b_sm], writes=[b_sm])
        st.op("vector", lambda e: e.tensor_tensor(sc[:], hf[:], hb[:], ALU.add), reads=[b_hf, b_hb], writes=[b_sc])
        st.op("vector", lambda e: e.tensor_scalar(a16[:], sc[:], sm[:, 27:28], None, ALU.mult), reads=[b_sc, b_sm], writes=[b_a16])
        st.op("vector", lambda e: e.tensor_tensor(sc[:], hb[:], hf[:], ALU.subtract), reads=[b_hf, b_hb, b_a16], writes=[b_sc])
        st.op("vector", lambda e: e.tensor_scalar(b16[:], sc[:], sm[:, 27:28], None, ALU.mult), reads=[b_sc, b_sm], writes=[b_b16])
        emit_tm_store(st, a16, b_a16, identb, b_idb, tp_ps, b_tp, stg, b_stg, d_stg, ATM_d, cc, kctr)
        emit_tm_store(st, b16, b_b16, identb, b_idb, tp_ps, b_tp, stg, b_stg, d_stg, BTM_d, cc, kctr)
    st.finish()


def stage_hy_kdft(nc, name, ATM_d, BTM_d, FC_d, FS_d, KH_d):
    st = Stage(nc, name)
    a_tm = st.sb("a_tm", [128, NTJ, D], BF16)
    b_tm = st.sb("b_tm", [128, NTJ, D], BF16)
    b_a, b_b = [Buf(), Buf()], [Buf(), Buf()]
    ATv = ATM_d.rearrange("(tc p) c -> p tc c", p=128)
    BTv = BTM_d.rearrange("(tc p) c -> p tc c", p=128)
    for h_ in range(2):
        st.dma("sync", a_tm[:, 16 * h_:16 * h_ + 16], ATv[:, 16 * h_:16 * h_ + 16], st.dsem(), writes=[b_a[h_]])
        st.dma("scalar", b_tm[:, 16 * h_:16 * h_ + 16], BTv[:, 16 * h_:16 * h_ + 16], st.dsem(), writes=[b_b[h_]])
    tb = [st.sb(f"tb{i}", [128, 2, NTJ, 128], BF16) for i in range(2)]
    b_tb = [Buf(), Buf()]
    d_tb = [st.dsem(), st.dsem()]
    kst = [st.sb(f"kst{i}", [128, 4, D], F32) for i in range(2)]
    b_kst = [Buf(), Buf()]
    d_k = [st.dsem(), st.dsem()]
    osb = [st.sb(f"osb{i}", [128, 2, 512], F32) for i in range(2)]
    b_osb = [Buf(), Buf()]
    acc = [[st.ps() for _ in range(4)] for _ in range(2)]
    b_acc = [[Buf() for _ in range(4)] for _ in range(2)]
    KHv = KH_d.rearrange("r (fc p) c -> fc p r c", p=128)

    def load(fc):
        s = fc % 2
        st.dma("sync", tb[s][:, 0], FC_d[fc], d_tb[s], writes=[b_tb[s]])
        st.dma("sync", tb[s][:, 1], FS_d[fc], d_tb[s], writes=[b_tb[s]])

    load(0)
    it = 0
    for fc in range(NFC):
        s = fc % 2
        if fc + 1 < NFC:
            load(fc + 1)
        for hc in range(2):
            q = it % 2
            it += 1
            cs = slice(hc * 512, (hc + 1) * 512)
            for which, (src, b_src) in enumerate(((a_tm, b_a), (b_tm, b_b))):
                for par in range(2):
                    k = which * 2 + par

                    def mm(e, which=which, par=par, k=k, src=src, s=s, q=q, cs=cs):
                        ins = None
                        for jj in range(16):
                            j = par * 16 + jj
                            ins = e.matmul(acc[q][k][:], tb[s][:, which, j, :], src[:, j, cs], start=(jj == 0), stop=(jj == 15))
                        return ins
                    st.op("tensor", mm, reads=[b_tb[s], b_src[par]], writes=[b_acc[q][k]])
            st.op("scalar", lambda e, q=q: e.activation(osb[q][:, 0, :], acc[q][1][:], AF.Copy), reads=[b_acc[q][1]], writes=[b_osb[q]])
            st.op("scalar", lambda e, q=q: e.activation(osb[q][:, 1, :], acc[q][3][:], AF.Copy), reads=[b_acc[q][3]], writes=[b_osb[q]])
            st.op("vector", lambda e, q=q, s=s, cs=cs: e.tensor_tensor(kst[s][:, 0, cs], acc[q][0][:], osb[q][:, 0, :], ALU.add), reads=[b_acc[q][0], b_osb[q]], writes=[b_kst[s]])
            st.op("vector", lambda e, q=q, s=s, cs=cs: e.tensor_tensor(kst[s][:, 1, cs], acc[q][2][:], osb[q][:, 1, :], ALU.add), reads=[b_acc[q][2], b_osb[q]], writes=[b_kst[s]])
            st.op("vector", lambda e, q=q, s=s, cs=cs: e.tensor_tensor(kst[s][:, 2, cs], acc[q][0][:], osb[q][:, 0, :], ALU.subtract), reads=[b_acc[q][0], b_osb[q]], writes=[b_kst[s]])
            st.op("vector", lambda e, q=q, s=s, cs=cs: e.scalar_tensor_tensor(kst[s][:, 3, cs], acc[q][2][:], -1.0, osb[q][:, 1, :], ALU.mult, ALU.add), reads=[b_acc[q][2], b_osb[q]], writes=[b_kst[s]])
        if fc == NFC - 1:
            st.op("vector", lambda e, s=s: e.memset(kst[s][0:1, 2:4, :], 0.0), writes=[b_kst[s]], memset=True)
        st.dma("sync", KHv[fc], kst[s][:], d_k[s], reads=[b_kst[s]])
    st.finish()


def stage_hy_in(nc, name, xT_d, tok0, w_d, vecs_d, nv, gcol, vc, identb_d, VTM_d, X0TM_d):
    st = Stage(nc, name)
    vecs = st.sb("vecs", [128, nv], F32)
    b_vecs = Buf()
    st.dma("sync", vecs[:], vecs_d, st.dsem(), writes=[b_vecs])
    identb = st.sb("identb", [128, 128], BF16)
    b_idb = Buf()
    st.dma("gpsimd", identb[:], identb_d, st.dsem(), writes=[b_idb])
    ones = st.sb("ones", [128, 128], BF16)
    b_ones = Buf()
    st.op("vector", lambda e: e.memset(ones[:], 1.0), writes=[b_ones], memset=True)
    hT = st.sb("hT", [128, 8, L], BF16)
    b_h = Buf()
    xt = st.sb("xt", [128, 8, 512], F32)
    b_x = Buf()
    d_x = st.dsem()
    sq = [st.sb(f"sq{i}", [128, 2, 512], BF16) for i in range(2)]
    b_sq = [Buf(), Buf()]
    rs = st.sb("rs", [128, 512], F32)
    b_rs = Buf()
    wc = [st.sb(f"wc{i}", [128, 8, 3, 128], BF16) for i in range(2)]
    b_wc = [Buf(), Buf()]
    d_wc = [st.dsem(), st.dsem()]
    u2 = [st.sb(f"u{i}", [128, L], F32) for i in range(2)]
    b_u2 = [Buf(), Buf()]
    uc = [st.sb(f"uc{i}", [128, L], F32) for i in range(2)]
    b_uc = [Buf(), Buf()]
    g16 = st.sb("g16", [128, L], BF16)
    b_g16 = Buf()
    x16 = st.sb("x16", [128, L], BF16)
    b_x16 = Buf()
    stg = [st.sb(f"stg{i}", [128, 32, 128], BF16) for i in range(2)]
    b_stg = [Buf(), Buf()]
    d_stg = [st.dsem(), st.dsem()]
    ss_ps = st.ps()
    b_ss = Buf()
    ps = [st.ps() for _ in range(4)]
    b_ps = [Buf() for _ in range(4)]
    tp_ps = [st.ps([128, 1024], BF16), st.ps([128, 1024], BF16)]
    b_tp = [Buf(), Buf()]
    xTv = xT_d.rearrange("(c p) t -> p c t", p=128)
    wv_ = w_d.rearrange("(c p) (a n) -> p c a n", p=128, a=3)

    def loadw(cc):
        s = cc % 2
        for a in range(3):
            st.dma("gpsimd", wc[s][:, :, a, :], wv_[:, :, a, cc * 128:(cc + 1) * 128], d_wc[s], writes=[b_wc[s]])

    loadw(0)
    for i in range(8):
        t0 = tok0 + i * 512
        st.dma("sync", xt[:], xTv[:, :, t0:t0 + 512], d_x, writes=[b_x])
        emit_norm_stats(st, xt, b_x, sq, b_sq, ones, b_ones, ss_ps, b_ss, rs, b_rs, lnexp=True)
        for c in range(8):
            st.op("vector", lambda e, c=c, i=i: e.scalar_tensor_tensor(
                hT[:, c, i * 512:(i + 1) * 512], xt[:, c, :], vecs[:, gcol + c:gcol + c + 1], rs[:], ALU.mult, ALU.mult),
                reads=[b_x, b_rs, b_vecs], writes=[b_h])
    kk = 0
    npart = 0
    kctr = [0]
    pending = []
    bi, cw, cb = vc["b_in"], vc["conv_w"], vc["conv_b"]
    for cc in range(8):
        s = cc % 2
        if cc + 1 < 8:
            loadw(cc + 1)
        for part, dst_i in ((2, 0), (1, 1), (0, 1)):
            col = part * 8 + cc
            dst, b_dst = uc[dst_i], b_uc[dst_i]
            u, b_u = u2[npart % 2], b_u2[npart % 2]
            npart += 1
            for tl in range(8):
                p = kk % 4
                kk += 1

                def mm(e, p=p, part=part, tl=tl, s=s):
                    ins = None
                    for c in range(8):
                        ins = e.matmul(ps[p][:], wc[s][:, c, part, :], hT[:, c, tl * 512:(tl + 1) * 512], start=(c == 0), stop=(c == 7))
                    return ins
                st.op("tensor", mm, reads=[b_h, b_wc[s]], writes=[b_ps[p]])
                st.op("scalar", lambda e, p=p, tl=tl, col=col, u=u: e.activation(u[:, tl * 512:(tl + 1) * 512], ps[p][:], AF.Identity, bias=vecs[:, bi + col:bi + col + 1], scale=1.0),
                      reads=[b_ps[p], b_vecs], writes=[b_u])
            for fn_ in pending:
                fn_()
            pending.clear()
            for hh in range(2):
                st.op("scalar", lambda e, hh=hh, col=col, dst=dst, u=u: e.activation(
                    dst[:, hh * 2048:(hh + 1) * 2048], u[:, hh * 2048:(hh + 1) * 2048], AF.Identity,
                    bias=vecs[:, cb + col:cb + col + 1], scale=vecs[:, cw + 24 + col:cw + 24 + col + 1]),
                    reads=[b_u, b_vecs], writes=[b_dst])
            st.op("vector", lambda e, col=col, dst=dst, u=u: e.scalar_tensor_tensor(
                dst[:, 1:L], u[:, 0:L - 1], vecs[:, cw + col:cw + col + 1], dst[:, 1:L], ALU.mult, ALU.add),
                reads=[b_u, b_dst, b_vecs], writes=[b_dst])
            st.op("vector", lambda e, col=col, dst=dst, u=u: e.scalar_tensor_tensor(
                dst[:, 0:L - 1], u[:, 1:L], vecs[:, cw + 48 + col:cw + 48 + col + 1], dst[:, 0:L - 1], ALU.mult, ALU.add),
                reads=[b_u, b_dst, b_vecs], writes=[b_dst])
            if part == 1:
                st.op("vector", lambda e: e.tensor_tensor(g16[:], uc[0][:], uc[1][:], ALU.mult), reads=[b_uc[0], b_uc[1]], writes=[b_g16])
                pending.append(lambda cc=cc: emit_tm_store(st, g16, b_g16, identb, b_idb, tp_ps, b_tp, stg, b_stg, d_stg, VTM_d, cc, kctr))
            if part == 0:
                for hh in range(2):
                    st.op("scalar", lambda e, hh=hh: e.activation(x16[:, hh * 2048:(hh + 1) * 2048], uc[1][:, hh * 2048:(hh + 1) * 2048], AF.Copy), reads=[b_uc[1]], writes=[b_x16])
                pending.append(lambda cc=cc: emit_tm_store(st, x16, b_x16, identb, b_idb, tp_ps, b_tp, stg, b_stg, d_stg, X0TM_d, cc, kctr))
    for fn_ in pending:
        fn_()
    st.finish()


def stage_hy_fwd(nc, name, VTM_d, FC_d, FS_d, KH_d, YH_d):
    st = Stage(nc, name)
    v_tm = st.sb("v_tm", [128, NTJ, D], BF16)
    b_vh = [Buf(), Buf()]
    VTv = VTM_d.rearrange("(tc p) c -> p tc c", p=128)
    st.dma("sync", v_tm[:, 0:16], VTv[:, 0:16], st.dsem(), writes=[b_vh[0]])
    st.dma("scalar", v_tm[:, 16:32], VTv[:, 16:32], st.dsem(), writes=[b_vh[1]])
    tb = [st.sb(f"tb{i}", [128, 2, NTJ, 128], BF16) for i in range(2)]
    b_tb = [Buf(), Buf()]
    d_tb = [st.dsem(), st.dsem()]
    kh = [st.sb(f"kh{i}", [128, 4, D], F32) for i in range(2)]
    b_kh = [Buf(), Buf()]
    d_kh = [st.dsem(), st.dsem()]
    yst = [st.sb(f"yst{i}", [128, 4, D], BF16) for i in range(2)]
    b_yst = [Buf(), Buf()]
    d_y = [st.dsem(), st.dsem()]
    NB = 6
    bt = [[st.sb(f"bt{q}_{i}", [128, 512], F32) for i in range(NB)] for q in range(2)]
    b_bt = [[Buf() for _ in range(NB)] for _ in range(2)]
    mt = [st.sb(f"mt{i}", [128, 512], F32) for i in range(8)]
    b_mt = [Buf() for _ in range(8)]
    acc = [[st.ps() for _ in range(4)] for _ in range(2)]
    b_acc = [[Buf() for _ in range(4)] for _ in range(2)]
    KHv = KH_d.rearrange("r (fc p) c -> fc p r c", p=128)
    YHv = YH_d.rearrange("r (fc p) c -> fc p r c", p=128)

    def load(fc):
        s = fc % 2
        st.dma("sync", tb[s][:, 0], FC_d[fc], d_tb[s], writes=[b_tb[s]])
        st.dma("sync", tb[s][:, 1], FS_d[fc], d_tb[s], writes=[b_tb[s]])
        st.dma("sync", kh[s][:], KHv[fc], d_kh[s], writes=[b_kh[s]])

    def tt(eng, out, b_out, i0, b0, i1, b1, op):
        st.op(eng, lambda e: e.tensor_tensor(out, i0, i1, op), reads=[b0, b1], writes=[b_out])

    load(0)
    it = 0
    for fc in range(NFC):
        s = fc % 2
        if fc + 1 < NFC:
            load(fc + 1)
        for hc in range(2):
            q = it % 2
            it += 1
            cs = slice(hc * 512, (hc + 1) * 512)
            for which in range(2):
                for par in range(2):
                    k = which * 2 + par

                    def mm(e, which=which, par=par, k=k, s=s, q=q, cs=cs):
                        ins = None
                        for jj in range(16):
                            j = par * 16 + jj
                            ins = e.matmul(acc[q][k][:], tb[s][:, which, j, :], v_tm[:, j, cs], start=(jj == 0), stop=(jj == 15))
                        return ins
                    st.op("tensor", mm, reads=[b_tb[s], b_vh[par]], writes=[b_acc[q][k]])
            B, bB = bt[q], b_bt[q]
            A_, bA = acc[q], b_acc[q]
            st.op("scalar", lambda e, B=B, A_=A_: e.activation(B[0][:], A_[1][:], AF.Copy), reads=[bA[1]], writes=[bB[0]])
            st.op("scalar", lambda e, B=B, A_=A_: e.activation(B[1][:], A_[3][:], AF.Copy), reads=[bA[3]], writes=[bB[1]])
            tt("vector", B[2][:], bB[2], A_[0][:], bA[0], B[0][:], bB[0], ALU.add)
            tt("vector", B[3][:], bB[3], A_[0][:], bA[0], B[0][:], bB[0], ALU.subtract)
            tt("vector", B[4][:], bB[4], A_[2][:], bA[2], B[1][:], bB[1], ALU.add)
            tt("vector", B[5][:], bB[5], A_[2][:], bA[2], B[1][:], bB[1], ALU.subtract)
            K = kh[s]
            bK = b_kh[s]
            tt("vector", mt[0][:], b_mt[0], B[2][:], bB[2], K[:, 0, cs], bK, ALU.mult)
            tt("vector", mt[1][:], b_mt[1], B[4][:], bB[4], K[:, 1, cs], bK, ALU.mult)
            tt("vector", mt[0][:], b_mt[0], mt[0][:], b_mt[0], mt[1][:], b_mt[1], ALU.add)
            tt("vector", mt[2][:], b_mt[2], B[2][:], bB[2], K[:, 1, cs], bK, ALU.mult)
            tt("vector", mt[3][:], b_mt[3], B[4][:], bB[4], K[:, 0, cs], bK, ALU.mult)
            tt("vector", mt[2][:], b_mt[2], mt[2][:], b_mt[2], mt[3][:], b_mt[3], ALU.subtract)
            tt("gpsimd", mt[4][:], b_mt[4], B[3][:], bB[3], K[:, 2, cs], bK, ALU.mult)
            tt("gpsimd", mt[5][:], b_mt[5], B[5][:], bB[5], K[:, 3, cs], bK, ALU.mult)
            tt("gpsimd", mt[4][:], b_mt[4], mt[4][:], b_mt[4], mt[5][:], b_mt[5], ALU.subtract)
            tt("gpsimd", mt[6][:], b_mt[6], B[3][:], bB[3], K[:, 3, cs], bK, ALU.mult)
            tt("gpsimd", mt[7][:], b_mt[7], B[5][:], bB[5], K[:, 2, cs], bK, ALU.mult)
            tt("gpsimd", mt[6][:], b_mt[6], mt[6][:], b_mt[6], mt[7][:], b_mt[7], ALU.add)
            Y = yst[s]
            bY = b_yst[s]
            tt("vector", Y[:, 0, cs], bY, mt[0][:], b_mt[0], mt[4][:], b_mt[4], ALU.add)
            tt("vector", Y[:, 1, cs], bY, mt[2][:], b_mt[2], mt[6][:], b_mt[6], ALU.subtract)
            tt("gpsimd", Y[:, 2, cs], bY, mt[0][:], b_mt[0], mt[4][:], b_mt[4], ALU.subtract)
            tt("gpsimd", Y[:, 3, cs], bY, mt[2][:], b_mt[2], mt[6][:], b_mt[6], ALU.add)
        st.dma("sync", YHv[fc], yst[s][:], d_y[s], reads=[b_yst[s]])
    st.finish()


def stage_hy_inv(nc, name, YH_d, GC_d, GS_d, VTM_d, X0TM_d, dbc_d, identb_d, ZT_d):
    st = Stage(nc, name)
    identb = st.sb("identb", [128, 128], BF16)
    b_idb = Buf()
    st.dma("gpsimd", identb[:], identb_d, st.dsem(), writes=[b_idb])
    yh = st.sb("yh", [128, 4, NFC, D], BF16)
    b_yh = [Buf() for _ in range(4)]
    YHv = YH_d.rearrange("r (fc p) c -> r p fc c", p=128)
    for r in range(4):
        st.dma("sync" if r < 2 else "scalar", yh[:, r], YHv[r], st.dsem(), writes=[b_yh[r]])
    dbc = st.sb("dbc", [128, D], F32)
    b_dbc = Buf()
    st.dma("sync", dbc[:], dbc_d, st.dsem(), writes=[b_dbc])
    NS = 3
    tb = [st.sb(f"tb{i}", [128, 2, NFC, 128], BF16) for i in range(NS)]
    b_tb = [Buf() for _ in range(NS)]
    d_tb = [st.dsem() for _ in range(NS)]
    vx = [st.sb(f"vx{i}", [128, 2, D], BF16) for i in range(NS)]
    b_vx = [Buf() for _ in range(NS)]
    d_vx = [st.dsem() for _ in range(NS)]
    tm = [st.sb(f"tm{i}", [128, D], F32) for i in range(2)]
    b_tm = [Buf(), Buf()]
    z16 = [st.sb(f"z16{i}", [128, D], BF16) for i in range(2)]
    b_z = [Buf(), Buf()]
    zst = [st.sb(f"zst{i}", [128, 8, 256], BF16) for i in range(2)]
    b_zst = [Buf(), Buf()]
    d_z = [st.dsem(), st.dsem()]
    y_ps = [st.ps([128, 1024]) for _ in range(2)]
    b_y = [Buf(), Buf()]
    tp_ps = [st.ps([128, 1024], BF16), st.ps([128, 1024], BF16)]
    b_tp = [Buf(), Buf()]
    ZTv = ZT_d.rearrange("(c p) t -> p c t", p=128)
    order = [(jj, par) for jj in range(16) for par in range(2)]

    def load(n):
        jj, par = order[n]
        j = par * 16 + jj
        s3 = n % NS
        st.dma("sync", tb[s3][:, 0], GC_d[j], d_tb[s3], writes=[b_tb[s3]])
        st.dma("sync", tb[s3][:, 1], GS_d[j], d_tb[s3], writes=[b_tb[s3]])
        st.dma("sync", vx[s3][:, 0, :], VTM_d[j * 128:(j + 1) * 128, :], d_vx[s3], writes=[b_vx[s3]])
        st.dma("sync", vx[s3][:, 1, :], X0TM_d[j * 128:(j + 1) * 128, :], d_vx[s3], writes=[b_vx[s3]])

    load(0)
    load(1)
    deferred = None
    for n in range(32):
        jj, par = order[n]
        s = n % 2
        s3 = n % NS
        zs = jj % 2
        if n + 2 < 32:
            load(n + 2)

        def mm(e, s=s, par=par, s3=s3):
            ins = None
            for hc in range(2):
                k = 0
                for which in range(2):
                    for fc in range(NFC):
                        ins = e.matmul(y_ps[s][:, hc * 512:(hc + 1) * 512], tb[s3][:, which, fc, :], yh[:, 2 * par + which, fc, hc * 512:(hc + 1) * 512],
                                       start=(k == 0), stop=(k == 2 * NFC - 1))
                        k += 1
            return ins
        st.op("tensor", mm, reads=[b_tb[s3], b_yh[2 * par], b_yh[2 * par + 1]], writes=[b_y[s]])
        st.op("gpsimd", lambda e, s=s, s3=s3: e.tensor_tensor(tm[s][:], vx[s3][:, 0, :], dbc[:], ALU.mult), reads=[b_vx[s3], b_dbc], writes=[b_tm[s]])
        st.op("vector", lambda e, s=s: e.tensor_tensor(tm[s][:], y_ps[s][:], tm[s][:], ALU.add), reads=[b_y[s], b_tm[s]], writes=[b_tm[s]])
        st.op("gpsimd", lambda e, s=s, s3=s3: e.tensor_tensor(z16[s][:], tm[s][:], vx[s3][:, 1, :], ALU.mult), reads=[b_tm[s], b_vx[s3]], writes=[b_z[s]])

        def emit_tr(s=s, zs=zs, par=par, jj=jj):
            def tr(e):
                ins = None
                for c in range(8):
                    ins = e.transpose(tp_ps[s][:, c * 128:(c + 1) * 128], z16[s][:, c * 128:(c + 1) * 128], identb[:])
                return ins
            st.op("tensor", tr, reads=[b_z[s], b_idb], writes=[b_tp[s]])
            st.op("scalar", lambda e: e.activation(zst[zs][:, :, par:256:2], tp_ps[s][:].rearrange("p (c t) -> p c t", t=128), AF.Copy),
                  reads=[b_tp[s]], writes=[b_zst[zs]])
            if par == 1:
                st.dma("sync", ZTv[:, :, jj * 256:(jj + 1) * 256], zst[zs][:], d_z[zs], reads=[b_zst[zs]])
        if deferred is not None:
            deferred()
        deferred = emit_tr
    deferred()
    st.finish()


def stage_hy_out(nc, name, xT_d, tok0, ZT_d, wo_d, vecs_d, nv, bocol):
    st = Stage(nc, name)
    vecs = st.sb("vecs", [128, nv], F32)
    b_vecs = Buf()
    st.dma("sync", vecs[:], vecs_d, st.dsem(), writes=[b_vecs])
    wo = st.sb("wo", [128, 8, D], BF16)
    b_wo = Buf()
    st.dma("gpsimd", wo[:], wo_d.rearrange("(c p) n -> p c n", p=128), st.dsem(), writes=[b_wo])
    xt = [st.sb(f"xt{i}", [128, 8, 512], F32) for i in range(2)]
    b_x = [Buf(), Buf()]
    d_x = [st.dsem(), st.dsem()]
    d_o = [st.dsem(), st.dsem()]
    zt = [st.sb(f"zt{i}", [128, 8, 512], BF16) for i in range(2)]
    b_z = [Buf(), Buf()]
    d_zt = [st.dsem(), st.dsem()]
    y_ps = [st.ps(), st.ps()]
    b_y = [Buf(), Buf()]
    xTv = xT_d.rearrange("(c p) t -> p c t", p=128)
    ZTv = ZT_d.rearrange("(c p) t -> p c t", p=128)

    def load(i):
        s = i % 2
        st.dma("sync", xt[s][:], xTv[:, :, tok0 + i * 512:tok0 + (i + 1) * 512], d_x[s], writes=[b_x[s]])
        st.dma("sync", zt[s][:], ZTv[:, :, i * 512:(i + 1) * 512], d_zt[s], writes=[b_z[s]])

    load(0)
    for i in range(8):
        s = i % 2
        if i + 1 < 8:
            load(i + 1)
        for m in range(8):
            p = m % 2

            def mm(e, m=m, p=p, s=s):
                ins = None
                for c in range(8):
                    ins = e.matmul(y_ps[p][:], wo[:, c, m * 128:(m + 1) * 128], zt[s][:, c, :], start=(c == 0), stop=(c == 7))
                return ins
            st.op("tensor", mm, reads=[b_z[s], b_wo], writes=[b_y[p]])
            st.op("vector", lambda e, m=m, p=p, s=s: e.scalar_tensor_tensor(
                xt[s][:, m, :], y_ps[p][:], vecs[:, bocol + m:bocol + m + 1], xt[s][:, m, :], ALU.add, ALU.add),
                reads=[b_y[p], b_x[s], b_vecs], writes=[b_x[s]])
        st.dma("sync", xTv[:, :, tok0 + i * 512:tok0 + (i + 1) * 512], xt[s][:], d_o[s], reads=[b_x[s]])
    st.finish()


DEPTH = 4
_TABLE_CACHE = {}


def _tables():
    if not _TABLE_CACHE:
        FC, FS, GC, GS = dft_tables()
        rc, rsn = rope_tables()
        zT, trow = hyena_pos_tables()
        _TABLE_CACHE.update(FC=FC, FS=FS, GC=GC, GS=GS, rc=rc, rsn=rsn, zT=zT, trow=trow,
                            ident=np.eye(128, dtype=np.float32))
    return _TABLE_CACHE


def prep_shared(inp):
    f32 = lambda a: np.ascontiguousarray(np.asarray(a, np.float32))
    sh = {}
    vt = VecTable()
    for i in range(DEPTH):
        vt.add_feat(f"f{i}0", inp["ffn_norm_g"][i, 0])
        vt.add_feat(f"mix{i}", inp["mix_norm_g"][i])
        vt.add_feat(f"f{i}1", inp["ffn_norm_g"][i, 1])
    vt.add_feat("fin", inp["final_norm_g"])
    sw = np.arange(64) ^ 1
    na = inp["attn_w_in"].shape[0]
    nh = inp["hy_w_in"].shape[0]
    wq, wk, wv = [], [], []
    for j in range(na):
        qg = np.asarray(inp["attn_q_gain"][j], np.float32)
        kg = np.asarray(inp["attn_k_gain"][j], np.float32)
        vt.add(f"qg{j}", np.tile(qg, 2)[:, None])
        vt.add(f"qgs{j}", np.tile(qg[sw], 2)[:, None])
        vt.add(f"kg{j}", np.tile(kg, 2)[:, None])
        vt.add(f"kgs{j}", np.tile(kg[sw], 2)[:, None])
        a, b, c = prep_attn_weights(np.asarray(inp["attn_w_in"][j], np.float32))
        wq.append(a)
        wk.append(b)
        wv.append(c)
    for j in range(nh):
        vt.add_feat(f"b_in{j}", inp["hy_b_in"][j])
        vt.add_feat(f"conv_w{j}", np.asarray(inp["hy_conv_w"][j]).reshape(-1))
        vt.add_feat(f"conv_b{j}", inp["hy_conv_b"][j])
        vt.add_feat(f"decay{j}", np.asarray(inp["hy_decay"][j]).reshape(-1))
        vt.add_feat(f"b_out{j}", inp["hy_b_out"][j])
        for key, src in (("b1", "hy_f_b1"), ("b2", "hy_f_b2"), ("b3", "hy_f_b3"), ("freq", "hy_f_freq")):
            vt.add(f"{key}_{j}", np.asarray(inp[src][j], np.float32)[:, None])
    sh["vecs"] = vt.build()
    sh["wq"] = np.stack(wq)
    sh["wk"] = np.stack(wk)
    sh["wv"] = np.stack(wv)
    sh["hy_dbc"] = np.ascontiguousarray(np.broadcast_to(np.asarray(inp["hy_d_bias"], np.float32)[:, None, :], (nh, 128, D)))
    sh["attn_w_out"] = np.ascontiguousarray(np.asarray(inp["attn_w_out"], np.float32)[:, q_head_perm(), :])
    for k in ("ffn_w_in", "ffn_w_out", "hy_w_in", "hy_w_out", "hy_f_w1", "hy_f_w2", "hy_f_w3", "hy_f_w_out"):
        sh[k] = f32(inp[k])
    sh.update(_tables())
    return sh, vt


def build_program(sh, vt, nseq, plan=None):
    nc = bass.Bass("TRN2", target_bir_lowering=False)
    ntok = nseq * L
    ap = {}
    for k, a in sh.items():
        dt = BF16 if a.dtype == ml_dtypes.bfloat16 else F32
        ap[k] = nc.dram_tensor(k, list(a.shape), dt, kind="ExternalInput").ap()
    x_d = nc.dram_tensor("x", [ntok, D], F32, kind="ExternalInput").ap()
    out_d = nc.dram_tensor("out", [ntok, D], F32, kind="ExternalOutput").ap()
    xT = nc.dram_tensor("xT", [D, ntok], F32, kind="Internal").ap()
    KT = nc.dram_tensor("KT", [nseq, 2, 128, L], BF16, kind="Internal").ap()
    VA = nc.dram_tensor("VA", [nseq, 128, 32 * 512], BF16, kind="Internal").ap()
    ATM = nc.dram_tensor("ATM", [L, D], BF16, kind="Internal").ap()
    BTM = nc.dram_tensor("BTM", [L, D], BF16, kind="Internal").ap()
    KH = nc.dram_tensor("KH", [4, NFP, D], F32, kind="Internal").ap()
    VTM = nc.dram_tensor("VTM", [L, D], BF16, kind="Internal").ap()
    X0TM = nc.dram_tensor("X0TM", [L, D], BF16, kind="Internal").ap()
    YH = nc.dram_tensor("YH", [4, NFP, D], BF16, kind="Internal").ap()
    ZT = nc.dram_tensor("ZT", [D, L], BF16, kind="Internal").ap()
    vecs_d = ap["vecs"]
    nv = sh["vecs"].shape[1]
    col = lambda key: vt.idx[key][0]
    if plan is None:
        plan = ["tin"]
        for i in range(DEPTH):
            plan += [f"ffn{i}0", f"mix{i}", f"ffn{i}1"]
        plan += ["fin"]
    for item in plan:
        if item == "tin":
            stage_transpose_in(nc, x_d, xT, ap["ident"], ntok)
        elif item == "fin":
            stage_final(nc, xT, out_d, vecs_d, nv, col("fin"), ap["ident"], ntok)
        elif item.startswith("ffn"):
            i, k = int(item[3]), int(item[4])
            stage_ffn(nc, item, xT, ap["ffn_w_in"][i, k], ap["ffn_w_out"][i, k], vecs_d, nv, col(f"f{i}{k}"), ntok)
        elif item.startswith("mix"):
            i = int(item[3])
            j = i // 2
            if i % 2 == 0:
                stage_attn_kv(nc, f"akv{i}", xT, ap["wk"][j], ap["wv"][j], vecs_d, nv, col(f"mix{i}"), col(f"kg{j}"), col(f"kgs{j}"),
                              ap["rc"], ap["rsn"], KT, VA, nseq)
                for s in range(nseq):
                    stage_attn_q(nc, f"aq{i}{s}", xT, s * L, ap["wq"][j], ap["attn_w_out"][j], vecs_d, nv, col(f"mix{i}"),
                                 col(f"qg{j}"), col(f"qgs{j}"), ap["rc"], ap["rsn"], KT[s], VA[s])
            else:
                vc = {k: col(f"{k}_{j}") for k in ("b1", "b2", "b3", "freq")}
                vc.update({k: col(f"{k}{j}") for k in ("b_in", "conv_w", "conv_b", "decay")})
                stage_hy_filter(nc, f"hf{i}", ap["zT"], ap["trow"], ap["hy_f_w1"][j], ap["hy_f_w2"][j], ap["hy_f_w3"][j], ap["hy_f_w_out"][j],
                                vecs_d, nv, vc, ap["ident"], ATM, BTM)
                stage_hy_kdft(nc, f"hk{i}", ATM, BTM, ap["FC"], ap["FS"], KH)
                for s in range(nseq):
                    stage_hy_in(nc, f"hi{i}{s}", xT, s * L, ap["hy_w_in"][j], vecs_d, nv, col(f"mix{i}"), vc, ap["ident"], VTM, X0TM)
                    stage_hy_fwd(nc, f"hw{i}{s}", VTM, ap["FC"], ap["FS"], KH, YH)
                    stage_hy_inv(nc, f"hv{i}{s}", YH, ap["GC"], ap["GS"], VTM, X0TM, ap["hy_dbc"][j], ap["ident"], ZT)
                    stage_hy_out(nc, f"ho{i}{s}", xT, s * L, ZT, ap["hy_w_out"][j], vecs_d, nv, col(f"b_out{j}"))
    return nc


def kernel(**inputs):
    x = np.asarray(inputs["x"], np.float32)
    B = x.shape[0]
    nseq = B // NCORES
    sh, vt = prep_shared(inputs)
    nc = build_program(sh, vt, nseq)
    in_maps = []
    for c in range(NCORES):
        m = dict(sh)
        m["x"] = np.ascontiguousarray(x[c * nseq:(c + 1) * nseq].reshape(nseq * L, D))
        in_maps.append(m)
    res = run_bass_kernel_spmd(nc, in_maps, core_ids=list(range(NCORES)))
    out = np.stack([np.asarray(r["out"], np.float32).reshape(nseq, L, D) for r in res.results], axis=0)
    return out.reshape(B, L, D)
```
